# Optimizing a Trainium2 kernel written in Bass

```python
import jax, jax.numpy as jnp
from jax import lax
import numpy as np

D_MODEL = 2048
BATCH = 4
SEQ = 4096
DEPTH = 1

N_META = 16
HEAD_DIM = 64
RWKV_HEADS = 16
RWKV_WIDTH = RWKV_HEADS * HEAD_DIM
FOX_HEADS = 16
FOX_WIDTH = FOX_HEADS * HEAD_DIM
DECAY_LORA = 96
AAA_LORA = 96
GATE_LORA = 256
D_FF = -(-8 * D_MODEL // (3 * 256)) * 256
Q_BLOCK = 128
RMS_EPS = 1e-6
GN_EPS = 64e-5
ATTN_SCALE = HEAD_DIM ** -0.5

RWKV_SPLITS = (RWKV_WIDTH, 2 * RWKV_WIDTH, 3 * RWKV_WIDTH,
               3 * RWKV_WIDTH + DECAY_LORA, 3 * RWKV_WIDTH + DECAY_LORA + AAA_LORA)
RWKV_COLS = 3 * RWKV_WIDTH + DECAY_LORA + AAA_LORA + GATE_LORA
FOX_SPLITS = (FOX_WIDTH, 2 * FOX_WIDTH, 3 * FOX_WIDTH)
FOX_COLS = 3 * FOX_WIDTH + FOX_HEADS
N_IN = RWKV_COLS + FOX_COLS + 2 * D_MODEL

kernel_name = 'hybrid_rwkv7_fox_meta_gated_block'

F32 = jnp.float32


def _rms(x, g):
    xf = x.astype(F32)
    xf = xf * lax.rsqrt(jnp.mean(xf * xf, axis=-1, keepdims=True) + RMS_EPS)
    return xf.astype(x.dtype) * g


def _wkv7_scan(r, w, k, v, a, b):
    B, L, H, N = r.shape

    def step(S, inp):
        r_t, w_t, k_t, v_t, a_t, b_t = inp
        sa = jnp.einsum('bhvk,bhk->bhv', S, a_t)
        S = (S * w_t[:, :, None, :] + sa[..., None] * b_t[:, :, None, :]
             + v_t[..., None] * k_t[:, :, None, :])
        return S, jnp.einsum('bhvk,bhk->bhv', S, r_t)

    xs = tuple(jnp.swapaxes(t, 0, 1) for t in (r, w, k, v, a, b))
    _, y = lax.scan(step, jnp.zeros((B, H, N, N), F32), xs)
    return jnp.swapaxes(y, 0, 1)


def _rwkv7(z, w0, w2, a0, a2, g2, k_k, k_a, r_k, gn_w, gn_b):
    B, L, _ = z.shape
    r, k, v, wd, ad, gd = jnp.split(z, RWKV_SPLITS, axis=-1)
    w_log = -jax.nn.softplus(-(w0 + jnp.tanh(wd) @ w2).astype(F32)) - 0.5
    decay = jnp.exp(-jnp.exp(w_log))
    a = jax.nn.sigmoid((a0 + ad @ a2).astype(F32))
    g = jax.nn.sigmoid(gd) @ g2
    heads = lambda t: t.reshape(B, L, RWKV_HEADS, HEAD_DIM)
    kk = heads((k * k_k).astype(F32))
    kk = kk / jnp.maximum(jnp.sqrt(jnp.sum(kk * kk, axis=-1, keepdims=True)), 1e-12)
    kf = k.astype(F32) * (1.0 + (a - 1.0) * k_a.astype(F32))
    rh, kh, vh, ah, dh = heads(r.astype(F32)), heads(kf), heads(v.astype(F32)), heads(a), heads(decay)
    y = _wkv7_scan(rh, dh, kh, vh, -kk, kk * ah)
    mu = jnp.mean(y, axis=-1, keepdims=True)
    var = jnp.mean(jnp.square(y - mu), axis=-1, keepdims=True)
    y = (y - mu) * lax.rsqrt(var + GN_EPS)
    y = y * gn_w.astype(F32).reshape(RWKV_HEADS, HEAD_DIM) + gn_b.astype(F32).reshape(RWKV_HEADS, HEAD_DIM)
    y = y + jnp.sum(rh * kh * r_k.astype(F32), axis=-1, keepdims=True) * vh
    return y.reshape(B, L, RWKV_WIDTH).astype(z.dtype) * g


def _fox(z, q_g, k_g, f_bias):
    B, L, _ = z.shape
    q, k, v, fl = jnp.split(z, FOX_SPLITS, axis=-1)
    q = _rms(q.reshape(B, L, FOX_HEADS, HEAD_DIM), q_g)
    k = _rms(k.reshape(B, L, FOX_HEADS, HEAD_DIM), k_g)
    v = v.reshape(B, L, FOX_HEADS, HEAD_DIM)
    log_f = jax.nn.log_sigmoid(fl.astype(F32) + f_bias.astype(F32))
    c = jnp.swapaxes(jnp.cumsum(log_f, axis=1), 1, 2)
    bounds = [(0, N_META)] + [(s, min(s + Q_BLOCK, L)) for s in range(N_META, L, Q_BLOCK)]
    outs = []
    for s, e in bounds:
        sc = jnp.einsum('bqhd,bkhd->bhqk', q[:, s:e], k[:, :e]).astype(F32) * ATTN_SCALE
        sc = sc + c[:, :, s:e, None] - c[:, :, None, :e]
        causal = jnp.arange(s, e)[:, None] >= jnp.arange(e)[None, :]
        p = jax.nn.softmax(jnp.where(causal, sc, -jnp.inf), axis=-1)
        outs.append(jnp.einsum('bhqk,bkhd->bqhd', p.astype(v.dtype), v[:, :e]))
    return jnp.concatenate(outs, axis=1).reshape(B, L, FOX_WIDTH)


def _layer(h, n1, w_in, mu, w0, w2, a0, a2, g2, k_k, k_a, r_k, gn_w, gn_b,
           q_g, k_g, f_bias, w_a, w_b, w_o, n2, w_gu, w_dn):
    xn = _rms(h, n1)
    proj = xn @ w_in
    z_rwkv, z_fox, z_gate = jnp.split(proj, (RWKV_COLS, RWKV_COLS + FOX_COLS), axis=-1)
    z_prev = jnp.pad(z_rwkv, ((0, 0), (1, 0), (0, 0)))[:, :-1]
    z_rwkv = z_rwkv + (z_prev - z_rwkv) * mu
    y_a = _rwkv7(z_rwkv, w0, w2, a0, a2, g2, k_k, k_a, r_k, gn_w, gn_b)
    y_b = _fox(z_fox, q_g, k_g, f_bias)
    gates = jax.nn.sigmoid(z_gate.astype(F32)).astype(h.dtype)
    g_a, g_b = jnp.split(gates, 2, axis=-1)
    merged = g_a * (y_a @ w_a) + g_b * (y_b @ w_b)
    h = h + merged @ w_o
    gate, up = jnp.split(_rms(h, n2) @ w_gu, 2, axis=-1)
    return h + (jax.nn.silu(gate) * up) @ w_dn


def setup_inputs(seed: int = 0) -> dict:
    key = jax.random.key(seed)
    ks = jax.random.split(key, 24)
    nrm = lambda k, shape, scale: jax.random.normal(k, shape, F32) * scale
    Dp = DEPTH
    return {
        'x': nrm(ks[0], (BATCH, SEQ, D_MODEL), 1.0),
        'meta_tokens': nrm(ks[1], (N_META, D_MODEL), 1.0),
        'norm1_g': 1.0 + nrm(ks[2], (Dp, D_MODEL), 0.05),
        'w_in': nrm(ks[3], (Dp, D_MODEL, N_IN), D_MODEL ** -0.5),
        'rwkv_mu': jax.random.uniform(ks[4], (Dp, RWKV_COLS), F32, 0.0, 1.0),
        'rwkv_w0': jax.random.uniform(ks[5], (Dp, RWKV_WIDTH), F32, -5.0, -1.0),
        'rwkv_w2': nrm(ks[6], (Dp, DECAY_LORA, RWKV_WIDTH), 0.5 * DECAY_LORA ** -0.5),
        'rwkv_a0': nrm(ks[7], (Dp, RWKV_WIDTH), 0.1),
        'rwkv_a2': nrm(ks[8], (Dp, AAA_LORA, RWKV_WIDTH), 0.5 * AAA_LORA ** -0.5),
        'rwkv_g2': nrm(ks[9], (Dp, GATE_LORA, RWKV_WIDTH), GATE_LORA ** -0.5),
        'rwkv_k_k': 0.85 + nrm(ks[10], (Dp, RWKV_WIDTH), 0.05),
        'rwkv_k_a': 1.0 + nrm(ks[11], (Dp, RWKV_WIDTH), 0.05),
        'rwkv_r_k': nrm(ks[12], (Dp, RWKV_HEADS, HEAD_DIM), 0.1),
        'rwkv_gn_w': 1.0 + nrm(ks[13], (Dp, RWKV_WIDTH), 0.05),
        'rwkv_gn_b': nrm(ks[14], (Dp, RWKV_WIDTH), 0.01),
        'fox_q_norm_g': 1.0 + nrm(ks[15], (Dp, HEAD_DIM), 0.05),
        'fox_k_norm_g': 1.0 + nrm(ks[16], (Dp, HEAD_DIM), 0.05),
        'fox_f_bias': jax.random.uniform(ks[17], (Dp, FOX_HEADS), F32, 1.0, 4.0),
        'w_branch_a': nrm(ks[18], (Dp, RWKV_WIDTH, D_MODEL), RWKV_WIDTH ** -0.5),
        'w_branch_b': nrm(ks[19], (Dp, FOX_WIDTH, D_MODEL), FOX_WIDTH ** -0.5),
        'w_o': nrm(ks[20], (Dp, D_MODEL, D_MODEL), D_MODEL ** -0.5),
        'norm2_g': 1.0 + nrm(ks[21], (Dp, D_MODEL), 0.05),
        'w_gate_up': nrm(ks[22], (Dp, D_MODEL, 2 * D_FF), D_MODEL ** -0.5),
        'w_down': nrm(ks[23], (Dp, D_FF, D_MODEL), D_FF ** -0.5),
    }


def reference(x, meta_tokens, norm1_g, w_in, rwkv_mu, rwkv_w0, rwkv_w2, rwkv_a0, rwkv_a2,
              rwkv_g2, rwkv_k_k, rwkv_k_a, rwkv_r_k, rwkv_gn_w, rwkv_gn_b, fox_q_norm_g,
              fox_k_norm_g, fox_f_bias, w_branch_a, w_branch_b, w_o, norm2_g, w_gate_up, w_down):
    B = x.shape[0]
    meta = jnp.broadcast_to(meta_tokens.astype(x.dtype)[None], (B, N_META, D_MODEL))
    h = jnp.concatenate([meta, x], axis=1)
    for l in range(DEPTH):
        h = _layer(h, norm1_g[l], w_in[l], rwkv_mu[l], rwkv_w0[l], rwkv_w2[l], rwkv_a0[l],
                   rwkv_a2[l], rwkv_g2[l], rwkv_k_k[l], rwkv_k_a[l], rwkv_r_k[l], rwkv_gn_w[l],
                   rwkv_gn_b[l], fox_q_norm_g[l], fox_k_norm_g[l], fox_f_bias[l], w_branch_a[l],
                   w_branch_b[l], w_o[l], norm2_g[l], w_gate_up[l], w_down[l])
    return h[:, N_META:]
```

```python
import os
import numpy as np
from contextlib import ExitStack
import concourse.bass as bass
import concourse.mybir as mybir
from concourse.bass_utils import run_bass_kernel_spmd

F32 = mybir.dt.float32
BF16 = mybir.dt.bfloat16
AF = mybir.ActivationFunctionType
ALU = mybir.AluOpType
AX = mybir.AxisListType

D = 2048
NPT, NOT_, NT = 17, 16, 33
TP, TO, TA = NPT * 128, NOT_ * 128, NT * 128
RW = 1024
RC = 3520
FC = 3088
NIN = 10704
DFF = 5632
DECAY_C = 0.6065306597126334
DEBUG = os.environ.get("KDEBUG", "") != ""
STAGES = os.environ.get("KSTAGES", "ABCDEF")


class KB:
    def __init__(self, nc, es):
        self.nc = nc
        self.eng = {"pe": nc.tensor, "act": nc.scalar, "dve": nc.vector, "pool": nc.gpsimd, "sp": nc.sync}
        self.sem = {e: es.enter_context(nc.semaphore("s_" + e)) for e in ("pe", "act", "dve", "pool")}
        self.cnt = {e: 0 for e in self.sem}
        self.seen = {}
        self.nd = 12
        self.dsem = {q: [es.enter_context(nc.semaphore(f"d_{q}{i}")) for i in range(self.nd)] for q in ("sp", "pool", "act")}
        self.dcnt = {q: 0 for q in self.dsem}
        self.dlast = {q: [None] * self.nd for q in self.dsem}
        self.res = {}

    def ensure(self, e, tok):
        if tok is None:
            return
        if tok[0] == "c":
            _, pe_, idx = tok
            if pe_ == e and e == "pe":
                return
            k = (e, pe_)
            if self.seen.get(k, 0) >= idx:
                return
            self.eng[e].wait_ge(self.sem[pe_], idx)
            self.seen[k] = idx
        else:
            _, q, slot, val = tok
            k = (e, q, slot)
            if self.seen.get(k, 0) >= val:
                return
            self.eng[e].wait_ge(self.dsem[q][slot], val)
            self.seen[k] = val

    def _deps(self, e, r, w):
        for k in r:
            st = self.res.get(k)
            if st:
                self.ensure(e, st["w"])
        for k in w:
            st = self.res.get(k)
            if st:
                self.ensure(e, st["w"])
                for t in st["r"].values():
                    self.ensure(e, t)

    def _upd(self, tok, rk, r, w):
        for k in w:
            self.res[k] = {"w": tok, "r": {}}
        for k in r:
            st = self.res.setdefault(k, {"w": None, "r": {}})
            st["r"][rk] = tok

    def op(self, e, fn, r=(), w=()):
        self._deps(e, r, w)
        inst = fn(self.eng[e])
        self.cnt[e] += 1
        inst.then_inc(self.sem[e], 1)
        tok = ("c", e, self.cnt[e])
        self._upd(tok, e, r, w)
        return tok

    def dma(self, q, out, in_, r=(), w=(), slow=False):
        self._deps(q, r, w)
        n = self.dcnt[q]
        slot = n % self.nd
        self.ensure(q, self.dlast[q][slot])
        val = 16 * (n // self.nd + 1)
        if slow:
            self.eng[q].dma_start(out=out, in_=in_, allow_slow_non_contiguous=True).then_inc(self.dsem[q][slot], 16)
        else:
            self.eng[q].dma_start(out=out, in_=in_).then_inc(self.dsem[q][slot], 16)
        tok = ("d", q, slot, val)
        self.dlast[q][slot] = tok
        self.dcnt[q] = n + 1
        self._upd(tok, ("d", q, slot), r, w)
        return tok

    def barrier(self):
        for e in ("pe", "act", "dve", "pool", "sp"):
            for c in self.cnt:
                if c != e and self.cnt[c] > 0:
                    self.ensure(e, ("c", c, self.cnt[c]))
            if e in self.cnt and e != "pe" and self.cnt[e] > 0:
                self.ensure(e, ("c", e, self.cnt[e]))
            for q in self.dsem:
                for t in self.dlast[q]:
                    self.ensure(e, t)
        self.res = {}


def bcast_last(ap, n):
    dims = [list(d) for d in ap.ap]
    if dims[-1][1] == 1 and len(dims) == 3:
        dims = dims[:2]
    return bass.AP(ap.tensor, ap.offset, dims + [[0, n]])


def build_program():
    nc = bass.Bass("TRN2", target_bir_lowering=False)
    _u = {"n": 0}
    def _un(n):
        _u["n"] += 1
        return "%s_%d" % (n, _u["n"])
    SB = lambda n, s, d: nc.sbuf_tensor(_un(n), s, d)
    PSM = lambda n, s, d: nc.psum_tensor(_un(n), s, d)
    dt_in = lambda n, s, d=F32: nc.dram_tensor(n, list(s), d, kind="ExternalInput").ap()
    okind = "ExternalOutput" if DEBUG else None
    DBGOUT = os.environ.get("KDBGOUT", "").split(",")
    def scr(n, s, d):
        if DEBUG and n in DBGOUT:
            return nc.dram_tensor(n, list(s), d, kind="ExternalOutput").ap()
        return nc.dram_tensor(n, list(s), d).ap()

    xall = dt_in("xall", [TA, D])
    keypen = dt_in("keypen", [128, NT])
    w_in = dt_in("w_in", [D, NIN])
    w_a = dt_in("w_a", [RW, D])
    w_b = dt_in("w_b", [RW, D])
    w_o = dt_in("w_o", [D, D])
    w_gu = dt_in("w_gu", [D, 2 * DFF])
    w_dn = dt_in("w_dn", [DFF, D])
    g1bc = dt_in("g1bc", [128, D])
    g2bc = dt_in("g2bc", [128, D])
    mu24 = dt_in("mu24", [128, 24])
    mul4 = dt_in("mul4", [128, 4])
    w0bc = dt_in("w0bc", [128, RW])
    w2 = dt_in("w2", [96, RW])
    a2 = dt_in("a2", [96, RW])
    g2 = dt_in("g2", [256, RW])
    vecp = dt_in("vecp", [128, 4, 8])
    gnwbc = dt_in("gnwbc", [128, RW])
    gnbbc = dt_in("gnbbc", [128, RW])
    qgbc = dt_in("qgbc", [128, RW])
    kgbc = dt_in("kgbc", [128, RW])
    fbbc = dt_in("fbbc", [128, 16])
    out = nc.dram_tensor("out", [TO, D], F32, kind="ExternalOutput").ap()

    ZR = scr("ZR", [3584, TA + 1], F32)
    ZFX = scr("ZFX", [TA, FC], F32)
    GT = scr("GT", [4096, TO], BF16)
    YAT = scr("YAT", [RW, TO], BF16)
    QTA = scr("QTA", [16, 67, TO], BF16)
    KTA = scr("KTA", [16, 67, TA], BF16)
    VFX = scr("VFX", [TA, RW], BF16)
    NB = scr("NB", [16, 128, NT], F32)
    YBT = scr("YBT", [RW, TO], BF16)
    H2 = scr("H2", [TO, D], F32)
    XN2T = scr("XN2T", [D, TO], BF16)
    ACTT = scr("ACTT", [DFF, TO], BF16)
    WDNB = scr("WDNB", [DFF, D], BF16)

    with ExitStack() as es:
        kb = KB(nc, es)
        es.enter_context(nc.Block())
        op, dma = kb.op, kb.dma

        ident = es.enter_context(SB("ident", [128, 128], BF16))
        identf = es.enter_context(SB("identf", [128, 128], F32))
        onesf = es.enter_context(SB("onesf", [128, 128], F32))
        zerof = es.enter_context(SB("zerof", [128, 32], F32))
        cst = es.enter_context(SB("cst", [128, 8], F32))
        mUs = es.enter_context(SB("mUs", [128, 128], F32))
        mUi = es.enter_context(SB("mUi", [128, 128], F32))
        mLs = es.enter_context(SB("mLs", [128, 128], F32))
        op("pool", lambda g: g.memset(identf[:], 1.0), w=["identf"])
        op("pool", lambda g: g.affine_select(out=identf[:], in_=identf[:], pattern=[[-1, 128]], compare_op=ALU.is_equal, fill=0.0, base=0, channel_multiplier=1), r=["identf"], w=["identf"])
        op("pool", lambda g: g.memset(onesf[:], 1.0), w=["onesf"])
        op("pool", lambda g: g.memset(zerof[:], 0.0), w=["zerof"])
        op("pool", lambda g: g.affine_select(out=mUs[:], in_=onesf[:], pattern=[[1, 128]], compare_op=ALU.is_gt, fill=0.0, base=0, channel_multiplier=-1), r=["onesf"], w=["mUs"])
        op("pool", lambda g: g.affine_select(out=mUi[:], in_=onesf[:], pattern=[[1, 128]], compare_op=ALU.is_ge, fill=0.0, base=0, channel_multiplier=-1), r=["onesf"], w=["mUi"])
        op("pool", lambda g: g.affine_select(out=mLs[:], in_=onesf[:], pattern=[[-1, 128]], compare_op=ALU.is_gt, fill=0.0, base=0, channel_multiplier=1), r=["onesf"], w=["mLs"])
        op("dve", lambda v: v.tensor_copy(out=ident[:], in_=identf[:]), r=["identf"], w=["ident"])
        for j, val in enumerate((1e-6, 64e-5, 1e-24, 1.0)):
            op("pool", lambda g, j=j, val=val: g.memset(cst[:, j:j + 1], val), w=[("cst", j)])
        kb.barrier()

        if "A" in STAGES:
            with ExitStack() as sa:
                xnT = sa.enter_context(SB("xnT", [128, 16, TP], BF16))
                g1t = sa.enter_context(SB("g1t", [128, D], F32))
                xt = [sa.enter_context(SB(f"xt{i}", [128, D], F32)) for i in range(2)]
                xnbs = [sa.enter_context(SB(f"xnb{i}", [128, D], BF16)) for i in range(2)]
                junk = sa.enter_context(SB("junk", [128, D], BF16))
                st1s = [sa.enter_context(SB(f"st1{i}", [128, 4], F32)) for i in range(2)]
                wf = [sa.enter_context(SB(f"wf{i}", [128, 16, 256], F32)) for i in range(2)]
                wb = [sa.enter_context(SB(f"wb{i}", [128, 16, 256], BF16)) for i in range(2)]
                og = [sa.enter_context(SB(f"og{i}", [128, 512], F32)) for i in range(4)]
                ogb = [sa.enter_context(SB(f"ogb{i}", [128, 512], BF16)) for i in range(2)]
                pst = [sa.enter_context(PSM(f"pst{i}", [128, 8, 128], BF16)) for i in range(2)]
                psg = [sa.enter_context(PSM(f"psg{i}", [128, 512], F32)) for i in range(4)]

                dma("sp", g1t[:], g1bc[:, :], w=["g1t"])
                dma("sp", ZR[:, 0:1].rearrange("(b p) o -> p b o", p=128), zerof[:, 0:28].rearrange("p (b o) -> p b o", o=1), r=["zerof"], w=[("ZRc0",)], slow=True)

                def norm_pass(tok0, ntiles):
                    for i in range(ntiles):
                        b = i % 2
                        xnb = xnbs[b]; st1 = st1s[b]
                        dma("sp", xt[b][:], xall[tok0 + i * 128: tok0 + (i + 1) * 128, :], w=[("xt", b)])
                        op("dve", lambda v: v.memset(st1[:, 0:1], 0.0), w=[("ss", b)])
                        op("act", lambda a: a.activation(out=junk[:], in_=xt[b][:], func=AF.Square, accum_out=st1[:, 0:1]), r=[("xt", b), ("ss", b)], w=[("ss", b)])
                        op("act", lambda a: a.activation(out=st1[:, 1:2], in_=st1[:, 0:1], func=AF.Ln, bias=cst[:, 0:1], scale=1.0 / D), r=[("ss", b)], w=[("lnv", b)])
                        op("act", lambda a: a.activation(out=st1[:, 2:3], in_=st1[:, 1:2], func=AF.Exp, scale=-0.5), r=[("lnv", b)], w=[("rstd", b)])
                        op("dve", lambda v: v.scalar_tensor_tensor(out=xnb[:], in0=xt[b][:], scalar=st1[:, 2:3], in1=g1t[:], op0=ALU.mult, op1=ALU.mult), r=[("xt", b), ("rstd", b), "g1t"], w=[("xnb", b)])
                        for hh in range(2):
                            for c in range(8):
                                kc = hh * 8 + c
                                op("pe", lambda t: t.transpose(out=pst[hh][:, c, :], in_=xnb[:, kc * 128:(kc + 1) * 128], identity=ident[:]), r=[("xnb", b), "ident"], w=[("pst", hh)])
                            e = "act" if hh == 0 else "dve"
                            if e == "act":
                                op("act", lambda a: a.copy(out=xnT[:, hh * 8:(hh + 1) * 8, i * 128:(i + 1) * 128], in_=pst[hh][:]), r=[("pst", hh)], w=[("xnT", i)])
                            else:
                                op("dve", lambda v: v.tensor_copy(out=xnT[:, hh * 8:(hh + 1) * 8, i * 128:(i + 1) * 128], in_=pst[hh][:]), r=[("pst", hh)], w=[("xnT", i)])

                state = {"ld": 0, "pg": 0, "og": 0, "ogb": 0, "ev": 0}

                def load_w(c0, w):
                    b = state["ld"] % 2
                    state["ld"] += 1
                    for q4 in range(4):
                        dma("sp", wf[b][:, q4 * 4:(q4 + 1) * 4, 0:w], w_in[q4 * 512:(q4 + 1) * 512, c0:c0 + w].rearrange("(kc p) n -> p kc n", p=128), w=[("wf", b, q4)])
                    for q4 in range(4):
                        op("pool", lambda g: g.tensor_copy(out=wb[b][:, q4 * 4:(q4 + 1) * 4, 0:w], in_=wf[b][:, q4 * 4:(q4 + 1) * 4, 0:w]), r=[("wf", b, q4)], w=[("wb", b, q4)])
                    return b

                def evac_copy(dst, src, rk, wk):
                    state["ev"] += 1
                    if state["ev"] % 2:
                        op("act", lambda a: a.copy(out=dst, in_=src), r=rk, w=wk)
                    else:
                        op("dve", lambda v: v.tensor_copy(out=dst, in_=src), r=rk, w=wk)

                def fm_job(b, c0, w, t0, t1, ntile_used, kind):
                    for sub in range(0, w, 128):
                        m = min(128, w - sub)
                        for ts in range(t0, t1, 512):
                            n = min(512, t1 - ts)
                            pi = state["pg"] % 4
                            state["pg"] += 1
                            for kc in range(16):
                                op("pe", lambda t: t.matmul(psg[pi][0:m, 0:n], lhsT=wb[b][:, kc, sub:sub + m], rhs=xnT[:, kc, ts:ts + n], start=(kc == 0), stop=(kc == 15)),
                                   r=[("wb", b, kc // 4)] + [("xnT", ti) for ti in range(ts // 128, (ts + n + 127) // 128)], w=[("psg", pi)])
                            col = c0 + sub
                            if kind == "rw":
                                oi = state["og"] % 4
                                state["og"] += 1
                                evac_copy(og[oi][0:m, 0:n], psg[pi][0:m, 0:n], [("psg", pi)], [("og", oi)])
                                dma("sp", ZR[col:col + m, 1 + ntile_used + ts: 1 + ntile_used + ts + n], og[oi][0:m, 0:n], r=[("og", oi)], w=[("ZR", col, ntile_used + ts)])
                            else:
                                oi = state["ogb"] % 2
                                state["ogb"] += 1
                                op("act", lambda a: a.activation(out=ogb[oi][0:m, 0:n], in_=psg[pi][0:m, 0:n], func=AF.Sigmoid), r=[("psg", pi)], w=[("ogb", oi)])
                                dma("sp", GT[col - 6608:col - 6608 + m, ts:ts + n], ogb[oi][0:m, 0:n], r=[("ogb", oi)], w=[("GT", col, ts)])

                def tm_job(b, c0, w, ntiles, tokbase):
                    for i in range(ntiles):
                        pi = state["pg"] % 4
                        state["pg"] += 1
                        for kc in range(16):
                            op("pe", lambda t: t.matmul(psg[pi][:, 0:w], lhsT=xnT[:, kc, i * 128:(i + 1) * 128], rhs=wb[b][:, kc, 0:w], start=(kc == 0), stop=(kc == 15)),
                               r=[("wb", b, kc // 4), ("xnT", i)], w=[("psg", pi)])
                        oi = state["og"] % 4
                        state["og"] += 1
                        evac_copy(og[oi][:, 0:w], psg[pi][:, 0:w], [("psg", pi)], [("og", oi)])
                        dma("sp", ZFX[tokbase + i * 128: tokbase + (i + 1) * 128, c0 - RC:c0 - RC + w], og[oi][:, 0:w], r=[("og", oi)], w=[("ZFX", c0, tokbase + i)])

                def run_jobs(jobs):
                    nxt = load_w(jobs[0][0], jobs[0][1])
                    for ji, (c0, w, fn) in enumerate(jobs):
                        cur = nxt
                        if ji + 1 < len(jobs):
                            nxt = load_w(jobs[ji + 1][0], jobs[ji + 1][1])
                        fn(cur)

                def col_blocks(lo, hi):
                    return [(c, min(256, hi - c)) for c in range(lo, hi, 256)]

                norm_pass(0, NPT)
                jobs = []
                for (c0, w) in col_blocks(1024, 3264):
                    jobs.append((c0, w, lambda b, c0=c0, w=w: fm_job(b, c0, w, 0, TP, 0, "rw")))
                for (c0, w) in col_blocks(0, 1024) + col_blocks(3264, 3520):
                    jobs.append((c0, w, lambda b, c0=c0, w=w: fm_job(b, c0, w, TP - 128, TP, 0, "rw")))
                for (c0, w) in col_blocks(4544, 6608):
                    jobs.append((c0, w, lambda b, c0=c0, w=w: tm_job(b, c0, w, NPT, 0)))
                run_jobs(jobs)
                norm_pass(TP, NOT_)
                jobs = []
                for (c0, w) in col_blocks(0, 3520):
                    jobs.append((c0, w, lambda b, c0=c0, w=w: fm_job(b, c0, w, 0, TO, TP, "rw")))
                for (c0, w) in col_blocks(3520, 6608):
                    jobs.append((c0, w, lambda b, c0=c0, w=w: tm_job(b, c0, w, NOT_, TP)))
                for (c0, w) in col_blocks(6608, NIN):
                    jobs.append((c0, w, lambda b, c0=c0, w=w: fm_job(b, c0, w, 0, TO, TP, "gt")))
                run_jobs(jobs)
                kb.barrier()

        if "B" in STAGES:
            with ExitStack() as sb:
                T = lambda n, s, d=F32: sb.enter_context(SB(n, list(s), d))
                PS = lambda n, s, d=F32: sb.enter_context(PSM(n, list(s), d))
                mask4 = T("mask4", [128, 512])
                maskL4 = T("maskL4", [128, 512])
                identb4 = T("identb4", [128, 512], BF16)
                tri2 = T("tri2", [128, 256])
                bones = T("bones", [128, 128], BF16)
                hsel = T("hsel", [128, 2], BF16)
                mu24t = T("mu24t", [128, 24]); mul4t = T("mul4t", [128, 4])
                w0t = T("w0t", [128, RW]); vp = T("vp", [128, 4, 8])
                gnwt = T("gnwt", [128, RW]); gnbt = T("gnbt", [128, RW])
                w2b = T("w2b", [96, RW], BF16); a2b = T("a2b", [96, RW], BF16); g2b = T("g2b", [128, 2, RW], BF16)
                zin = T("zin", [128, 24, 129]); zl = T("zl", [128, 4, 129])
                zs = T("zs", [128, 24, 128]); zls = T("zls", [128, 4, 128])
                lt2 = [T(f"lt{i}", [128, 4, 128], BF16) for i in range(2)]
                gam2 = [T(f"gam{i}", [128, 8]) for i in range(2)]
                sgT = T("sgT", [128, RW])
                e1 = T("e1", [128, 8, 128]); e2 = T("e2", [128, 8, 128]); e3 = T("e3", [128, 8, 128])
                alpha = T("alpha", [128, 8, 128])
                kkr = T("kkr", [128, 8, 128]); sqb = T("sqb", [128, 8, 128], BF16)
                rn = T("rn", [128, 8, 128]); kk = T("kk", [128, 8, 128])
                t1 = T("t1", [128, 8, 128]); kf = T("kf", [128, 8, 128]); bb = T("bb", [128, 8, 128])
                AR2 = [T(f"AR{i}", [128, 8, 2, 128], BF16) for i in range(2)]
                KB2 = T("KB2", [128, 8, 2, 128], BF16)
                prod2 = [T(f"prod{i}", [128, 8, 128], BF16) for i in range(2)]
                vsb = T("vsb", [128, 8, 128], BF16)
                Vtok2 = [T(f"Vtok{i}", [128, RW], BF16) for i in range(2)]
                KBtok2 = [T(f"KBtok{i}", [128, 2, RW], BF16) for i in range(2)]
                A4 = T("A4", [128, 16, 512], BF16)
                Nj = [T(f"Nj{i}", [128, 16, 128], BF16) for i in range(2)]
                Mj = [T(f"Mj{i}", [128, 16, 128], BF16) for i in range(2)]
                Qj = [T(f"Qj{i}", [128, 16, 128], BF16) for i in range(2)]
                Hf = T("Hf", [128, 8, 64]); Hb = T("Hb", [128, 8, 64], BF16)
                Xb = T("Xb", [128, RW], BF16); Ub = T("Ub", [128, RW], BF16)
                ysb = T("ysb", [128, 16, 64]); ysq = T("ysq", [128, 16, 64])
                gst = T("gst", [128, 6, 16])
                bon = T("bon", [128, 16])
                yab = T("yab", [128, RW], BF16)
                yaT = T("yaT", [128, 8, 128], BF16)
                psA = [PS(f"psA{i}", [128, 512]) for i in range(4)]
                psB = [PS(f"psB{i}", [128, 512]) for i in range(2)]
                psTb = [PS(f"psTb{i}", [128, 8, 128], BF16) for i in range(2)]

                for j in range(4):
                    src = mUs if j % 2 == 0 else mUi
                    op("dve", lambda v: v.tensor_copy(out=mask4[:, j * 128:(j + 1) * 128], in_=src[:]), w=[("mask4", j)])
                    op("dve", lambda v: v.tensor_copy(out=maskL4[:, j * 128:(j + 1) * 128], in_=mLs[:]), w=[("maskL4", j)])
                    op("dve", lambda v: v.tensor_copy(out=identb4[:, j * 128:(j + 1) * 128], in_=identf[:]), w=[("identb4", j)])
                op("dve", lambda v: v.tensor_scalar(out=tri2[:, 0:128], in0=mUi[:], scalar1=-DECAY_C, scalar2=None, op0=ALU.mult), w=[("tri2", 0)])
                op("dve", lambda v: v.tensor_scalar(out=tri2[:, 128:256], in0=mUs[:], scalar1=-DECAY_C, scalar2=None, op0=ALU.mult), w=[("tri2", 1)])
                op("pool", lambda g: g.memset(bones[:], 0.0), w=["bones"])
                op("pool", lambda g: g.memset(bones[0:64, 0:64], 1.0), r=["bones"], w=["bones"])
                op("pool", lambda g: g.memset(bones[64:128, 64:128], 1.0), r=["bones"], w=["bones"])
                op("pool", lambda g: g.memset(hsel[:], 0.0), w=["hsel"])
                op("pool", lambda g: g.memset(hsel[0:64, 0:1], 1.0), r=["hsel"], w=["hsel"])
                op("pool", lambda g: g.memset(hsel[64:128, 1:2], 1.0), r=["hsel"], w=["hsel"])
                dma("sp", mu24t[:], mu24[:, :], w=["mu24t"]); dma("sp", mul4t[:], mul4[:, :], w=["mul4t"])
                dma("sp", w0t[:], w0bc[:, :], w=["w0t"]); dma("sp", vp[:], vecp[:, :, :], w=["vp"])
                dma("sp", gnwt[:], gnwbc[:, :], w=["gnwt"]); dma("sp", gnbt[:], gnbbc[:, :], w=["gnbt"])
                ldf = zs[:, 0:16, :].rearrange("p (c a) t -> p c (a t)", c=2)
                dma("sp", ldf[0:96, 0, :], w2[:, :], w=[("zs", 0), ("zs", 8)])
                op("dve", lambda v: v.tensor_copy(out=w2b[:], in_=ldf[0:96, 0, :]), r=[("zs", 0), ("zs", 8)], w=["w2b"])
                dma("sp", ldf[0:96, 1, :], a2[:, :], r=[("zs", 0), ("zs", 8)], w=[("zs", 0), ("zs", 8)])
                op("dve", lambda v: v.tensor_copy(out=a2b[:], in_=ldf[0:96, 1, :]), r=[("zs", 0), ("zs", 8)], w=["a2b"])
                dma("sp", ldf[:, :, :], g2.rearrange("(c p) n -> p c n", p=128), r=[("zs", 0), ("zs", 8)], w=[("zs", 0), ("zs", 8)])
                op("dve", lambda v: v.tensor_copy(out=g2b[:], in_=ldf[:, :, :]), r=[("zs", 0), ("zs", 8)], w=["g2b"])
                for i in range(2):
                    op("pool", lambda g: g.memset(AR2[i][:], 0.0), w=[("ARa", i), ("ARr", i)])
                op("dve", lambda v: v.memset(Hf[:], 0.0), w=["Hf"])
                op("dve", lambda v: v.memset(Hb[:], 0.0), w=["Hb"])
                ZR3 = ZR.rearrange("(b p) t -> p b t", p=128)

                BT = [int(t) for t in os.environ.get("KBTILES", "").split(",") if t] or list(range(NT))
                hb = lambda h: (h // 2, 64 * (h % 2))
                ek = [("e1", b_) for b_ in range(8)]; e2k = [("e2", b_) for b_ in range(8)]; e3k = [("e3", b_) for b_ in range(8)]
                alk = [("alpha", b_) for b_ in range(8)]
                kkrk = [("kkr", b_) for b_ in range(8)]
                t1k = [("t1", b_) for b_ in range(8)]

                def prep(ti, p):
                    own = ti >= NPT
                    AR, lt, gam, prod, Vtok, KBtok = AR2[p], lt2[p], gam2[p], prod2[p], Vtok2[p], KBtok2[p]
                    c0 = ti * 128
                    blo = 0 if own else 8
                    dma("sp", zin[:, blo:24, :], ZR3[:, blo:24, c0:c0 + 129], w=["zin"])
                    nl = 4 if own else 2
                    dma("sp", zl[0:96, 0, :], ZR[3072:3168, c0:c0 + 129], w=["zl0"])
                    dma("sp", zl[0:96, 1, :], ZR[3168:3264, c0:c0 + 129], w=["zl1"])
                    if own:
                        dma("sp", zl[:, 2:4, :], ZR[3264:3520, c0:c0 + 129].rearrange("(b p) t -> p b t", p=128), w=["zl2"])
                    yield
                    for (g0, g1) in ((8, 16), (16, 24), (0, 8)):
                        if g1 <= blo:
                            continue
                        zk = ("zs", g0)
                        op("pool", lambda g: g.tensor_tensor(out=zs[:, g0:g1, :], in0=zin[:, g0:g1, 0:128], in1=zin[:, g0:g1, 1:129], op=ALU.subtract), r=["zin"], w=[zk])
                        op("pool", lambda g: g.tensor_tensor(out=zs[:, g0:g1, :], in0=zs[:, g0:g1, :], in1=bcast_last(mu24t[:, g0:g1], 128), op=ALU.mult), r=[zk, "mu24t"], w=[zk])
                        op("pool", lambda g: g.tensor_tensor(out=zs[:, g0:g1, :], in0=zs[:, g0:g1, :], in1=zin[:, g0:g1, 1:129], op=ALU.add), r=[zk, "zin"], w=[zk])
                        yield
                    zK, zV, zR = ("zs", 8), ("zs", 16), ("zs", 0)
                    for j in range(nl):
                        pr = 96 if j < 2 else 128
                        zk = "zl%d" % min(j, 2)
                        op("dve", lambda v: v.tensor_tensor(out=zls[0:pr, j, :], in0=zl[0:pr, j, 0:128], in1=zl[0:pr, j, 1:129], op=ALU.subtract), r=[zk], w=[("zls", j)])
                        op("dve", lambda v: v.scalar_tensor_tensor(out=zls[0:pr, j, :], in0=zls[0:pr, j, :], scalar=mul4t[0:pr, j:j + 1], in1=zl[0:pr, j, 1:129], op0=ALU.mult, op1=ALU.add), r=[("zls", j), zk, "mul4t"], w=[("zls", j)])
                    yield
                    op("act", lambda a: a.activation(out=lt[0:96, 0, :], in_=zls[0:96, 0, :], func=AF.Tanh), r=[("zls", 0)], w=[("lt", 0, p)])
                    op("act", lambda a: a.copy(out=lt[0:96, 1, :], in_=zls[0:96, 1, :]), r=[("zls", 1)], w=[("lt", 1, p)])
                    if own:
                        op("act", lambda a: a.activation(out=lt[:, 2:4, :], in_=zls[:, 2:4, :], func=AF.Sigmoid), r=[("zls", 2), ("zls", 3)], w=[("lt", 2, p)])
                    yield
                    for hh in range(2):
                        op("pe", lambda t: t.matmul(psB[hh][:, :], lhsT=lt[0:96, 0, :], rhs=w2b[:, hh * 512:(hh + 1) * 512], start=True, stop=True), r=[("lt", 0, p), "w2b"], w=[("psB", hh)])
                        op("dve", lambda v: v.tensor_tensor(out=sgT[:, hh * 512:(hh + 1) * 512], in0=psB[hh][:, :], in1=w0t[:, hh * 512:(hh + 1) * 512], op=ALU.add), r=[("psB", hh), "w0t"], w=[("sgT", hh)])
                        op("act", lambda a: a.activation(out=sgT[:, hh * 512:(hh + 1) * 512], in_=sgT[:, hh * 512:(hh + 1) * 512], func=AF.Sigmoid), r=[("sgT", hh)], w=[("sgT", hh)])
                        yield
                    for hh in range(2):
                        for c in range(4):
                            blk = hh * 4 + c
                            op("pe", lambda t: t.matmul(psB[hh][:, c * 128:(c + 1) * 128], lhsT=a2b[:, blk * 128:(blk + 1) * 128], rhs=lt[0:96, 1, :], start=True, stop=True), r=[("lt", 1, p), "a2b"], w=[("psB", hh)])
                        for c in range(4):
                            blk = hh * 4 + c
                            op("act", lambda a: a.activation(out=alpha[:, blk, :], in_=psB[hh][:, c * 128:(c + 1) * 128], func=AF.Sigmoid, bias=vp[:, 0, blk:blk + 1]), r=[("psB", hh), "vp"], w=[("alpha", blk)])
                        yield
                    for blk in range(8):
                        pi = blk % 2
                        op("pe", lambda t: t.matmul(psB[pi][:, 0:256], lhsT=sgT[:, blk * 128:(blk + 1) * 128], rhs=tri2[:, :], start=True, stop=True), r=[("sgT", blk // 4), ("tri2", 0), ("tri2", 1)], w=[("psB", pi)])
                        op("act", lambda a: a.activation(out=e1[:, blk, :], in_=psB[pi][:, 0:128], func=AF.Exp), r=[("psB", pi)], w=[("e1", blk)])
                        op("act", lambda a: a.activation(out=e2[:, blk, :], in_=psB[pi][:, 0:128], func=AF.Exp, scale=-1.0), r=[("psB", pi)], w=[("e2", blk)])
                        op("act", lambda a: a.activation(out=e3[:, blk, :], in_=psB[pi][:, 128:256], func=AF.Exp), r=[("psB", pi)], w=[("e3", blk)])
                        yield
                    op("act", lambda a: a.copy(out=gam[:], in_=e1[:, :, 127]), r=ek, w=[("gam", p)])
                    for blk in range(8):
                        op("dve", lambda v: v.tensor_scalar(out=kkr[:, blk, :], in0=zs[:, 8 + blk, :], scalar1=vp[:, 1, blk:blk + 1], scalar2=None, op0=ALU.mult), r=[zK, "vp"], w=[("kkr", blk)])
                    op("dve", lambda v: v.tensor_tensor(out=sqb[:], in0=kkr[:], in1=kkr[:], op=ALU.mult), r=kkrk, w=["sqb"])
                    yield
                    for hh in range(2):
                        for c in range(4):
                            blk = hh * 4 + c
                            op("pe", lambda t: t.matmul(psB[hh][:, c * 128:(c + 1) * 128], lhsT=bones[:], rhs=sqb[:, blk, :], start=True, stop=True), r=["sqb", "bones"], w=[("psB", hh)])
                        op("act", lambda a: a.activation(out=rn[:, hh * 4:(hh + 1) * 4, :], in_=psB[hh][:, :].rearrange("p (c t) -> p c t", c=4), func=AF.Ln, bias=cst[:, 2:3]), r=[("psB", hh)], w=[("rn", hh)])
                        op("act", lambda a: a.activation(out=rn[:, hh * 4:(hh + 1) * 4, :], in_=rn[:, hh * 4:(hh + 1) * 4, :], func=AF.Exp, scale=-0.5), r=[("rn", hh)], w=[("rn", hh)])
                        yield
                    op("dve", lambda v: v.tensor_tensor(out=kk[:], in0=kkr[:], in1=rn[:], op=ALU.mult), r=kkrk + [("rn", 0), ("rn", 1)], w=["kk"])
                    for blk in range(8):
                        op("dve", lambda v: v.tensor_scalar(out=t1[:, blk, :], in0=alpha[:, blk, :], scalar1=-1.0, scalar2=vp[:, 2, blk:blk + 1], op0=ALU.add, op1=ALU.mult), r=[("alpha", blk), "vp"], w=[("t1", blk)])
                    yield
                    op("dve", lambda v: v.scalar_tensor_tensor(out=kf[:], in0=t1[:], scalar=1.0, in1=zs[:, 8:16, :], op0=ALU.add, op1=ALU.mult), r=t1k + [zK], w=["kf"])
                    op("pool", lambda g: g.tensor_tensor(out=bb[:], in0=kk[:], in1=alpha[:], op=ALU.mult), r=["kk"] + alk, w=["bb"])
                    yield
                    op("dve", lambda v: v.scalar_tensor_tensor(out=AR[:, :, 0, :], in0=kk[:], scalar=-1.0, in1=e3[:], op0=ALU.mult, op1=ALU.mult), r=["kk"] + e3k, w=[("ARa", p)])
                    if own:
                        op("pool", lambda g: g.tensor_tensor(out=AR[:, :, 1, :], in0=zs[:, 0:8, :], in1=e1[:], op=ALU.mult), r=[zR] + ek, w=[("ARr", p)])
                    yield
                    op("dve", lambda v: v.tensor_tensor(out=KB2[:, :, 0, :], in0=kf[:], in1=e2[:], op=ALU.mult), r=["kf"] + e2k, w=["KBk"])
                    op("pool", lambda g: g.tensor_tensor(out=KB2[:, :, 1, :], in0=bb[:], in1=e2[:], op=ALU.mult), r=["bb"] + e2k, w=["KBb"])
                    op("act", lambda a: a.copy(out=vsb[:], in_=zs[:, 16:24, :]), r=[zV], w=["vsb"])
                    yield
                    for c in range(8):
                        op("pe", lambda t: t.transpose(out=psTb[0][:, c, :], in_=vsb[:, c, :], identity=ident[:]), r=["vsb", "ident"], w=[("psTb", 0)])
                    op("act", lambda a: a.copy(out=Vtok[:], in_=psTb[0][:].rearrange("p c t -> p (c t)")), r=[("psTb", 0)], w=[("Vtok", p)])
                    yield
                    for j in range(2):
                        for c in range(8):
                            op("pe", lambda t: t.transpose(out=psTb[1][:, c, :], in_=KB2[:, c, j, :], identity=ident[:]), r=["KBk", "KBb", "ident"], w=[("psTb", 1)])
                        op("dve", lambda v: v.tensor_copy(out=KBtok[:, j, :], in_=psTb[1][:].rearrange("p c t -> p (c t)")), r=[("psTb", 1)], w=[("KBtok", j, p)])
                        yield
                    if own:
                        for blk in range(8):
                            op("dve", lambda v: v.scalar_tensor_tensor(out=prod[:, blk, :], in0=zs[:, blk, :], scalar=vp[:, 3, blk:blk + 1], in1=kf[:, blk, :], op0=ALU.mult, op1=ALU.mult), r=[zR, "kf", "vp"], w=[("prod", blk, p)])
                        yield

                def mm(ti, p):
                    AR = AR2[p]
                    for h in range(16):
                        blk, pb = hb(h)
                        pi = h % 4
                        rd = [("ARa", p), "KBk", "KBb", ("ARr", p)]
                        op("pe", lambda t: t.matmul(psA[pi][:, 0:256], lhsT=KB2[pb:pb + 64, blk, 1, :], rhs=AR[pb:pb + 64, blk, :, :].rearrange("p a t -> p (a t)"), start=True, stop=True), r=rd, w=[("psA", pi)])
                        op("pe", lambda t: t.matmul(psA[pi][:, 256:512], lhsT=KB2[pb:pb + 64, blk, 0, :], rhs=AR[pb:pb + 64, blk, :, :].rearrange("p a t -> p (a t)"), start=True, stop=True), r=rd, w=[("psA", pi)])
                        op("dve", lambda v: v.tensor_tensor(out=A4[:, h, :], in0=psA[pi][:, :], in1=mask4[:], op=ALU.mult), r=[("psA", pi)] + [("mask4", j) for j in range(4)], w=[("A4", h)])
                    for hg in range(4):
                        hk = [("A4", hg * 4 + i) for i in range(4)]
                        op("pool", lambda g: g.tensor_copy(out=Mj[0][:, hg * 4:(hg + 1) * 4, :], in_=A4[:, hg * 4:(hg + 1) * 4, 0:128]), r=hk, w=[("M0", hg)])
                        op("pool", lambda g: g.tensor_tensor(out=Qj[0][:, hg * 4:(hg + 1) * 4, :], in0=A4[:, hg * 4:(hg + 1) * 4, 0:128], in1=identb4[:].rearrange("p (h t) -> p h t", h=4), op=ALU.add), r=hk + [("identb4", j) for j in range(4)], w=[("Q0", hg)])
                    for g8 in range(2):
                        for par in range(2):
                            for i in range(4):
                                h = g8 * 8 + 2 * i + par
                                blk, pb = hb(h)
                                op("pe", lambda t: t.matmul(psA[2 + par][:, i * 128:(i + 1) * 128], lhsT=AR[pb:pb + 64, blk, 0, :], rhs=KB2[pb:pb + 64, blk, 1, :], start=True, stop=True), r=[("ARa", p), "KBb"], w=[("psA", 2 + par)])
                        for par in range(2):
                            op("dve", lambda v: v.tensor_tensor(out=Nj[0][:, g8 * 8 + par:g8 * 8 + 8:2, :], in0=psA[2 + par][:, :].rearrange("p (h t) -> p h t", h=4), in1=maskL4[:].rearrange("p (h t) -> p h t", h=4), op=ALU.mult), r=[("psA", 2 + par)] + [("maskL4", j) for j in range(4)], w=[("N0", 2 * g8), ("N0", 2 * g8 + 1)])

                def inverse(ti):
                    bank = [0]
                    def nb():
                        bank[0] += 1
                        return bank[0] % 4
                    for lvl in range(6):
                        ci, ni = lvl % 2, (lvl + 1) % 2
                        for hg in range(4):
                            pi = nb()
                            for i in range(4):
                                h = hg * 4 + i
                                op("pe", lambda t: t.matmul(psA[pi][:, i * 128:(i + 1) * 128], lhsT=Mj[ci][:, h, :], rhs=Nj[ci][:, h, :], start=True, stop=True), r=[("M%d" % ci, hg), ("N%d" % ci, hg)], w=[("psA", pi)])
                            op("act", lambda a: a.copy(out=Nj[ni][:, hg * 4:(hg + 1) * 4, :].rearrange("p h t -> p (h t)"), in_=psA[pi][:, :]), r=[("psA", pi)], w=[("N%d" % ni, hg)])
                            yield
                        if lvl < 5:
                            for hg in range(4):
                                pi = nb()
                                for i in range(4):
                                    h = hg * 4 + i
                                    op("pe", lambda t: t.matmul(psA[pi][:, i * 128:(i + 1) * 128], lhsT=Nj[ci][:, h, :], rhs=Mj[ci][:, h, :], start=True, stop=True), r=[("M%d" % ci, hg), ("N%d" % ci, hg)], w=[("psA", pi)])
                                op("act", lambda a: a.copy(out=Mj[ni][:, hg * 4:(hg + 1) * 4, :].rearrange("p h t -> p (h t)"), in_=psA[pi][:, :]), r=[("psA", pi)], w=[("M%d" % ni, hg)])
                                yield
                        for hg in range(4):
                            pi = nb()
                            for i in range(4):
                                h = hg * 4 + i
                                op("pe", lambda t: t.matmul(psA[pi][:, i * 128:(i + 1) * 128], lhsT=Nj[ni][:, h, :], rhs=Qj[ci][:, h, :], start=True, stop=True), r=[("N%d" % ni, hg), ("Q%d" % ci, hg)], w=[("psA", pi)])
                            op("dve", lambda v: v.tensor_tensor(out=Qj[ni][:, hg * 4:(hg + 1) * 4, :].rearrange("p h t -> p (h t)"), in0=psA[pi][:, :], in1=Qj[ci][:, hg * 4:(hg + 1) * 4, :].rearrange("p h t -> p (h t)"), op=ALU.add), r=[("psA", pi), ("Q%d" % ci, hg)], w=[("Q%d" % ni, hg)])
                            yield

                def chain(ti, p):
                    own = ti >= NPT
                    AR, lt, gam, prod, Vtok, KBtok = AR2[p], lt2[p], gam2[p], prod2[p], Vtok2[p], KBtok2[p]
                    QT = Qj[0]
                    vtk = ("Vtok", p)
                    for hh in range(2):
                        for i in range(8):
                            h = hh * 8 + i
                            blk, pb = hb(h)
                            op("pe", lambda t: t.matmul(psB[hh][:, i * 64:(i + 1) * 64], lhsT=AR[pb:pb + 64, blk, 0, :], rhs=Hb[pb:pb + 64, blk, :], start=True, stop=False), r=[("ARa", p), "Hb"], w=[("psB", hh)])
                            op("pe", lambda t: t.matmul(psB[hh][:, i * 64:(i + 1) * 64], lhsT=A4[:, h, 256:384], rhs=Vtok[:, h * 64:(h + 1) * 64], start=False, stop=True), r=[("A4", h), vtk], w=[("psB", hh)])
                        if hh == 0:
                            op("act", lambda a: a.copy(out=Xb[:, 0:512], in_=psB[0][:, :]), r=[("psB", 0)], w=[("Xb", 0)])
                        else:
                            op("dve", lambda v: v.tensor_copy(out=Xb[:, 512:1024], in_=psB[1][:, :]), r=[("psB", 1)], w=[("Xb", 1)])
                    for hh in range(2):
                        for i in range(8):
                            h = hh * 8 + i
                            op("pe", lambda t: t.matmul(psB[hh][:, i * 64:(i + 1) * 64], lhsT=QT[:, h, :], rhs=Xb[:, h * 64:(h + 1) * 64], start=True, stop=True), r=[("Q0", h // 4), ("Xb", hh)], w=[("psB", hh)])
                        if hh == 0:
                            op("act", lambda a: a.copy(out=Ub[:, 0:512], in_=psB[0][:, :]), r=[("psB", 0)], w=[("Ub", 0)])
                        else:
                            op("dve", lambda v: v.tensor_copy(out=Ub[:, 512:1024], in_=psB[1][:, :]), r=[("psB", 1)], w=[("Ub", 1)])
                    if own:
                        for hh in range(2):
                            pi = 2 + hh
                            for i in range(8):
                                h = hh * 8 + i
                                blk, pb = hb(h)
                                op("pe", lambda t: t.matmul(psA[pi][:, i * 64:(i + 1) * 64], lhsT=AR[pb:pb + 64, blk, 1, :], rhs=Hb[pb:pb + 64, blk, :], start=True, stop=False), r=[("ARr", p), "Hb"], w=[("psA", pi)])
                                op("pe", lambda t: t.matmul(psA[pi][:, i * 64:(i + 1) * 64], lhsT=A4[:, h, 128:256], rhs=Ub[:, h * 64:(h + 1) * 64], start=False, stop=False), r=[("A4", h), ("Ub", hh)], w=[("psA", pi)])
                                op("pe", lambda t: t.matmul(psA[pi][:, i * 64:(i + 1) * 64], lhsT=A4[:, h, 384:512], rhs=Vtok[:, h * 64:(h + 1) * 64], start=False, stop=True), r=[("A4", h), vtk], w=[("psA", pi)])
                            op("act", lambda a: a.copy(out=ysb[:, hh * 8:(hh + 1) * 8, :].rearrange("p h v -> p (h v)"), in_=psA[pi][:, :]), r=[("psA", pi)], w=[("ysb", hh)])
                    for blk in range(8):
                        for half in range(2):
                            h = blk * 2 + half
                            pb = 64 * half
                            op("pe", lambda t: t.matmul(psA[0][pb:pb + 64, blk * 64:(blk + 1) * 64], lhsT=KBtok[:, 0, blk * 128 + pb: blk * 128 + pb + 64], rhs=Vtok[:, h * 64:(h + 1) * 64], start=True, stop=False), r=[("KBtok", 0, p), vtk], w=[("psA", 0)])
                            op("pe", lambda t: t.matmul(psA[0][pb:pb + 64, blk * 64:(blk + 1) * 64], lhsT=KBtok[:, 1, blk * 128 + pb: blk * 128 + pb + 64], rhs=Ub[:, h * 64:(h + 1) * 64], start=False, stop=True), r=[("KBtok", 1, p), ("Ub", h // 8)], w=[("psA", 0)])
                    op("dve", lambda v: v.tensor_tensor(out=Hf[:].rearrange("p b v -> p (b v)"), in0=psA[0][:, :], in1=Hf[:].rearrange("p b v -> p (b v)"), op=ALU.add), r=[("psA", 0), "Hf"], w=["Hf"])
                    op("dve", lambda v: v.tensor_tensor(out=Hf[:], in0=Hf[:], in1=bcast_last(gam[:, :], 64), op=ALU.mult), r=["Hf", ("gam", p)], w=["Hf"])
                    op("act", lambda a: a.copy(out=Hb[:], in_=Hf[:]), r=["Hf"], w=["Hb"])
                    if own:
                        oi = ti - NPT
                        yk = [("ysb", 0), ("ysb", 1)]
                        for hh in range(2):
                            for kc in range(2):
                                op("pe", lambda t: t.matmul(psB[hh][:, :], lhsT=lt[:, 2 + kc, :], rhs=g2b[:, kc, hh * 512:(hh + 1) * 512], start=(kc == 0), stop=(kc == 1)), r=[("lt", 2, p), "g2b"], w=[("psB", hh)])
                        for blk in range(8):
                            op("pe", lambda t: t.matmul(psA[1][:, blk * 2:(blk + 1) * 2], lhsT=prod[:, blk, :], rhs=hsel[:], start=True, stop=True), r=[("prod", blk, p), "hsel"], w=[("psA", 1)])
                        op("act", lambda a: a.copy(out=bon[:], in_=psA[1][:, 0:16]), r=[("psA", 1)], w=["bon"])
                        op("dve", lambda v: v.tensor_reduce(out=gst[:, 0, :], in_=ysb[:], axis=AX.X, op=ALU.add), r=yk, w=[("gst", 0)])
                        op("pool", lambda g: g.tensor_tensor(out=ysq[:], in0=ysb[:], in1=ysb[:], op=ALU.mult), r=yk, w=["ysq"])
                        op("dve", lambda v: v.tensor_reduce(out=gst[:, 1, :], in_=ysq[:], axis=AX.X, op=ALU.add), r=["ysq"], w=[("gst", 1)])
                        op("dve", lambda v: v.tensor_scalar(out=gst[:, 2, :], in0=gst[:, 0, :], scalar1=1.0 / 64, scalar2=None, op0=ALU.mult), r=[("gst", 0)], w=[("gst", 2)])
                        op("dve", lambda v: v.tensor_tensor(out=gst[:, 3, :], in0=gst[:, 2, :], in1=gst[:, 2, :], op=ALU.mult), r=[("gst", 2)], w=[("gst", 3)])
                        op("dve", lambda v: v.scalar_tensor_tensor(out=gst[:, 4, :], in0=gst[:, 1, :], scalar=1.0 / 64, in1=gst[:, 3, :], op0=ALU.mult, op1=ALU.subtract), r=[("gst", 1), ("gst", 3)], w=[("gst", 4)])
                        op("act", lambda a: a.activation(out=gst[:, 5, :], in_=gst[:, 4, :], func=AF.Ln, bias=cst[:, 1:2]), r=[("gst", 4)], w=[("gst", 5)])
                        op("act", lambda a: a.activation(out=gst[:, 5, :], in_=gst[:, 5, :], func=AF.Exp, scale=-0.5), r=[("gst", 5)], w=[("gst", 5)])
                        op("dve", lambda v: v.tensor_tensor(out=ysq[:], in0=ysb[:], in1=bcast_last(gst[:, 2, :], 64), op=ALU.subtract), r=yk + [("gst", 2), "ysq"], w=["ysq"])
                        op("dve", lambda v: v.tensor_tensor(out=ysq[:], in0=ysq[:], in1=bcast_last(gst[:, 5, :], 64), op=ALU.mult), r=["ysq", ("gst", 5)], w=["ysq"])
                        ysq2 = ysq[:].rearrange("p h v -> p (h v)")
                        op("dve", lambda v: v.tensor_tensor(out=ysq2, in0=ysq2, in1=gnwt[:], op=ALU.mult), r=["ysq", "gnwt"], w=["ysq"])
                        op("dve", lambda v: v.tensor_tensor(out=ysq2, in0=ysq2, in1=gnbt[:], op=ALU.add), r=["ysq", "gnbt"], w=["ysq"])
                        op("dve", lambda g: g.tensor_tensor(out=ysb[:], in0=Vtok[:].rearrange("p (h v) -> p h v", h=16), in1=bcast_last(bon[:, :], 64), op=ALU.mult), r=[vtk, "bon"] + yk, w=yk)
                        op("dve", lambda v: v.tensor_tensor(out=ysq[:], in0=ysq[:], in1=ysb[:], op=ALU.add), r=["ysq"] + yk, w=["ysq"])
                        for hh in range(2):
                            op("dve", lambda v: v.tensor_tensor(out=yab[:, hh * 512:(hh + 1) * 512], in0=ysq[:, hh * 8:(hh + 1) * 8, :].rearrange("p h v -> p (h v)"), in1=psB[hh][:, :], op=ALU.mult), r=["ysq", ("psB", hh)], w=[("yab", hh)])
                        for c in range(8):
                            op("pe", lambda t: t.transpose(out=psTb[0][:, c, :], in_=yab[:, c * 128:(c + 1) * 128], identity=ident[:]), r=[("yab", c // 4), "ident"], w=[("psTb", 0)])
                        op("act", lambda a: a.copy(out=yaT[:], in_=psTb[0][:]), r=[("psTb", 0)], w=["yaT"])
                        dma("sp", YAT.rearrange("(c p) t -> p c t", p=128)[:, :, oi * 128:(oi + 1) * 128], yaT[:], r=["yaT"], w=[("YAT", oi)])

                def run_rr(gens):
                    gens = list(gens)
                    while gens:
                        for g_ in list(gens):
                            try:
                                next(g_)
                            except StopIteration:
                                gens.remove(g_)

                run_rr([prep(BT[0], 0)])
                for k_, ti in enumerate(BT):
                    p = k_ % 2
                    mm(ti, p)
                    gens = [inverse(ti)]
                    if k_ + 1 < len(BT):
                        gens.append(prep(BT[k_ + 1], 1 - p))
                    run_rr(gens)
                    chain(ti, p)
                kb.barrier()


        nbt = es.enter_context(SB("nbt", [128, 16, NT], F32))
        if "C" in STAGES:
            with ExitStack() as sc:
                T = lambda n, s, d=F32: sc.enter_context(SB(n, list(s), d))
                PS = lambda n, s, d=F32: sc.enter_context(PSM(n, list(s), d))
                qgt = T("qgt", [128, RW]); kgt = T("kgt", [128, RW]); fbt = T("fbt", [128, 16]); kpt = T("kpt", [128, NT])
                carry = T("carry", [128, 16])
                cT = T("cT", [16, TO]); r1 = T("r1", [16, TO]); chi = T("chi", [16, 3, TO], BF16)
                onesb = T("onesb", [16, 3, 1408], BF16)
                psT = [PS(f"psT{i}", [64, 16, 128], BF16) for i in range(2)]
                dma("sp", qgt[:], qgbc[:, :], w=["qgt"]); dma("sp", kgt[:], kgbc[:, :], w=["kgt"])
                dma("sp", fbt[:], fbbc[:, :], w=["fbt"]); dma("sp", kpt[:], keypen[:, :], w=["kpt"])
                op("dve", lambda v: v.tensor_scalar(out=qgt[:], in0=qgt[:], scalar1=0.125, scalar2=None, op0=ALU.mult), r=["qgt"], w=["qgt"])
                op("dve", lambda v: v.memset(carry[:], 0.0), w=["carry"])
                op("pool", lambda g: g.memset(onesb[:], 1.0), w=["onesb"])
                for c3 in range(3):
                    dma("sp", KTA[:, 64:67, c3 * 1408:(c3 + 1) * 1408], onesb[:], r=["onesb"], w=[("KTA1", c3)])

                NP_ = 3
                sqs = [T(f"sq{i}", [128, 16, 64]) for i in range(NP_)]; sts = [T(f"st{i}", [128, 4, 16]) for i in range(NP_)]
                nrms = [T(f"nrm{i}", [128, 16, 64]) for i in range(NP_)]
                sq2s = [T(f"sqq{i}", [128, 16, 64]) for i in range(NP_)]; st2s = [T(f"stq{i}", [128, 4, 16]) for i in range(NP_)]
                nrm2s = [T(f"nrmq{i}", [128, 16, 64]) for i in range(NP_)]
                zfs = [T(f"zfp{i}", [128, FC]) for i in range(NP_)]
                knbs = [T(f"knb{i}", [128, RW], BF16) for i in range(NP_)]; qnbs = [T(f"qnb{i}", [128, RW], BF16) for i in range(NP_)]
                vbs = [T(f"vb{i}", [128, RW], BF16) for i in range(NP_)]
                ktiles = [T(f"ktile{i}", [64, 16, 128], BF16) for i in range(NP_)]; qtiles = [T(f"qtile{i}", [64, 16, 128], BF16) for i in range(NP_)]
                fls = [T(f"fl{i}", [128, 4, 16]) for i in range(NP_)]; ctiles = [T(f"ctile{i}", [128, 16]) for i in range(NP_)]
                pscs = [PS(f"pscp{i}", [128, 512]) for i in range(2)]

                def rmsn(src, gt, dst, tag, dkey, sq, st, nrm, kk_):
                    s3 = src.rearrange("p (h v) -> p h v", h=16)
                    op("pool", lambda g: g.tensor_tensor(out=sq[:], in0=s3, in1=s3, op=ALU.mult), r=[tag], w=[("sq",) + kk_])
                    yield
                    op("dve", lambda v: v.tensor_reduce(out=st[:, 0, :], in_=sq[:], axis=AX.X, op=ALU.add), r=[("sq",) + kk_], w=[("st", 0) + kk_])
                    op("act", lambda a: a.activation(out=st[:, 1, :], in_=st[:, 0, :], func=AF.Ln, bias=cst[:, 0:1], scale=1.0 / 64), r=[("st", 0) + kk_], w=[("st", 1) + kk_])
                    op("act", lambda a: a.activation(out=st[:, 2, :], in_=st[:, 1, :], func=AF.Exp, scale=-0.5), r=[("st", 1) + kk_], w=[("st", 2) + kk_])
                    yield
                    op("dve", lambda v: v.tensor_tensor(out=nrm[:], in0=s3, in1=bcast_last(st[:, 2, :], 64), op=ALU.mult), r=[tag, ("st", 2) + kk_], w=[("nrm",) + kk_])
                    op("dve", lambda v: v.tensor_tensor(out=dst[:], in0=nrm[:].rearrange("p h v -> p (h v)"), in1=gt[:], op=ALU.mult), r=[("nrm",) + kk_, "qgt", "kgt"], w=[dkey])
                    yield

                def cprep(ti, p):
                    own = ti >= NPT
                    zfb = zfs[p]
                    zt = ("zf", p)
                    c_lo = 0 if own else 1024
                    dma("sp", zfb[:, c_lo:FC], ZFX[ti * 128:(ti + 1) * 128, c_lo:FC], w=[zt])
                    yield
                    fl = fls[p]; ctile = ctiles[p]; psc = pscs[p % 2]
                    op("dve", lambda v: v.tensor_tensor(out=fl[:, 0, :], in0=zfb[:, 3072:3088], in1=fbt[:], op=ALU.add), r=[zt, "fbt"], w=[("fl", 0, p)])
                    op("act", lambda a: a.activation(out=fl[:, 1, :], in_=fl[:, 0, :], func=AF.Exp, scale=-1.0), r=[("fl", 0, p)], w=[("fl", 1, p)])
                    op("act", lambda a: a.activation(out=fl[:, 2, :], in_=fl[:, 1, :], func=AF.Ln, bias=cst[:, 3:4]), r=[("fl", 1, p)], w=[("fl", 2, p)])
                    op("pe", lambda t: t.matmul(psc[:, 0:16], lhsT=mUi[:], rhs=fl[:, 2, :], start=True, stop=True), r=[("fl", 2, p)], w=[("psc", p % 2)])
                    op("pe", lambda t: t.matmul(psc[:, 16:32], lhsT=onesf[:], rhs=fl[:, 2, :], start=True, stop=True), r=[("fl", 2, p)], w=[("psc", p % 2)])
                    op("dve", lambda v: v.tensor_tensor(out=ctile[:], in0=carry[:], in1=psc[:, 0:16], op=ALU.subtract), r=["carry", ("psc", p % 2)], w=[("ctile", p)])
                    op("dve", lambda v: v.tensor_tensor(out=carry[:], in0=carry[:], in1=psc[:, 16:32], op=ALU.subtract), r=["carry", ("psc", p % 2)], w=["carry"])
                    op("dve", lambda v: v.tensor_scalar(out=nbt[:, :, ti], in0=ctile[:], scalar1=-1.0, scalar2=kpt[:, ti:ti + 1], op0=ALU.mult, op1=ALU.add), r=[("ctile", p), "kpt"], w=[("nbt", ti)])
                    if own:
                        oi = ti - NPT
                        op("pe", lambda t: t.transpose(out=psc[0:16, 128:256], in_=ctile[:], identity=identf[:]), r=[("ctile", p)], w=[("psc", p % 2)])
                        op("act", lambda a: a.copy(out=cT[:, oi * 128:(oi + 1) * 128], in_=psc[0:16, 128:256]), r=[("psc", p % 2)], w=[("cT", oi)])
                    yield
                    op("pool", lambda g: g.tensor_copy(out=vbs[p][:], in_=zfb[:, 2048:3072]), r=[zt], w=[("vb", p)])
                    dma("sp", VFX[ti * 128:(ti + 1) * 128, :], vbs[p][:], r=[("vb", p)], w=[("VFX", ti)])
                    yield
                    yield from rmsn(zfb[:, 1024:2048], kgt, knbs[p], zt, ("knb", p), sqs[p], sts[p], nrms[p], ("k", p))
                    for hh in range(2):
                        for i in range(8):
                            h = hh * 8 + i
                            op("pe", lambda t: t.transpose(out=psT[0][:, h, :], in_=knbs[p][:, h * 64:(h + 1) * 64], identity=ident[:]), r=[("knb", p), "ident"], w=["psT0"])
                    op("act", lambda a: a.copy(out=ktiles[p][:], in_=psT[0][:]), r=["psT0"], w=[("ktile", p)])
                    dma("sp", KTA[:, 0:64, ti * 128:(ti + 1) * 128].rearrange("h r t -> r h t"), ktiles[p][:], r=[("ktile", p)], w=[("KTA", ti)])
                    yield
                    if own:
                        oi = ti - NPT
                        yield from rmsn(zfb[:, 0:1024], qgt, qnbs[p], zt, ("qnb", p), sq2s[p], st2s[p], nrm2s[p], ("q", p))
                        for h in range(16):
                            op("pe", lambda t: t.transpose(out=psT[1][:, h, :], in_=qnbs[p][:, h * 64:(h + 1) * 64], identity=ident[:]), r=[("qnb", p), "ident"], w=["psT1"])
                        op("act", lambda a: a.copy(out=qtiles[p][:], in_=psT[1][:]), r=["psT1"], w=[("qtile", p)])
                        dma("sp", QTA[:, 0:64, oi * 128:(oi + 1) * 128].rearrange("h r t -> r h t"), qtiles[p][:], r=[("qtile", p)], w=[("QTA", oi)])
                        yield

                live = []
                nxt_t = 0
                while live or nxt_t < NT:
                    while len(live) < NP_ and nxt_t < NT:
                        live.append(cprep(nxt_t, nxt_t % NP_))
                        nxt_t += 1
                    for g_ in list(live):
                        try:
                            next(g_)
                        except StopIteration:
                            live.remove(g_)
                ctk = [("cT", i) for i in range(NOT_)]
                op("dve", lambda v: v.tensor_copy(out=chi[:, 0, :], in_=cT[:]), r=ctk, w=[("chi", 0)])
                op("dve", lambda v: v.tensor_tensor(out=r1[:], in0=cT[:], in1=chi[:, 0, :], op=ALU.subtract), r=ctk + [("chi", 0)], w=["r1"])
                op("dve", lambda v: v.tensor_copy(out=chi[:, 1, :], in_=r1[:]), r=["r1"], w=[("chi", 1)])
                op("dve", lambda v: v.tensor_tensor(out=r1[:], in0=r1[:], in1=chi[:, 1, :], op=ALU.subtract), r=["r1", ("chi", 1)], w=["r1"])
                op("dve", lambda v: v.tensor_copy(out=chi[:, 2, :], in_=r1[:]), r=["r1"], w=[("chi", 2)])
                dma("sp", QTA[:, 64:67, :], chi[:], r=[("chi", 0), ("chi", 1), ("chi", 2)], w=["QTA1"])
                kb.barrier()

            with ExitStack() as sc:
                T = lambda n, s, d=F32: sc.enter_context(SB(n, list(s), d))
                PS = lambda n, s, d=F32: sc.enter_context(PSM(n, list(s), d))
                QT = [T(f"QT{i}", [67, TO], BF16) for i in range(2)]
                KT = [T(f"KT{i}", [67, TA], BF16) for i in range(2)]
                Vh = [T(f"Vh{i}", [128, NT, 65], BF16) for i in range(2)]
                NSB = 5
                pt = [T(f"pt{i}", [128, 512], BF16) for i in range(NSB)]
                recr = T("recr", [128, 512]); recb = T("recb", [64, 512])
                ybo = [T(f"ybo{i}", [64, 512], BF16) for i in range(2)]
                pss = [PS(f"pss{i}", [128, 512]) for i in range(NSB)]
                pso = [PS(f"pso{i}", [128, 512]) for i in range(2)]
                psr = PS("psr", [128, 512])
                for i in range(2):
                    op("pool", lambda g: g.memset(Vh[i][:, :, 64:65], 1.0), w=[("Vh1", i)])
                cnt = {"s": 0, "o": 0}

                def load_head(h):
                    b = h % 2
                    dma("sp", QT[b][:], QTA[h, :, :], w=[("QT", b)])
                    dma("sp", KT[b][:], KTA[h, :, :], w=[("KT", b)])
                    for q4 in range(3):
                        t0, t1_ = q4 * 11, (q4 + 1) * 11
                        dma("sp", Vh[b][:, t0:t1_, 0:64], VFX[t0 * 128:t1_ * 128, h * 64:(h + 1) * 64].rearrange("(t p) v -> p t v", p=128), w=[("Vh", b, q4)])

                steps = []
                for h in range(16):
                    for qsb in range(4):
                        d0 = NPT + 4 * qsb
                        for kbi in range(d0 + 4):
                            steps.append((h, qsb, kbi, d0))

                def emit_S(idx):
                    h, qsb, kbi, d0 = steps[idx]
                    b = h % 2
                    j0 = max(0, kbi - d0)
                    ncol = 512 - 128 * j0
                    si = idx % NSB
                    op("pe", lambda t: t.matmul(pss[si][:, 0:ncol], lhsT=KT[b][:, kbi * 128:(kbi + 1) * 128], rhs=QT[b][:, qsb * 512 + j0 * 128:(qsb + 1) * 512], start=True, stop=True), r=[("QT", b), ("KT", b)], w=[("pss", si)])

                def emit_rest(idx):
                    h, qsb, kbi, d0 = steps[idx]
                    b = h % 2
                    if qsb == 0 and kbi == 0 and h + 1 < 16:
                        load_head(h + 1)
                    vk = [("Vh", b, q4) for q4 in range(3)] + [("Vh1", b)]
                    j0 = max(0, kbi - d0)
                    ncol = 512 - 128 * j0
                    si = idx % NSB
                    oi = (h * 4 + qsb) % 2
                    op("act", lambda a: a.activation(out=pt[si][:, 0:ncol], in_=pss[si][:, 0:ncol], func=AF.Exp, bias=nbt[:, h, kbi:kbi + 1]), r=[("pss", si)], w=[("pt", si)])
                    if kbi >= d0:
                        op("dve", lambda g: g.tensor_tensor(out=pt[si][:, 0:128], in0=pt[si][:, 0:128], in1=mUi[:], op=ALU.mult), r=[("pt", si)], w=[("pt", si)])
                    op("pe", lambda t: t.matmul(pso[oi][0:65, j0 * 128:512], lhsT=Vh[b][:, kbi, :], rhs=pt[si][:, 0:ncol], start=(kbi == 0), stop=(kbi == d0 + 3)), r=[("pt", si)] + vk, w=[("pso", oi)])
                    if kbi == d0 + 3:
                        op("dve", lambda v: v.reciprocal(out=recr[64:65, :], in_=pso[oi][64:65, :]), r=[("pso", oi)], w=["recr"])
                        op("pe", lambda t: t.matmul(psr[0:64, :], lhsT=onesf[64:65, 0:64], rhs=recr[64:65, :], start=True, stop=True), r=["recr"], w=["psr"])
                        op("act", lambda a: a.copy(out=recb[:], in_=psr[0:64, :]), r=["psr"], w=["recb"])
                        op("dve", lambda v: v.tensor_tensor(out=ybo[oi][:], in0=pso[oi][0:64, :], in1=recb[:], op=ALU.mult), r=[("pso", oi), "recb"], w=[("ybo", oi)])
                        dma("sp", YBT[h * 64:(h + 1) * 64, qsb * 512:(qsb + 1) * 512], ybo[oi][:], r=[("ybo", oi)], w=[("YBT", h, qsb)])

                load_head(0)
                LOOK = 4
                for idx in range(min(LOOK, len(steps))):
                    emit_S(idx)
                for idx in range(len(steps)):
                    emit_rest(idx)
                    if idx + LOOK < len(steps):
                        emit_S(idx + LOOK)
                kb.barrier()

        MT = scr("MT", [D, TO], BF16)
        if "D" in STAGES:
            with ExitStack() as sd:
                T = lambda n, s, d=F32: sd.enter_context(SB(n, list(s), d))
                PS = lambda n, s, d=F32: sd.enter_context(PSM(n, list(s), d))
                wab = T("wab", [128, 8, D], BF16); wbb = T("wbb", [128, 8, D], BF16)
                wst = [T(f"wst{i}", [128, 2, D]) for i in range(2)]
                yaTs = T("yaTs", [128, 8, TO], BF16); ybTs = T("ybTs", [128, 8, TO], BF16)
                ybl = [T(f"ybl{i}", [128, RW], BF16) for i in range(2)]
                gat = [T(f"gat{i}", [128, 2, 512], BF16) for i in range(4)]
                m1 = [T(f"m1{i}", [128, 512]) for i in range(2)]; m2 = [T(f"m2{i}", [128, 512]) for i in range(2)]
                mo = [T(f"mo{i}", [128, 512], BF16) for i in range(2)]
                psa = [PS(f"psa{i}", [128, 512]) for i in range(2)]; psb_ = [PS(f"psb{i}", [128, 512]) for i in range(2)]
                pstr = [PS(f"pstr{i}", [128, 8, 128], BF16) for i in range(2)]
                n = 0
                for (wsrc, wdst, nm) in ((w_a, wab, "wab"), (w_b, wbb, "wbb")):
                    for i in range(4):
                        bb_ = n % 2; n += 1
                        dma("sp", wst[bb_][:], wsrc[i * 256:(i + 1) * 256, :].rearrange("(c p) n -> p c n", p=128), w=[("wst", bb_)])
                        if n % 2:
                            op("dve", lambda g: g.tensor_copy(out=wdst[:, 2 * i:2 * i + 2, :], in_=wst[bb_][:]), r=[("wst", bb_)], w=[(nm, i)])
                        else:
                            op("act", lambda g: g.copy(out=wdst[:, 2 * i:2 * i + 2, :], in_=wst[bb_][:]), r=[("wst", bb_)], w=[(nm, i)])
                dma("sp", yaTs[:], YAT.rearrange("(c p) t -> p c t", p=128), w=["yaTs"])
                dma("sp", ybTs[:], YBT.rearrange("(c p) t -> p c t", p=128), w=["ybTs"])
                wak = [("wab", i) for i in range(4)]; wbk = [("wbb", i) for i in range(4)]
                it = 0
                for dblk in range(16):
                    for tsb in range(4):
                        b = it % 2; gb = it % 4; it += 1
                        dma("sp", gat[gb][:, 0, :], GT[dblk * 128:(dblk + 1) * 128, tsb * 512:(tsb + 1) * 512], w=[("gat", gb, 0)])
                        dma("sp", gat[gb][:, 1, :], GT[2048 + dblk * 128:2048 + (dblk + 1) * 128, tsb * 512:(tsb + 1) * 512], w=[("gat", gb, 1)])
                        for kc in range(8):
                            op("pe", lambda t: t.matmul(psa[b][:, :], lhsT=wab[:, kc, dblk * 128:(dblk + 1) * 128], rhs=yaTs[:, kc, tsb * 512:(tsb + 1) * 512], start=(kc == 0), stop=(kc == 7)), r=wak + ["yaTs"], w=[("psa", b)])
                        for kc in range(8):
                            op("pe", lambda t: t.matmul(psb_[b][:, :], lhsT=wbb[:, kc, dblk * 128:(dblk + 1) * 128], rhs=ybTs[:, kc, tsb * 512:(tsb + 1) * 512], start=(kc == 0), stop=(kc == 7)), r=wbk + ["ybTs"], w=[("psb", b)])
                        op("dve", lambda v: v.tensor_tensor(out=m1[b][:], in0=psa[b][:, :], in1=gat[gb][:, 0, :], op=ALU.mult), r=[("psa", b), ("gat", gb, 0)], w=[("m1", b)])
                        op("dve", lambda v: v.tensor_tensor(out=m2[b][:], in0=psb_[b][:, :], in1=gat[gb][:, 1, :], op=ALU.mult), r=[("psb", b), ("gat", gb, 1)], w=[("m2", b)])
                        op("pool", lambda g: g.tensor_tensor(out=mo[b][:], in0=m1[b][:], in1=m2[b][:], op=ALU.add), r=[("m1", b), ("m2", b)], w=[("mo", b)])
                        dma("sp", MT[dblk * 128:(dblk + 1) * 128, tsb * 512:(tsb + 1) * 512], mo[b][:], r=[("mo", b)], w=[("MT", dblk, tsb)])
                kb.barrier()

            with ExitStack() as sd:
                T = lambda n, s, d=F32: sd.enter_context(SB(n, list(s), d))
                PS = lambda n, s, d=F32: sd.enter_context(PSM(n, list(s), d))
                wob = T("wob", [128, 16, D], BF16)
                wst = [T(f"wst{i}", [128, 2, D]) for i in range(2)]
                mTt = [T(f"mTt{i}", [128, 16, 128], BF16) for i in range(2)]
                xt = [T(f"xt{i}", [128, D]) for i in range(2)]
                h2t = [T(f"h2t{i}", [128, D]) for i in range(2)]
                g2t = T("g2t", [128, D]); xnb = T("xnb", [128, D], BF16); junk = T("junk", [128, D], BF16)
                st1 = T("st1", [128, 4]); xn2t = T("xn2t", [128, 16, 128], BF16)
                pso_ = [PS(f"pso{i}", [128, 512]) for i in range(4)]
                pstr = [PS(f"pstr{i}", [128, 8, 128], BF16) for i in range(2)]
                for i in range(8):
                    bb_ = i % 2
                    dma("sp", wst[bb_][:], w_o[i * 256:(i + 1) * 256, :].rearrange("(c p) n -> p c n", p=128), w=[("wst", bb_)])
                    if i % 2:
                        op("dve", lambda g: g.tensor_copy(out=wob[:, 2 * i:2 * i + 2, :], in_=wst[bb_][:]), r=[("wst", bb_)], w=[("wob", i)])
                    else:
                        op("act", lambda g: g.copy(out=wob[:, 2 * i:2 * i + 2, :], in_=wst[bb_][:]), r=[("wst", bb_)], w=[("wob", i)])
                wok = [("wob", i) for i in range(8)]
                dma("sp", g2t[:], g2bc[:, :], w=["g2t"])
                MT3 = MT.rearrange("(c p) t -> p c t", p=128)
                for ti in range(NOT_):
                    b = ti % 2
                    dma("sp", mTt[b][:], MT3[:, :, ti * 128:(ti + 1) * 128], w=[("mTt", b)])
                    dma("sp", xt[b][:], xall[TP + ti * 128:TP + (ti + 1) * 128, :], w=[("xt", b)])
                    for cb in range(4):
                        for kc in range(16):
                            op("pe", lambda t: t.matmul(pso_[cb][:, :], lhsT=mTt[b][:, kc, :], rhs=wob[:, kc, cb * 512:(cb + 1) * 512], start=(kc == 0), stop=(kc == 15)), r=[("mTt", b)] + wok, w=[("pso", cb)])
                        op("dve", lambda v: v.tensor_tensor(out=h2t[b][:, cb * 512:(cb + 1) * 512], in0=pso_[cb][:, :], in1=xt[b][:, cb * 512:(cb + 1) * 512], op=ALU.add), r=[("pso", cb), ("xt", b)], w=[("h2t", b, cb)])
                    hk = [("h2t", b, cb) for cb in range(4)]
                    dma("sp", H2[ti * 128:(ti + 1) * 128, :], h2t[b][:], r=hk, w=[("H2", ti)])
                    op("dve", lambda v: v.memset(st1[:, 0:1], 0.0), w=["ss"])
                    op("act", lambda a: a.activation(out=junk[:], in_=h2t[b][:], func=AF.Square, accum_out=st1[:, 0:1]), r=hk + ["ss"], w=["junk", "ss"])
                    op("act", lambda a: a.activation(out=st1[:, 1:2], in_=st1[:, 0:1], func=AF.Ln, bias=cst[:, 0:1], scale=1.0 / D), r=["ss"], w=["lnv"])
                    op("act", lambda a: a.activation(out=st1[:, 2:3], in_=st1[:, 1:2], func=AF.Exp, scale=-0.5), r=["lnv"], w=["rstd"])
                    op("dve", lambda v: v.scalar_tensor_tensor(out=xnb[:], in0=h2t[b][:], scalar=st1[:, 2:3], in1=g2t[:], op0=ALU.mult, op1=ALU.mult), r=hk + ["rstd", "g2t"], w=["xnb"])
                    for hh in range(2):
                        for c in range(8):
                            kc = hh * 8 + c
                            op("pe", lambda t: t.transpose(out=pstr[hh][:, c, :], in_=xnb[:, kc * 128:(kc + 1) * 128], identity=ident[:]), r=["xnb"], w=[("pstr", hh)])
                        op("act", lambda a: a.copy(out=xn2t[:, hh * 8:(hh + 1) * 8, :], in_=pstr[hh][:]), r=[("pstr", hh)], w=[("xn2t", hh)])
                    dma("sp", XN2T.rearrange("(c p) t -> p c t", p=128)[:, :, ti * 128:(ti + 1) * 128], xn2t[:], r=[("xn2t", 0), ("xn2t", 1)], w=[("XN2T", ti)])
                kb.barrier()

        if "E" in STAGES:
            with ExitStack() as se:
                T = lambda n, s, d=F32: se.enter_context(SB(n, list(s), d))
                PS = lambda n, s, d=F32: se.enter_context(PSM(n, list(s), d))
                xn2 = T("xn2", [128, 16, TO], BF16)
                wf = [T(f"wf{i}", [128, 16, 256]) for i in range(2)]
                wb = [T(f"wb{i}", [128, 16, 256], BF16) for i in range(2)]
                sg = [T(f"sg{i}", [128, 512]) for i in range(2)]
                ao = [T(f"ao{i}", [128, 512], BF16) for i in range(2)]
                psg = [PS(f"psg{i}", [128, 512]) for i in range(2)]; psu = [PS(f"psu{i}", [128, 512]) for i in range(2)]
                wst = [T(f"wst{i}", [128, 2, D]) for i in range(2)]
                wcv = [T(f"wcv{i}", [128, 2, D], BF16) for i in range(2)]

                def conv_wdn(i):
                    b = i % 2
                    dma("sp", wst[b][:], w_dn[i * 256:(i + 1) * 256, :].rearrange("(c p) n -> p c n", p=128), w=[("wst", b)])
                    op("dve", lambda v: v.tensor_copy(out=wcv[b][:], in_=wst[b][:]), r=[("wst", b)], w=[("wcv", b)])
                    dma("sp", WDNB[i * 256:(i + 1) * 256, :].rearrange("(c p) n -> p c n", p=128), wcv[b][:], r=[("wcv", b)], w=[("WDNB", i)])
                XN3 = XN2T.rearrange("(c p) t -> p c t", p=128)
                for q4 in range(4):
                    dma("sp", xn2[:, q4 * 4:(q4 + 1) * 4, :], XN3[:, q4 * 4:(q4 + 1) * 4, :], w=[("xn2", q4)])
                xk = [("xn2", q4) for q4 in range(4)]

                def load_gu(fb):
                    b = fb % 2
                    for q4 in range(4):
                        dma("sp", wf[b][:, q4 * 4:(q4 + 1) * 4, 0:128], w_gu[q4 * 512:(q4 + 1) * 512, fb * 128:(fb + 1) * 128].rearrange("(kc p) n -> p kc n", p=128), w=[("wf", b, q4, 0)])
                        dma("sp", wf[b][:, q4 * 4:(q4 + 1) * 4, 128:256], w_gu[q4 * 512:(q4 + 1) * 512, DFF + fb * 128:DFF + (fb + 1) * 128].rearrange("(kc p) n -> p kc n", p=128), w=[("wf", b, q4, 1)])
                    for q4 in range(4):
                        op("pool", lambda g: g.tensor_copy(out=wb[b][:, q4 * 4:(q4 + 1) * 4, :], in_=wf[b][:, q4 * 4:(q4 + 1) * 4, :]), r=[("wf", b, q4, 0), ("wf", b, q4, 1)], w=[("wb", b, q4)])

                load_gu(0)
                it = 0
                for fb in range(DFF // 128):
                    b = fb % 2
                    if fb + 1 < DFF // 128:
                        load_gu(fb + 1)
                    if fb % 2 == 0:
                        conv_wdn(fb // 2)
                    for tsb in range(4):
                        pi = it % 2; it += 1
                        for kc in range(16):
                            op("pe", lambda t: t.matmul(psg[pi][:, :], lhsT=wb[b][:, kc, 0:128], rhs=xn2[:, kc, tsb * 512:(tsb + 1) * 512], start=(kc == 0), stop=(kc == 15)), r=[("wb", b, kc // 4)] + xk, w=[("psg", pi)])
                        for kc in range(16):
                            op("pe", lambda t: t.matmul(psu[pi][:, :], lhsT=wb[b][:, kc, 128:256], rhs=xn2[:, kc, tsb * 512:(tsb + 1) * 512], start=(kc == 0), stop=(kc == 15)), r=[("wb", b, kc // 4)] + xk, w=[("psu", pi)])
                        op("act", lambda a: a.activation(out=sg[pi][:], in_=psg[pi][:, :], func=AF.Silu), r=[("psg", pi)], w=[("sg", pi)])
                        op("dve", lambda v: v.tensor_tensor(out=ao[pi][:], in0=sg[pi][:], in1=psu[pi][:, :], op=ALU.mult), r=[("sg", pi), ("psu", pi)], w=[("ao", pi)])
                        dma("sp", ACTT[fb * 128:(fb + 1) * 128, tsb * 512:(tsb + 1) * 512], ao[pi][:], r=[("ao", pi)], w=[("ACTT", fb, tsb)])
                kb.barrier()

        if "F" in STAGES:
            with ExitStack() as sf:
                T = lambda n, s, d=F32: sf.enter_context(SB(n, list(s), d))
                PS = lambda n, s, d=F32: sf.enter_context(PSM(n, list(s), d))
                NC_ = DFF // 128
                acts = T("acts", [128, NC_, 1024], BF16)
                wdb = [T(f"wdb{i}", [128, NC_, 512], BF16) for i in range(2)]
                h2s = [T(f"h2s{i}", [128, 512]) for i in range(4)]
                ot = [T(f"ot{i}", [128, 512]) for i in range(4)]
                psd = [PS(f"psd{i}", [128, 512]) for i in range(4)]
                AC3 = ACTT.rearrange("(c p) t -> p c t", p=128)
                WD3 = WDNB.rearrange("(c p) n -> p c n", p=128)
                it = 0
                def load_wd(cb, b):
                    for q4 in range(4):
                        dma("sp", wdb[b][:, q4 * 11:(q4 + 1) * 11, :], WD3[:, q4 * 11:(q4 + 1) * 11, cb * 512:(cb + 1) * 512], w=[("wdb", b, q4)])
                seq = [(th, cb) for th in range(2) for cb in range(4)]
                load_wd(seq[0][1], 0)
                for si, (th, cb) in enumerate(seq):
                    b = si % 2
                    if cb == 0:
                        for q4 in range(4):
                            dma("act", acts[:, q4 * 11:(q4 + 1) * 11, :], AC3[:, q4 * 11:(q4 + 1) * 11, th * 1024:(th + 1) * 1024], w=[("acts", q4)])
                    if si + 1 < len(seq):
                        load_wd(seq[si + 1][1], (si + 1) % 2)
                    for j in range(8):
                        pi = it % 4; it += 1
                        tok0 = th * 1024 + j * 128
                        dma("act", h2s[pi][:], H2[tok0:tok0 + 128, cb * 512:(cb + 1) * 512], w=[("h2s", pi)])
                        for c in range(NC_):
                            op("pe", lambda t: t.matmul(psd[pi][:, :], lhsT=acts[:, c, j * 128:(j + 1) * 128], rhs=wdb[b][:, c, :], start=(c == 0), stop=(c == NC_ - 1)), r=[("acts", c // 11), ("wdb", b, c // 11)], w=[("psd", pi)])
                        op("dve", lambda v: v.tensor_tensor(out=ot[pi][:], in0=psd[pi][:, :], in1=h2s[pi][:], op=ALU.add), r=[("psd", pi), ("h2s", pi)], w=[("ot", pi)])
                        dma("act", out[tok0:tok0 + 128, cb * 512:(cb + 1) * 512], ot[pi][:], r=[("ot", pi)], w=[("out", tok0, cb)])
                kb.barrier()

        kb.barrier()
    return nc


def prep_core_inputs(inp, core):
    b, half = core // 2, core % 2
    x = np.asarray(inp["x"], dtype=np.float32)
    meta = np.asarray(inp["meta_tokens"], dtype=np.float32)
    xall = np.zeros((TA, D), np.float32)
    keypen = np.zeros((128, NT), np.float32)
    if half == 0:
        xall[TP - 16:TP] = meta
        xall[TP:] = x[b, 0:2048]
        ndummy = TP - 16
    else:
        xall[112:128] = meta
        xall[128:TP] = x[b, 0:2048]
        xall[TP:] = x[b, 2048:4096]
        ndummy = 112
    kp = np.zeros(TA, np.float32)
    kp[:ndummy] = -30000.0
    keypen[:, :] = kp.reshape(NT, 128).T
    f = lambda k: np.asarray(inp[k], dtype=np.float32)[0]
    bc = lambda v, n=128: np.ascontiguousarray(np.broadcast_to(v[None, :], (n, v.shape[0])))
    mu = f("rwkv_mu")
    mu24 = np.ascontiguousarray(mu[0:3072].reshape(24, 128).T)
    mul4 = np.zeros((128, 4), np.float32)
    mul4[0:96, 0] = mu[3072:3168]
    mul4[0:96, 1] = mu[3168:3264]
    mul4[:, 2] = mu[3264:3392]
    mul4[:, 3] = mu[3392:3520]
    pp = lambda v: np.ascontiguousarray(v.reshape(8, 128).T)
    vecp = np.stack([pp(f("rwkv_a0")), pp(f("rwkv_k_k")), pp(f("rwkv_k_a")), pp(f("rwkv_r_k").reshape(-1))], axis=1)
    d = {
        "xall": xall, "keypen": keypen,
        "w_in": f("w_in"), "w_a": f("w_branch_a"), "w_b": f("w_branch_b"), "w_o": f("w_o"),
        "w_gu": f("w_gate_up"), "w_dn": f("w_down"),
        "g1bc": bc(f("norm1_g")), "g2bc": bc(f("norm2_g")),
        "mu24": mu24, "mul4": mul4, "w0bc": bc(f("rwkv_w0")),
        "w2": f("rwkv_w2"), "a2": f("rwkv_a2"), "g2": f("rwkv_g2"),
        "vecp": np.ascontiguousarray(vecp),
        "gnwbc": bc(f("rwkv_gn_w")), "gnbbc": bc(f("rwkv_gn_b")),
        "qgbc": bc(np.tile(f("fox_q_norm_g"), 16)), "kgbc": bc(np.tile(f("fox_k_norm_g"), 16)),
        "fbbc": bc(f("fox_f_bias")),
    }
    return d


_NC_CACHE = {}


def kernel(**inputs):
    if "nc" not in _NC_CACHE:
        _NC_CACHE["nc"] = build_program()
    nc = _NC_CACHE["nc"]
    shared = None
    in_maps = []
    ncores = int(os.environ.get("KCORES", "8"))
    for c in range(ncores):
        in_maps.append(prep_core_inputs(inputs, c))
    res = run_bass_kernel_spmd(nc, in_maps, core_ids=list(range(ncores)))
    B = inputs["x"].shape[0]
    outp = np.zeros((B, 4096, D), np.float32)
    for c in range(ncores):
        b, half = c // 2, c % 2
        outp[b, half * 2048:(half + 1) * 2048] = res.results[c]["out"]
    if DEBUG:
        kernel.last = res
    return outp
```

```python
import os
import numpy as np
from contextlib import ExitStack
import concourse.bass as bass
import concourse.mybir as mybir
from concourse.bass_utils import run_bass_kernel_spmd

F32 = mybir.dt.float32
BF16 = mybir.dt.bfloat16
AF = mybir.ActivationFunctionType
ALU = mybir.AluOpType
AX = mybir.AxisListType

D = 2048
NPT, NOT_, NT = 17, 16, 33
TP, TO, TA = NPT * 128, NOT_ * 128, NT * 128
RW = 1024
RC = 3520
FC = 3088
NIN = 10704
DFF = 5632
DECAY_C = 0.6065306597126334
DEBUG = os.environ.get("KDEBUG", "") != ""
STAGES = os.environ.get("KSTAGES", "ABCDEF")


class KB:
    def __init__(self, nc, es):
        self.nc = nc
        self.eng = {"pe": nc.tensor, "act": nc.scalar, "dve": nc.vector, "pool": nc.gpsimd, "sp": nc.sync}
        self.sem = {e: es.enter_context(nc.semaphore("s_" + e)) for e in ("pe", "act", "dve", "pool")}
        self.cnt = {e: 0 for e in self.sem}
        self.seen = {}
        self.nd = 12
        self.dsem = {q: [es.enter_context(nc.semaphore(f"d_{q}{i}")) for i in range(self.nd)] for q in ("sp", "pool", "act")}
        self.dcnt = {q: 0 for q in self.dsem}
        self.dlast = {q: [None] * self.nd for q in self.dsem}
        self.res = {}

    def ensure(self, e, tok):
        if tok is None:
            return
        if tok[0] == "c":
            _, pe_, idx = tok
            if pe_ == e and e == "pe":
                return
            k = (e, pe_)
            if self.seen.get(k, 0) >= idx:
                return
            self.eng[e].wait_ge(self.sem[pe_], idx)
            self.seen[k] = idx
        else:
            _, q, slot, val = tok
            k = (e, q, slot)
            if self.seen.get(k, 0) >= val:
                return
            self.eng[e].wait_ge(self.dsem[q][slot], val)
            self.seen[k] = val

    def _deps(self, e, r, w):
        for k in r:
            st = self.res.get(k)
            if st:
                self.ensure(e, st["w"])
        for k in w:
            st = self.res.get(k)
            if st:
                self.ensure(e, st["w"])
                for t in st["r"].values():
                    self.ensure(e, t)

    def _upd(self, tok, rk, r, w):
        for k in w:
            self.res[k] = {"w": tok, "r": {}}
        for k in r:
            st = self.res.setdefault(k, {"w": None, "r": {}})
            st["r"][rk] = tok

    def op(self, e, fn, r=(), w=()):
        self._deps(e, r, w)
        inst = fn(self.eng[e])
        self.cnt[e] += 1
        inst.then_inc(self.sem[e], 1)
        tok = ("c", e, self.cnt[e])
        self._upd(tok, e, r, w)
        return tok

    def dma(self, q, out, in_, r=(), w=(), slow=False):
        self._deps(q, r, w)
        n = self.dcnt[q]
        slot = n % self.nd
        self.ensure(q, self.dlast[q][slot])
        val = 16 * (n // self.nd + 1)
        if slow:
            self.eng[q].dma_start(out=out, in_=in_, allow_slow_non_contiguous=True).then_inc(self.dsem[q][slot], 16)
        else:
            self.eng[q].dma_start(out=out, in_=in_).then_inc(self.dsem[q][slot], 16)
        tok = ("d", q, slot, val)
        self.dlast[q][slot] = tok
        self.dcnt[q] = n + 1
        self._upd(tok, ("d", q, slot), r, w)
        return tok

    def barrier(self):
        for e in ("pe", "act", "dve", "pool", "sp"):
            for c in self.cnt:
                if c != e and self.cnt[c] > 0:
                    self.ensure(e, ("c", c, self.cnt[c]))
            if e in self.cnt and e != "pe" and self.cnt[e] > 0:
                self.ensure(e, ("c", e, self.cnt[e]))
            for q in self.dsem:
                for t in self.dlast[q]:
                    self.ensure(e, t)
        self.res = {}


def bcast_last(ap, n):
    dims = [list(d) for d in ap.ap]
    if dims[-1][1] == 1 and len(dims) == 3:
        dims = dims[:2]
    return bass.AP(ap.tensor, ap.offset, dims + [[0, n]])


def build_program():
    nc = bass.Bass("TRN2", target_bir_lowering=False)
    _u = {"n": 0}
    def _un(n):
        _u["n"] += 1
        return "%s_%d" % (n, _u["n"])
    SB = lambda n, s, d: nc.sbuf_tensor(_un(n), s, d)
    PSM = lambda n, s, d: nc.psum_tensor(_un(n), s, d)
    dt_in = lambda n, s, d=F32: nc.dram_tensor(n, list(s), d, kind="ExternalInput").ap()
    okind = "ExternalOutput" if DEBUG else None
    DBGOUT = os.environ.get("KDBGOUT", "").split(",")
    def scr(n, s, d):
        if DEBUG and n in DBGOUT:
            return nc.dram_tensor(n, list(s), d, kind="ExternalOutput").ap()
        return nc.dram_tensor(n, list(s), d).ap()

    xall = dt_in("xall", [TA, D])
    keypen = dt_in("keypen", [128, NT])
    w_in = dt_in("w_in", [D, NIN])
    w_a = dt_in("w_a", [RW, D])
    w_b = dt_in("w_b", [RW, D])
    w_o = dt_in("w_o", [D, D])
    w_gu = dt_in("w_gu", [D, 2 * DFF])
    w_dn = dt_in("w_dn", [DFF, D])
    g1bc = dt_in("g1bc", [128, D])
    g2bc = dt_in("g2bc", [128, D])
    mu24 = dt_in("mu24", [128, 24])
    mul4 = dt_in("mul4", [128, 4])
    w0bc = dt_in("w0bc", [128, RW])
    w2 = dt_in("w2", [96, RW])
    a2 = dt_in("a2", [96, RW])
    g2 = dt_in("g2", [256, RW])
    vecp = dt_in("vecp", [128, 4, 8])
    gnwbc = dt_in("gnwbc", [128, RW])
    gnbbc = dt_in("gnbbc", [128, RW])
    qgbc = dt_in("qgbc", [128, RW])
    kgbc = dt_in("kgbc", [128, RW])
    fbbc = dt_in("fbbc", [128, 16])
    out = nc.dram_tensor("out", [TO, D], F32, kind="ExternalOutput").ap()

    ZR = scr("ZR", [3584, TA + 1], F32)
    ZFX = scr("ZFX", [TA, FC], F32)
    GT = scr("GT", [4096, TO], BF16)
    YAT = scr("YAT", [RW, TO], BF16)
    QTA = scr("QTA", [16, 67, TO], BF16)
    KTA = scr("KTA", [16, 67, TA], BF16)
    VFX = scr("VFX", [TA, RW], BF16)
    NB = scr("NB", [16, 128, NT], F32)
    YBT = scr("YBT", [RW, TO], BF16)
    H2 = scr("H2", [TO, D], F32)
    XN2T = scr("XN2T", [D, TO], BF16)
    ACTT = scr("ACTT", [DFF, TO], BF16)
    WDNB = scr("WDNB", [DFF, D], BF16)

    with ExitStack() as es:
        kb = KB(nc, es)
        es.enter_context(nc.Block())
        op, dma = kb.op, kb.dma

        ident = es.enter_context(SB("ident", [128, 128], BF16))
        identf = es.enter_context(SB("identf", [128, 128], F32))
        onesf = es.enter_context(SB("onesf", [128, 128], F32))
        zerof = es.enter_context(SB("zerof", [128, 32], F32))
        cst = es.enter_context(SB("cst", [128, 8], F32))
        mUs = es.enter_context(SB("mUs", [128, 128], F32))
        mUi = es.enter_context(SB("mUi", [128, 128], F32))
        mLs = es.enter_context(SB("mLs", [128, 128], F32))
        op("pool", lambda g: g.memset(identf[:], 1.0), w=["identf"])
        op("pool", lambda g: g.affine_select(out=identf[:], in_=identf[:], pattern=[[-1, 128]], compare_op=ALU.is_equal, fill=0.0, base=0, channel_multiplier=1), r=["identf"], w=["identf"])
        op("pool", lambda g: g.memset(onesf[:], 1.0), w=["onesf"])
        op("pool", lambda g: g.memset(zerof[:], 0.0), w=["zerof"])
        op("pool", lambda g: g.affine_select(out=mUs[:], in_=onesf[:], pattern=[[1, 128]], compare_op=ALU.is_gt, fill=0.0, base=0, channel_multiplier=-1), r=["onesf"], w=["mUs"])
        op("pool", lambda g: g.affine_select(out=mUi[:], in_=onesf[:], pattern=[[1, 128]], compare_op=ALU.is_ge, fill=0.0, base=0, channel_multiplier=-1), r=["onesf"], w=["mUi"])
        op("pool", lambda g: g.affine_select(out=mLs[:], in_=onesf[:], pattern=[[-1, 128]], compare_op=ALU.is_gt, fill=0.0, base=0, channel_multiplier=1), r=["onesf"], w=["mLs"])
        op("dve", lambda v: v.tensor_copy(out=ident[:], in_=identf[:]), r=["identf"], w=["ident"])
        for j, val in enumerate((1e-6, 64e-5, 1e-24, 1.0)):
            op("pool", lambda g, j=j, val=val: g.memset(cst[:, j:j + 1], val), w=[("cst", j)])
        kb.barrier()

        if "A" in STAGES:
            with ExitStack() as sa:
                xnT = sa.enter_context(SB("xnT", [128, 16, TP], BF16))
                g1t = sa.enter_context(SB("g1t", [128, D], F32))
                xt = [sa.enter_context(SB(f"xt{i}", [128, D], F32)) for i in range(2)]
                xnbs = [sa.enter_context(SB(f"xnb{i}", [128, D], BF16)) for i in range(2)]
                junk = sa.enter_context(SB("junk", [128, D], BF16))
                st1s = [sa.enter_context(SB(f"st1{i}", [128, 4], F32)) for i in range(2)]
                wf = [sa.enter_context(SB(f"wf{i}", [128, 16, 256], F32)) for i in range(2)]
                wb = [sa.enter_context(SB(f"wb{i}", [128, 16, 256], BF16)) for i in range(2)]
                og = [sa.enter_context(SB(f"og{i}", [128, 512], F32)) for i in range(4)]
                ogb = [sa.enter_context(SB(f"ogb{i}", [128, 512], BF16)) for i in range(2)]
                pst = [sa.enter_context(PSM(f"pst{i}", [128, 8, 128], BF16)) for i in range(2)]
                psg = [sa.enter_context(PSM(f"psg{i}", [128, 512], F32)) for i in range(4)]

                dma("sp", g1t[:], g1bc[:, :], w=["g1t"])
                dma("sp", ZR[:, 0:1].rearrange("(b p) o -> p b o", p=128), zerof[:, 0:28].rearrange("p (b o) -> p b o", o=1), r=["zerof"], w=[("ZRc0",)], slow=True)

                def norm_pass(tok0, ntiles):
                    for i in range(ntiles):
                        b = i % 2
                        xnb = xnbs[b]; st1 = st1s[b]
                        dma("sp", xt[b][:], xall[tok0 + i * 128: tok0 + (i + 1) * 128, :], w=[("xt", b)])
                        op("dve", lambda v: v.memset(st1[:, 0:1], 0.0), w=[("ss", b)])
                        op("act", lambda a: a.activation(out=junk[:], in_=xt[b][:], func=AF.Square, accum_out=st1[:, 0:1]), r=[("xt", b), ("ss", b)], w=[("ss", b)])
                        op("act", lambda a: a.activation(out=st1[:, 1:2], in_=st1[:, 0:1], func=AF.Ln, bias=cst[:, 0:1], scale=1.0 / D), r=[("ss", b)], w=[("lnv", b)])
                        op("act", lambda a: a.activation(out=st1[:, 2:3], in_=st1[:, 1:2], func=AF.Exp, scale=-0.5), r=[("lnv", b)], w=[("rstd", b)])
                        op("dve", lambda v: v.scalar_tensor_tensor(out=xnb[:], in0=xt[b][:], scalar=st1[:, 2:3], in1=g1t[:], op0=ALU.mult, op1=ALU.mult), r=[("xt", b), ("rstd", b), "g1t"], w=[("xnb", b)])
                        for hh in range(2):
                            for c in range(8):
                                kc = hh * 8 + c
                                op("pe", lambda t: t.transpose(out=pst[hh][:, c, :], in_=xnb[:, kc * 128:(kc + 1) * 128], identity=ident[:]), r=[("xnb", b), "ident"], w=[("pst", hh)])
                            e = "act" if hh == 0 else "dve"
                            if e == "act":
                                op("act", lambda a: a.copy(out=xnT[:, hh * 8:(hh + 1) * 8, i * 128:(i + 1) * 128], in_=pst[hh][:]), r=[("pst", hh)], w=[("xnT", i)])
                            else:
                                op("dve", lambda v: v.tensor_copy(out=xnT[:, hh * 8:(hh + 1) * 8, i * 128:(i + 1) * 128], in_=pst[hh][:]), r=[("pst", hh)], w=[("xnT", i)])

                state = {"ld": 0, "pg": 0, "og": 0, "ogb": 0, "ev": 0}

                def load_w(c0, w):
                    b = state["ld"] % 2
                    state["ld"] += 1
                    for q4 in range(4):
                        dma("sp", wf[b][:, q4 * 4:(q4 + 1) * 4, 0:w], w_in[q4 * 512:(q4 + 1) * 512, c0:c0 + w].rearrange("(kc p) n -> p kc n", p=128), w=[("wf", b, q4)])
                    for q4 in range(4):
                        op("pool", lambda g: g.tensor_copy(out=wb[b][:, q4 * 4:(q4 + 1) * 4, 0:w], in_=wf[b][:, q4 * 4:(q4 + 1) * 4, 0:w]), r=[("wf", b, q4)], w=[("wb", b, q4)])
                    return b

                def evac_copy(dst, src, rk, wk):
                    state["ev"] += 1
                    if state["ev"] % 2:
                        op("act", lambda a: a.copy(out=dst, in_=src), r=rk, w=wk)
                    else:
                        op("dve", lambda v: v.tensor_copy(out=dst, in_=src), r=rk, w=wk)

                def fm_job(b, c0, w, t0, t1, ntile_used, kind):
                    for sub in range(0, w, 128):
                        m = min(128, w - sub)
                        for ts in range(t0, t1, 512):
                            n = min(512, t1 - ts)
                            pi = state["pg"] % 4
                            state["pg"] += 1
                            for kc in range(16):
                                op("pe", lambda t: t.matmul(psg[pi][0:m, 0:n], lhsT=wb[b][:, kc, sub:sub + m], rhs=xnT[:, kc, ts:ts + n], start=(kc == 0), stop=(kc == 15)),
                                   r=[("wb", b, kc // 4)] + [("xnT", ti) for ti in range(ts // 128, (ts + n + 127) // 128)], w=[("psg", pi)])
                            col = c0 + sub
                            if kind == "rw":
                                oi = state["og"] % 4
                                state["og"] += 1
                                evac_copy(og[oi][0:m, 0:n], psg[pi][0:m, 0:n], [("psg", pi)], [("og", oi)])
                                dma("sp", ZR[col:col + m, 1 + ntile_used + ts: 1 + ntile_used + ts + n], og[oi][0:m, 0:n], r=[("og", oi)], w=[("ZR", col, ntile_used + ts)])
                            else:
                                oi = state["ogb"] % 2
                                state["ogb"] += 1
                                op("act", lambda a: a.activation(out=ogb[oi][0:m, 0:n], in_=psg[pi][0:m, 0:n], func=AF.Sigmoid), r=[("psg", pi)], w=[("ogb", oi)])
                                dma("sp", GT[col - 6608:col - 6608 + m, ts:ts + n], ogb[oi][0:m, 0:n], r=[("ogb", oi)], w=[("GT", col, ts)])

                def tm_job(b, c0, w, ntiles, tokbase):
                    for i in range(ntiles):
                        pi = state["pg"] % 4
                        state["pg"] += 1
                        for kc in range(16):
                            op("pe", lambda t: t.matmul(psg[pi][:, 0:w], lhsT=xnT[:, kc, i * 128:(i + 1) * 128], rhs=wb[b][:, kc, 0:w], start=(kc == 0), stop=(kc == 15)),
                               r=[("wb", b, kc // 4), ("xnT", i)], w=[("psg", pi)])
                        oi = state["og"] % 4
                        state["og"] += 1
                        evac_copy(og[oi][:, 0:w], psg[pi][:, 0:w], [("psg", pi)], [("og", oi)])
                        dma("sp", ZFX[tokbase + i * 128: tokbase + (i + 1) * 128, c0 - RC:c0 - RC + w], og[oi][:, 0:w], r=[("og", oi)], w=[("ZFX", c0, tokbase + i)])

                def run_jobs(jobs):
                    nxt = load_w(jobs[0][0], jobs[0][1])
                    for ji, (c0, w, fn) in enumerate(jobs):
                        cur = nxt
                        if ji + 1 < len(jobs):
                            nxt = load_w(jobs[ji + 1][0], jobs[ji + 1][1])
                        fn(cur)

                def col_blocks(lo, hi):
                    return [(c, min(256, hi - c)) for c in range(lo, hi, 256)]

                norm_pass(0, NPT)
                jobs = []
                for (c0, w) in col_blocks(1024, 3264):
                    jobs.append((c0, w, lambda b, c0=c0, w=w: fm_job(b, c0, w, 0, TP, 0, "rw")))
                for (c0, w) in col_blocks(0, 1024) + col_blocks(3264, 3520):
                    jobs.append((c0, w, lambda b, c0=c0, w=w: fm_job(b, c0, w, TP - 128, TP, 0, "rw")))
                for (c0, w) in col_blocks(4544, 6608):
                    jobs.append((c0, w, lambda b, c0=c0, w=w: tm_job(b, c0, w, NPT, 0)))
                run_jobs(jobs)
                norm_pass(TP, NOT_)
                jobs = []
                for (c0, w) in col_blocks(0, 3520):
                    jobs.append((c0, w, lambda b, c0=c0, w=w: fm_job(b, c0, w, 0, TO, TP, "rw")))
                for (c0, w) in col_blocks(3520, 6608):
                    jobs.append((c0, w, lambda b, c0=c0, w=w: tm_job(b, c0, w, NOT_, TP)))
                for (c0, w) in col_blocks(6608, NIN):
                    jobs.append((c0, w, lambda b, c0=c0, w=w: fm_job(b, c0, w, 0, TO, TP, "gt")))
                run_jobs(jobs)
                kb.barrier()

        if "B" in STAGES:
            with ExitStack() as sb:
                T = lambda n, s, d=F32: sb.enter_context(SB(n, list(s), d))
                PS = lambda n, s, d=F32: sb.enter_context(PSM(n, list(s), d))
                mask4 = T("mask4", [128, 512])
                maskL4 = T("maskL4", [128, 512])
                identb4 = T("identb4", [128, 512], BF16)
                tri2 = T("tri2", [128, 256])
                bones = T("bones", [128, 128], BF16)
                hsel = T("hsel", [128, 2], BF16)
                mu24t = T("mu24t", [128, 24]); mul4t = T("mul4t", [128, 4])
                w0t = T("w0t", [128, RW]); vp = T("vp", [128, 4, 8])
                gnwt = T("gnwt", [128, RW]); gnbt = T("gnbt", [128, RW])
                w2b = T("w2b", [96, RW], BF16); a2b = T("a2b", [96, RW], BF16); g2b = T("g2b", [128, 2, RW], BF16)
                zin = T("zin", [128, 24, 129]); zl = T("zl", [128, 4, 129])
                zs = T("zs", [128, 24, 128]); zls = T("zls", [128, 4, 128])
                lt2 = [T(f"lt{i}", [128, 4, 128], BF16) for i in range(2)]
                gam2 = [T(f"gam{i}", [128, 8]) for i in range(2)]
                sgT = T("sgT", [128, RW])
                e1 = T("e1", [128, 8, 128]); e2 = T("e2", [128, 8, 128]); e3 = T("e3", [128, 8, 128])
                alpha = T("alpha", [128, 8, 128])
                kkr = T("kkr", [128, 8, 128]); sqb = T("sqb", [128, 8, 128], BF16)
                rn = T("rn", [128, 8, 128]); kk = T("kk", [128, 8, 128])
                t1 = T("t1", [128, 8, 128]); kf = T("kf", [128, 8, 128]); bb = T("bb", [128, 8, 128])
                AR2 = [T(f"AR{i}", [128, 8, 2, 128], BF16) for i in range(2)]
                KB2 = T("KB2", [128, 8, 2, 128], BF16)
                prod2 = [T(f"prod{i}", [128, 8, 128], BF16) for i in range(2)]
                vsb = T("vsb", [128, 8, 128], BF16)
                Vtok2 = [T(f"Vtok{i}", [128, RW], BF16) for i in range(2)]
                KBtok2 = [T(f"KBtok{i}", [128, 2, RW], BF16) for i in range(2)]
                A4 = T("A4", [128, 16, 512], BF16)
                Nj = [T(f"Nj{i}", [128, 16, 128], BF16) for i in range(2)]
                Mj = [T(f"Mj{i}", [128, 16, 128], BF16) for i in range(2)]
                Qj = [T(f"Qj{i}", [128, 16, 128], BF16) for i in range(2)]
                Hf = T("Hf", [128, 8, 64]); Hb = T("Hb", [128, 8, 64], BF16)
                Xb = T("Xb", [128, RW], BF16); Ub = T("Ub", [128, RW], BF16)
                ysb = T("ysb", [128, 16, 64]); ysq = T("ysq", [128, 16, 64])
                gst = T("gst", [128, 6, 16])
                bon = T("bon", [128, 16])
                yab = T("yab", [128, RW], BF16)
                yaT = T("yaT", [128, 8, 128], BF16)
                psA = [PS(f"psA{i}", [128, 512]) for i in range(4)]
                psB = [PS(f"psB{i}", [128, 512]) for i in range(2)]
                psTb = [PS(f"psTb{i}", [128, 8, 128], BF16) for i in range(2)]

                for j in range(4):
                    src = mUs if j % 2 == 0 else mUi
                    op("dve", lambda v: v.tensor_copy(out=mask4[:, j * 128:(j + 1) * 128], in_=src[:]), w=[("mask4", j)])
                    op("dve", lambda v: v.tensor_copy(out=maskL4[:, j * 128:(j + 1) * 128], in_=mLs[:]), w=[("maskL4", j)])
                    op("dve", lambda v: v.tensor_copy(out=identb4[:, j * 128:(j + 1) * 128], in_=identf[:]), w=[("identb4", j)])
                op("dve", lambda v: v.tensor_scalar(out=tri2[:, 0:128], in0=mUi[:], scalar1=-DECAY_C, scalar2=None, op0=ALU.mult), w=[("tri2", 0)])
                op("dve", lambda v: v.tensor_scalar(out=tri2[:, 128:256], in0=mUs[:], scalar1=-DECAY_C, scalar2=None, op0=ALU.mult), w=[("tri2", 1)])
                op("pool", lambda g: g.memset(bones[:], 0.0), w=["bones"])
                op("pool", lambda g: g.memset(bones[0:64, 0:64], 1.0), r=["bones"], w=["bones"])
                op("pool", lambda g: g.memset(bones[64:128, 64:128], 1.0), r=["bones"], w=["bones"])
                op("pool", lambda g: g.memset(hsel[:], 0.0), w=["hsel"])
                op("pool", lambda g: g.memset(hsel[0:64, 0:1], 1.0), r=["hsel"], w=["hsel"])
                op("pool", lambda g: g.memset(hsel[64:128, 1:2], 1.0), r=["hsel"], w=["hsel"])
                dma("sp", mu24t[:], mu24[:, :], w=["mu24t"]); dma("sp", mul4t[:], mul4[:, :], w=["mul4t"])
                dma("sp", w0t[:], w0bc[:, :], w=["w0t"]); dma("sp", vp[:], vecp[:, :, :], w=["vp"])
                dma("sp", gnwt[:], gnwbc[:, :], w=["gnwt"]); dma("sp", gnbt[:], gnbbc[:, :], w=["gnbt"])
                ldf = zs[:, 0:16, :].rearrange("p (c a) t -> p c (a t)", c=2)
                dma("sp", ldf[0:96, 0, :], w2[:, :], w=[("zs", 0), ("zs", 8)])
                op("dve", lambda v: v.tensor_copy(out=w2b[:], in_=ldf[0:96, 0, :]), r=[("zs", 0), ("zs", 8)], w=["w2b"])
                dma("sp", ldf[0:96, 1, :], a2[:, :], r=[("zs", 0), ("zs", 8)], w=[("zs", 0), ("zs", 8)])
                op("dve", lambda v: v.tensor_copy(out=a2b[:], in_=ldf[0:96, 1, :]), r=[("zs", 0), ("zs", 8)], w=["a2b"])
                dma("sp", ldf[:, :, :], g2.rearrange("(c p) n -> p c n", p=128), r=[("zs", 0), ("zs", 8)], w=[("zs", 0), ("zs", 8)])
                op("dve", lambda v: v.tensor_copy(out=g2b[:], in_=ldf[:, :, :]), r=[("zs", 0), ("zs", 8)], w=["g2b"])
                for i in range(2):
                    op("pool", lambda g: g.memset(AR2[i][:], 0.0), w=[("ARa", i), ("ARr", i)])
                op("dve", lambda v: v.memset(Hf[:], 0.0), w=["Hf"])
                op("dve", lambda v: v.memset(Hb[:], 0.0), w=["Hb"])
                ZR3 = ZR.rearrange("(b p) t -> p b t", p=128)

                BT = [int(t) for t in os.environ.get("KBTILES", "").split(",") if t] or list(range(NT))
                hb = lambda h: (h // 2, 64 * (h % 2))
                ek = [("e1", b_) for b_ in range(8)]; e2k = [("e2", b_) for b_ in range(8)]; e3k = [("e3", b_) for b_ in range(8)]
                alk = [("alpha", b_) for b_ in range(8)]
                kkrk = [("kkr", b_) for b_ in range(8)]
                t1k = [("t1", b_) for b_ in range(8)]

                def prep(ti, p):
                    own = ti >= NPT
                    AR, lt, gam, prod, Vtok, KBtok = AR2[p], lt2[p], gam2[p], prod2[p], Vtok2[p], KBtok2[p]
                    c0 = ti * 128
                    blo = 0 if own else 8
                    dma("sp", zin[:, blo:24, :], ZR3[:, blo:24, c0:c0 + 129], w=["zin"])
                    nl = 4 if own else 2
                    dma("sp", zl[0:96, 0, :], ZR[3072:3168, c0:c0 + 129], w=["zl0"])
                    dma("sp", zl[0:96, 1, :], ZR[3168:3264, c0:c0 + 129], w=["zl1"])
                    if own:
                        dma("sp", zl[:, 2:4, :], ZR[3264:3520, c0:c0 + 129].rearrange("(b p) t -> p b t", p=128), w=["zl2"])
                    yield
                    for (g0, g1) in ((8, 16), (16, 24), (0, 8)):
                        if g1 <= blo:
                            continue
                        zk = ("zs", g0)
                        op("pool", lambda g: g.tensor_tensor(out=zs[:, g0:g1, :], in0=zin[:, g0:g1, 0:128], in1=zin[:, g0:g1, 1:129], op=ALU.subtract), r=["zin"], w=[zk])
                        op("pool", lambda g: g.tensor_tensor(out=zs[:, g0:g1, :], in0=zs[:, g0:g1, :], in1=bcast_last(mu24t[:, g0:g1], 128), op=ALU.mult), r=[zk, "mu24t"], w=[zk])
                        op("pool", lambda g: g.tensor_tensor(out=zs[:, g0:g1, :], in0=zs[:, g0:g1, :], in1=zin[:, g0:g1, 1:129], op=ALU.add), r=[zk, "zin"], w=[zk])
                        yield
                    zK, zV, zR = ("zs", 8), ("zs", 16), ("zs", 0)
                    for j in range(nl):
                        pr = 96 if j < 2 else 128
                        zk = "zl%d" % min(j, 2)
                        op("dve", lambda v: v.tensor_tensor(out=zls[0:pr, j, :], in0=zl[0:pr, j, 0:128], in1=zl[0:pr, j, 1:129], op=ALU.subtract), r=[zk], w=[("zls", j)])
                        op("dve", lambda v: v.scalar_tensor_tensor(out=zls[0:pr, j, :], in0=zls[0:pr, j, :], scalar=mul4t[0:pr, j:j + 1], in1=zl[0:pr, j, 1:129], op0=ALU.mult, op1=ALU.add), r=[("zls", j), zk, "mul4t"], w=[("zls", j)])
                    yield
                    op("act", lambda a: a.activation(out=lt[0:96, 0, :], in_=zls[0:96, 0, :], func=AF.Tanh), r=[("zls", 0)], w=[("lt", 0, p)])
                    op("act", lambda a: a.copy(out=lt[0:96, 1, :], in_=zls[0:96, 1, :]), r=[("zls", 1)], w=[("lt", 1, p)])
                    if own:
                        op("act", lambda a: a.activation(out=lt[:, 2:4, :], in_=zls[:, 2:4, :], func=AF.Sigmoid), r=[("zls", 2), ("zls", 3)], w=[("lt", 2, p)])
                    yield
                    for hh in range(2):
                        op("pe", lambda t: t.matmul(psB[hh][:, :], lhsT=lt[0:96, 0, :], rhs=w2b[:, hh * 512:(hh + 1) * 512], start=True, stop=True), r=[("lt", 0, p), "w2b"], w=[("psB", hh)])
                        op("dve", lambda v: v.tensor_tensor(out=sgT[:, hh * 512:(hh + 1) * 512], in0=psB[hh][:, :], in1=w0t[:, hh * 512:(hh + 1) * 512], op=ALU.add), r=[("psB", hh), "w0t"], w=[("sgT", hh)])
                        op("act", lambda a: a.activation(out=sgT[:, hh * 512:(hh + 1) * 512], in_=sgT[:, hh * 512:(hh + 1) * 512], func=AF.Sigmoid), r=[("sgT", hh)], w=[("sgT", hh)])
                        yield
                    for hh in range(2):
                        for c in range(4):
                            blk = hh * 4 + c
                            op("pe", lambda t: t.matmul(psB[hh][:, c * 128:(c + 1) * 128], lhsT=a2b[:, blk * 128:(blk + 1) * 128], rhs=lt[0:96, 1, :], start=True, stop=True), r=[("lt", 1, p), "a2b"], w=[("psB", hh)])
                        for c in range(4):
                            blk = hh * 4 + c
                            op("act", lambda a: a.activation(out=alpha[:, blk, :], in_=psB[hh][:, c * 128:(c + 1) * 128], func=AF.Sigmoid, bias=vp[:, 0, blk:blk + 1]), r=[("psB", hh), "vp"], w=[("alpha", blk)])
                        yield
                    for blk in range(8):
                        pi = blk % 2
                        op("pe", lambda t: t.matmul(psB[pi][:, 0:256], lhsT=sgT[:, blk * 128:(blk + 1) * 128], rhs=tri2[:, :], start=True, stop=True), r=[("sgT", blk // 4), ("tri2", 0), ("tri2", 1)], w=[("psB", pi)])
                        op("act", lambda a: a.activation(out=e1[:, blk, :], in_=psB[pi][:, 0:128], func=AF.Exp), r=[("psB", pi)], w=[("e1", blk)])
                        op("act", lambda a: a.activation(out=e2[:, blk, :], in_=psB[pi][:, 0:128], func=AF.Exp, scale=-1.0), r=[("psB", pi)], w=[("e2", blk)])
                        op("act", lambda a: a.activation(out=e3[:, blk, :], in_=psB[pi][:, 128:256], func=AF.Exp), r=[("psB", pi)], w=[("e3", blk)])
                        yield
                    op("act", lambda a: a.copy(out=gam[:], in_=e1[:, :, 127]), r=ek, w=[("gam", p)])
                    for blk in range(8):
                        op("dve", lambda v: v.tensor_scalar(out=kkr[:, blk, :], in0=zs[:, 8 + blk, :], scalar1=vp[:, 1, blk:blk + 1], scalar2=None, op0=ALU.mult), r=[zK, "vp"], w=[("kkr", blk)])
                    op("dve", lambda v: v.tensor_tensor(out=sqb[:], in0=kkr[:], in1=kkr[:], op=ALU.mult), r=kkrk, w=["sqb"])
                    yield
                    for hh in range(2):
                        for c in range(4):
                            blk = hh * 4 + c
                            op("pe", lambda t: t.matmul(psB[hh][:, c * 128:(c + 1) * 128], lhsT=bones[:], rhs=sqb[:, blk, :], start=True, stop=True), r=["sqb", "bones"], w=[("psB", hh)])
                        op("act", lambda a: a.activation(out=rn[:, hh * 4:(hh + 1) * 4, :], in_=psB[hh][:, :].rearrange("p (c t) -> p c t", c=4), func=AF.Ln, bias=cst[:, 2:3]), r=[("psB", hh)], w=[("rn", hh)])
                        op("act", lambda a: a.activation(out=rn[:, hh * 4:(hh + 1) * 4, :], in_=rn[:, hh * 4:(hh + 1) * 4, :], func=AF.Exp, scale=-0.5), r=[("rn", hh)], w=[("rn", hh)])
                        yield
                    op("dve", lambda v: v.tensor_tensor(out=kk[:], in0=kkr[:], in1=rn[:], op=ALU.mult), r=kkrk + [("rn", 0), ("rn", 1)], w=["kk"])
                    for blk in range(8):
                        op("dve", lambda v: v.tensor_scalar(out=t1[:, blk, :], in0=alpha[:, blk, :], scalar1=-1.0, scalar2=vp[:, 2, blk:blk + 1], op0=ALU.add, op1=ALU.mult), r=[("alpha", blk), "vp"], w=[("t1", blk)])
                    yield
                    op("dve", lambda v: v.scalar_tensor_tensor(out=kf[:], in0=t1[:], scalar=1.0, in1=zs[:, 8:16, :], op0=ALU.add, op1=ALU.mult), r=t1k + [zK], w=["kf"])
                    op("pool", lambda g: g.tensor_tensor(out=bb[:], in0=kk[:], in1=alpha[:], op=ALU.mult), r=["kk"] + alk, w=["bb"])
                    yield
                    op("dve", lambda v: v.scalar_tensor_tensor(out=AR[:, :, 0, :], in0=kk[:], scalar=-1.0, in1=e3[:], op0=ALU.mult, op1=ALU.mult), r=["kk"] + e3k, w=[("ARa", p)])
                    if own:
                        op("pool", lambda g: g.tensor_tensor(out=AR[:, :, 1, :], in0=zs[:, 0:8, :], in1=e1[:], op=ALU.mult), r=[zR] + ek, w=[("ARr", p)])
                    yield
                    op("dve", lambda v: v.tensor_tensor(out=KB2[:, :, 0, :], in0=kf[:], in1=e2[:], op=ALU.mult), r=["kf"] + e2k, w=["KBk"])
                    op("pool", lambda g: g.tensor_tensor(out=KB2[:, :, 1, :], in0=bb[:], in1=e2[:], op=ALU.mult), r=["bb"] + e2k, w=["KBb"])
                    op("act", lambda a: a.copy(out=vsb[:], in_=zs[:, 16:24, :]), r=[zV], w=["vsb"])
                    yield
                    for c in range(8):
                        op("pe", lambda t: t.transpose(out=psTb[0][:, c, :], in_=vsb[:, c, :], identity=ident[:]), r=["vsb", "ident"], w=[("psTb", 0)])
                    op("act", lambda a: a.copy(out=Vtok[:], in_=psTb[0][:].rearrange("p c t -> p (c t)")), r=[("psTb", 0)], w=[("Vtok", p)])
                    yield
                    for j in range(2):
                        for c in range(8):
                            op("pe", lambda t: t.transpose(out=psTb[1][:, c, :], in_=KB2[:, c, j, :], identity=ident[:]), r=["KBk", "KBb", "ident"], w=[("psTb", 1)])
                        op("dve", lambda v: v.tensor_copy(out=KBtok[:, j, :], in_=psTb[1][:].rearrange("p c t -> p (c t)")), r=[("psTb", 1)], w=[("KBtok", j, p)])
                        yield
                    if own:
                        for blk in range(8):
                            op("dve", lambda v: v.scalar_tensor_tensor(out=prod[:, blk, :], in0=zs[:, blk, :], scalar=vp[:, 3, blk:blk + 1], in1=kf[:, blk, :], op0=ALU.mult, op1=ALU.mult), r=[zR, "kf", "vp"], w=[("prod", blk, p)])
                        yield

                def mm(ti, p):
                    AR = AR2[p]
                    for h in range(16):
                        blk, pb = hb(h)
                        pi = h % 4
                        rd = [("ARa", p), "KBk", "KBb", ("ARr", p)]
                        op("pe", lambda t: t.matmul(psA[pi][:, 0:256], lhsT=KB2[pb:pb + 64, blk, 1, :], rhs=AR[pb:pb + 64, blk, :, :].rearrange("p a t -> p (a t)"), start=True, stop=True), r=rd, w=[("psA", pi)])
                        op("pe", lambda t: t.matmul(psA[pi][:, 256:512], lhsT=KB2[pb:pb + 64, blk, 0, :], rhs=AR[pb:pb + 64, blk, :, :].rearrange("p a t -> p (a t)"), start=True, stop=True), r=rd, w=[("psA", pi)])
                        op("dve", lambda v: v.tensor_tensor(out=A4[:, h, :], in0=psA[pi][:, :], in1=mask4[:], op=ALU.mult), r=[("psA", pi)] + [("mask4", j) for j in range(4)], w=[("A4", h)])
                    for hg in range(4):
                        hk = [("A4", hg * 4 + i) for i in range(4)]
                        op("pool", lambda g: g.tensor_copy(out=Mj[0][:, hg * 4:(hg + 1) * 4, :], in_=A4[:, hg * 4:(hg + 1) * 4, 0:128]), r=hk, w=[("M0", hg)])
                        op("pool", lambda g: g.tensor_tensor(out=Qj[0][:, hg * 4:(hg + 1) * 4, :], in0=A4[:, hg * 4:(hg + 1) * 4, 0:128], in1=identb4[:].rearrange("p (h t) -> p h t", h=4), op=ALU.add), r=hk + [("identb4", j) for j in range(4)], w=[("Q0", hg)])
                    for g8 in range(2):
                        for par in range(2):
                            for i in range(4):
                                h = g8 * 8 + 2 * i + par
                                blk, pb = hb(h)
                                op("pe", lambda t: t.matmul(psA[2 + par][:, i * 128:(i + 1) * 128], lhsT=AR[pb:pb + 64, blk, 0, :], rhs=KB2[pb:pb + 64, blk, 1, :], start=True, stop=True), r=[("ARa", p), "KBb"], w=[("psA", 2 + par)])
                        for par in range(2):
                            op("dve", lambda v: v.tensor_tensor(out=Nj[0][:, g8 * 8 + par:g8 * 8 + 8:2, :], in0=psA[2 + par][:, :].rearrange("p (h t) -> p h t", h=4), in1=maskL4[:].rearrange("p (h t) -> p h t", h=4), op=ALU.mult), r=[("psA", 2 + par)] + [("maskL4", j) for j in range(4)], w=[("N0", 2 * g8), ("N0", 2 * g8 + 1)])

                def inverse(ti):
                    bank = [0]
                    def nb():
                        bank[0] += 1
                        return bank[0] % 4
                    for lvl in range(6):
                        ci, ni = lvl % 2, (lvl + 1) % 2
                        for hg in range(4):
                            pi = nb()
                            for i in range(4):
                                h = hg * 4 + i
                                op("pe", lambda t: t.matmul(psA[pi][:, i * 128:(i + 1) * 128], lhsT=Mj[ci][:, h, :], rhs=Nj[ci][:, h, :], start=True, stop=True), r=[("M%d" % ci, hg), ("N%d" % ci, hg)], w=[("psA", pi)])
                            op("act", lambda a: a.copy(out=Nj[ni][:, hg * 4:(hg + 1) * 4, :].rearrange("p h t -> p (h t)"), in_=psA[pi][:, :]), r=[("psA", pi)], w=[("N%d" % ni, hg)])
                            yield
                        if lvl < 5:
                            for hg in range(4):
                                pi = nb()
                                for i in range(4):
                                    h = hg * 4 + i
                                    op("pe", lambda t: t.matmul(psA[pi][:, i * 128:(i + 1) * 128], lhsT=Nj[ci][:, h, :], rhs=Mj[ci][:, h, :], start=True, stop=True), r=[("M%d" % ci, hg), ("N%d" % ci, hg)], w=[("psA", pi)])
                                op("act", lambda a: a.copy(out=Mj[ni][:, hg * 4:(hg + 1) * 4, :].rearrange("p h t -> p (h t)"), in_=psA[pi][:, :]), r=[("psA", pi)], w=[("M%d" % ni, hg)])
                                yield
                        for hg in range(4):
                            pi = nb()
                            for i in range(4):
                                h = hg * 4 + i
                                op("pe", lambda t: t.matmul(psA[pi][:, i * 128:(i + 1) * 128], lhsT=Nj[ni][:, h, :], rhs=Qj[ci][:, h, :], start=True, stop=True), r=[("N%d" % ni, hg), ("Q%d" % ci, hg)], w=[("psA", pi)])
                            op("dve", lambda v: v.tensor_tensor(out=Qj[ni][:, hg * 4:(hg + 1) * 4, :].rearrange("p h t -> p (h t)"), in0=psA[pi][:, :], in1=Qj[ci][:, hg * 4:(hg + 1) * 4, :].rearrange("p h t -> p (h t)"), op=ALU.add), r=[("psA", pi), ("Q%d" % ci, hg)], w=[("Q%d" % ni, hg)])
                            yield

                def chain(ti, p):
                    own = ti >= NPT
                    AR, lt, gam, prod, Vtok, KBtok = AR2[p], lt2[p], gam2[p], prod2[p], Vtok2[p], KBtok2[p]
                    QT = Qj[0]
                    vtk = ("Vtok", p)
                    for hh in range(2):
                        for i in range(8):
                            h = hh * 8 + i
                            blk, pb = hb(h)
                            op("pe", lambda t: t.matmul(psB[hh][:, i * 64:(i + 1) * 64], lhsT=AR[pb:pb + 64, blk, 0, :], rhs=Hb[pb:pb + 64, blk, :], start=True, stop=False), r=[("ARa", p), "Hb"], w=[("psB", hh)])
                            op("pe", lambda t: t.matmul(psB[hh][:, i * 64:(i + 1) * 64], lhsT=A4[:, h, 256:384], rhs=Vtok[:, h * 64:(h + 1) * 64], start=False, stop=True), r=[("A4", h), vtk], w=[("psB", hh)])
                        if hh == 0:
                            op("act", lambda a: a.copy(out=Xb[:, 0:512], in_=psB[0][:, :]), r=[("psB", 0)], w=[("Xb", 0)])
                        else:
                            op("dve", lambda v: v.tensor_copy(out=Xb[:, 512:1024], in_=psB[1][:, :]), r=[("psB", 1)], w=[("Xb", 1)])
                    for hh in range(2):
                        for i in range(8):
                            h = hh * 8 + i
                            op("pe", lambda t: t.matmul(psB[hh][:, i * 64:(i + 1) * 64], lhsT=QT[:, h, :], rhs=Xb[:, h * 64:(h + 1) * 64], start=True, stop=True), r=[("Q0", h // 4), ("Xb", hh)], w=[("psB", hh)])
                        if hh == 0:
                            op("act", lambda a: a.copy(out=Ub[:, 0:512], in_=psB[0][:, :]), r=[("psB", 0)], w=[("Ub", 0)])
                        else:
                            op("dve", lambda v: v.tensor_copy(out=Ub[:, 512:1024], in_=psB[1][:, :]), r=[("psB", 1)], w=[("Ub", 1)])
                    if own:
                        for hh in range(2):
                            pi = 2 + hh
                            for i in range(8):
                                h = hh * 8 + i
                                blk, pb = hb(h)
                                op("pe", lambda t: t.matmul(psA[pi][:, i * 64:(i + 1) * 64], lhsT=AR[pb:pb + 64, blk, 1, :], rhs=Hb[pb:pb + 64, blk, :], start=True, stop=False), r=[("ARr", p), "Hb"], w=[("psA", pi)])
                                op("pe", lambda t: t.matmul(psA[pi][:, i * 64:(i + 1) * 64], lhsT=A4[:, h, 128:256], rhs=Ub[:, h * 64:(h + 1) * 64], start=False, stop=False), r=[("A4", h), ("Ub", hh)], w=[("psA", pi)])
                                op("pe", lambda t: t.matmul(psA[pi][:, i * 64:(i + 1) * 64], lhsT=A4[:, h, 384:512], rhs=Vtok[:, h * 64:(h + 1) * 64], start=False, stop=True), r=[("A4", h), vtk], w=[("psA", pi)])
                            op("act", lambda a: a.copy(out=ysb[:, hh * 8:(hh + 1) * 8, :].rearrange("p h v -> p (h v)"), in_=psA[pi][:, :]), r=[("psA", pi)], w=[("ysb", hh)])
                    for blk in range(8):
                        for half in range(2):
                            h = blk * 2 + half
                            pb = 64 * half
                            op("pe", lambda t: t.matmul(psA[0][pb:pb + 64, blk * 64:(blk + 1) * 64], lhsT=KBtok[:, 0, blk * 128 + pb: blk * 128 + pb + 64], rhs=Vtok[:, h * 64:(h + 1) * 64], start=True, stop=False), r=[("KBtok", 0, p), vtk], w=[("psA", 0)])
                            op("pe", lambda t: t.matmul(psA[0][pb:pb + 64, blk * 64:(blk + 1) * 64], lhsT=KBtok[:, 1, blk * 128 + pb: blk * 128 + pb + 64], rhs=Ub[:, h * 64:(h + 1) * 64], start=False, stop=True), r=[("KBtok", 1, p), ("Ub", h // 8)], w=[("psA", 0)])
                    op("dve", lambda v: v.tensor_tensor(out=Hf[:].rearrange("p b v -> p (b v)"), in0=psA[0][:, :], in1=Hf[:].rearrange("p b v -> p (b v)"), op=ALU.add), r=[("psA", 0), "Hf"], w=["Hf"])
                    op("dve", lambda v: v.tensor_tensor(out=Hf[:], in0=Hf[:], in1=bcast_last(gam[:, :], 64), op=ALU.mult), r=["Hf", ("gam", p)], w=["Hf"])
                    op("act", lambda a: a.copy(out=Hb[:], in_=Hf[:]), r=["Hf"], w=["Hb"])
                    if own:
                        oi = ti - NPT
                        yk = [("ysb", 0), ("ysb", 1)]
                        for hh in range(2):
                            for kc in range(2):
                                op("pe", lambda t: t.matmul(psB[hh][:, :], lhsT=lt[:, 2 + kc, :], rhs=g2b[:, kc, hh * 512:(hh + 1) * 512], start=(kc == 0), stop=(kc == 1)), r=[("lt", 2, p), "g2b"], w=[("psB", hh)])
                        for blk in range(8):
                            op("pe", lambda t: t.matmul(psA[1][:, blk * 2:(blk + 1) * 2], lhsT=prod[:, blk, :], rhs=hsel[:], start=True, stop=True), r=[("prod", blk, p), "hsel"], w=[("psA", 1)])
                        op("act", lambda a: a.copy(out=bon[:], in_=psA[1][:, 0:16]), r=[("psA", 1)], w=["bon"])
                        op("dve", lambda v: v.tensor_reduce(out=gst[:, 0, :], in_=ysb[:], axis=AX.X, op=ALU.add), r=yk, w=[("gst", 0)])
                        op("pool", lambda g: g.tensor_tensor(out=ysq[:], in0=ysb[:], in1=ysb[:], op=ALU.mult), r=yk, w=["ysq"])
                        op("dve", lambda v: v.tensor_reduce(out=gst[:, 1, :], in_=ysq[:], axis=AX.X, op=ALU.add), r=["ysq"], w=[("gst", 1)])
                        op("dve", lambda v: v.tensor_scalar(out=gst[:, 2, :], in0=gst[:, 0, :], scalar1=1.0 / 64, scalar2=None, op0=ALU.mult), r=[("gst", 0)], w=[("gst", 2)])
                        op("dve", lambda v: v.tensor_tensor(out=gst[:, 3, :], in0=gst[:, 2, :], in1=gst[:, 2, :], op=ALU.mult), r=[("gst", 2)], w=[("gst", 3)])
                        op("dve", lambda v: v.scalar_tensor_tensor(out=gst[:, 4, :], in0=gst[:, 1, :], scalar=1.0 / 64, in1=gst[:, 3, :], op0=ALU.mult, op1=ALU.subtract), r=[("gst", 1), ("gst", 3)], w=[("gst", 4)])
                        op("act", lambda a: a.activation(out=gst[:, 5, :], in_=gst[:, 4, :], func=AF.Ln, bias=cst[:, 1:2]), r=[("gst", 4)], w=[("gst", 5)])
                        op("act", lambda a: a.activation(out=gst[:, 5, :], in_=gst[:, 5, :], func=AF.Exp, scale=-0.5), r=[("gst", 5)], w=[("gst", 5)])
                        op("dve", lambda v: v.tensor_tensor(out=ysq[:], in0=ysb[:], in1=bcast_last(gst[:, 2, :], 64), op=ALU.subtract), r=yk + [("gst", 2), "ysq"], w=["ysq"])
                        op("dve", lambda v: v.tensor_tensor(out=ysq[:], in0=ysq[:], in1=bcast_last(gst[:, 5, :], 64), op=ALU.mult), r=["ysq", ("gst", 5)], w=["ysq"])
                        ysq2 = ysq[:].rearrange("p h v -> p (h v)")
                        op("dve", lambda v: v.tensor_tensor(out=ysq2, in0=ysq2, in1=gnwt[:], op=ALU.mult), r=["ysq", "gnwt"], w=["ysq"])
                        op("dve", lambda v: v.tensor_tensor(out=ysq2, in0=ysq2, in1=gnbt[:], op=ALU.add), r=["ysq", "gnbt"], w=["ysq"])
                        op("dve", lambda g: g.tensor_tensor(out=ysb[:], in0=Vtok[:].rearrange("p (h v) -> p h v", h=16), in1=bcast_last(bon[:, :], 64), op=ALU.mult), r=[vtk, "bon"] + yk, w=yk)
                        op("dve", lambda v: v.tensor_tensor(out=ysq[:], in0=ysq[:], in1=ysb[:], op=ALU.add), r=["ysq"] + yk, w=["ysq"])
                        for hh in range(2):
                            op("dve", lambda v: v.tensor_tensor(out=yab[:, hh * 512:(hh + 1) * 512], in0=ysq[:, hh * 8:(hh + 1) * 8, :].rearrange("p h v -> p (h v)"), in1=psB[hh][:, :], op=ALU.mult), r=["ysq", ("psB", hh)], w=[("yab", hh)])
                        for c in range(8):
                            op("pe", lambda t: t.transpose(out=psTb[0][:, c, :], in_=yab[:, c * 128:(c + 1) * 128], identity=ident[:]), r=[("yab", c // 4), "ident"], w=[("psTb", 0)])
                        op("act", lambda a: a.copy(out=yaT[:], in_=psTb[0][:]), r=[("psTb", 0)], w=["yaT"])
                        dma("sp", YAT.rearrange("(c p) t -> p c t", p=128)[:, :, oi * 128:(oi + 1) * 128], yaT[:], r=["yaT"], w=[("YAT", oi)])

                def run_rr(gens):
                    gens = list(gens)
                    while gens:
                        for g_ in list(gens):
                            try:
                                next(g_)
                            except StopIteration:
                                gens.remove(g_)

                run_rr([prep(BT[0], 0)])
                for k_, ti in enumerate(BT):
                    p = k_ % 2
                    mm(ti, p)
                    gens = [inverse(ti)]
                    if k_ + 1 < len(BT):
                        gens.append(prep(BT[k_ + 1], 1 - p))
                    run_rr(gens)
                    chain(ti, p)
                kb.barrier()


        nbt = es.enter_context(SB("nbt", [128, 16, NT], F32))
        if "C" in STAGES:
            with ExitStack() as sc:
                T = lambda n, s, d=F32: sc.enter_context(SB(n, list(s), d))
                PS = lambda n, s, d=F32: sc.enter_context(PSM(n, list(s), d))
                qgt = T("qgt", [128, RW]); kgt = T("kgt", [128, RW]); fbt = T("fbt", [128, 16]); kpt = T("kpt", [128, NT])
                carry = T("carry", [128, 16])
                cT = T("cT", [16, TO]); r1 = T("r1", [16, TO]); chi = T("chi", [16, 3, TO], BF16)
                onesb = T("onesb", [16, 3, 1408], BF16)
                psT = [PS(f"psT{i}", [64, 16, 128], BF16) for i in range(2)]
                dma("sp", qgt[:], qgbc[:, :], w=["qgt"]); dma("sp", kgt[:], kgbc[:, :], w=["kgt"])
                dma("sp", fbt[:], fbbc[:, :], w=["fbt"]); dma("sp", kpt[:], keypen[:, :], w=["kpt"])
                op("dve", lambda v: v.tensor_scalar(out=qgt[:], in0=qgt[:], scalar1=0.125, scalar2=None, op0=ALU.mult), r=["qgt"], w=["qgt"])
                op("dve", lambda v: v.memset(carry[:], 0.0), w=["carry"])
                op("pool", lambda g: g.memset(onesb[:], 1.0), w=["onesb"])
                for c3 in range(3):
                    dma("sp", KTA[:, 64:67, c3 * 1408:(c3 + 1) * 1408], onesb[:], r=["onesb"], w=[("KTA1", c3)])

                NP_ = 3
                sqs = [T(f"sq{i}", [128, 16, 64]) for i in range(NP_)]; sts = [T(f"st{i}", [128, 4, 16]) for i in range(NP_)]
                nrms = [T(f"nrm{i}", [128, 16, 64]) for i in range(NP_)]
                sq2s = [T(f"sqq{i}", [128, 16, 64]) for i in range(NP_)]; st2s = [T(f"stq{i}", [128, 4, 16]) for i in range(NP_)]
                nrm2s = [T(f"nrmq{i}", [128, 16, 64]) for i in range(NP_)]
                zfs = [T(f"zfp{i}", [128, FC]) for i in range(NP_)]
                knbs = [T(f"knb{i}", [128, RW], BF16) for i in range(NP_)]; qnbs = [T(f"qnb{i}", [128, RW], BF16) for i in range(NP_)]
                vbs = [T(f"vb{i}", [128, RW], BF16) for i in range(NP_)]
                ktiles = [T(f"ktile{i}", [64, 16, 128], BF16) for i in range(NP_)]; qtiles = [T(f"qtile{i}", [64, 16, 128], BF16) for i in range(NP_)]
                fls = [T(f"fl{i}", [128, 4, 16]) for i in range(NP_)]; ctiles = [T(f"ctile{i}", [128, 16]) for i in range(NP_)]
                pscs = [PS(f"pscp{i}", [128, 512]) for i in range(2)]

                def rmsn(src, gt, dst, tag, dkey, sq, st, nrm, kk_):
                    s3 = src.rearrange("p (h v) -> p h v", h=16)
                    op("pool", lambda g: g.tensor_tensor(out=sq[:], in0=s3, in1=s3, op=ALU.mult), r=[tag], w=[("sq",) + kk_])
                    yield
                    op("dve", lambda v: v.tensor_reduce(out=st[:, 0, :], in_=sq[:], axis=AX.X, op=ALU.add), r=[("sq",) + kk_], w=[("st", 0) + kk_])
                    op("act", lambda a: a.activation(out=st[:, 1, :], in_=st[:, 0, :], func=AF.Ln, bias=cst[:, 0:1], scale=1.0 / 64), r=[("st", 0) + kk_], w=[("st", 1) + kk_])
                    op("act", lambda a: a.activation(out=st[:, 2, :], in_=st[:, 1, :], func=AF.Exp, scale=-0.5), r=[("st", 1) + kk_], w=[("st", 2) + kk_])
                    yield
                    op("dve", lambda v: v.tensor_tensor(out=nrm[:], in0=s3, in1=bcast_last(st[:, 2, :], 64), op=ALU.mult), r=[tag, ("st", 2) + kk_], w=[("nrm",) + kk_])
                    op("dve", lambda v: v.tensor_tensor(out=dst[:], in0=nrm[:].rearrange("p h v -> p (h v)"), in1=gt[:], op=ALU.mult), r=[("nrm",) + kk_, "qgt", "kgt"], w=[dkey])
                    yield

                def cprep(ti, p):
                    own = ti >= NPT
                    zfb = zfs[p]
                    zt = ("zf", p)
                    c_lo = 0 if own else 1024
                    dma("sp", zfb[:, c_lo:FC], ZFX[ti * 128:(ti + 1) * 128, c_lo:FC], w=[zt])
                    yield
                    fl = fls[p]; ctile = ctiles[p]; psc = pscs[p % 2]
                    op("dve", lambda v: v.tensor_tensor(out=fl[:, 0, :], in0=zfb[:, 3072:3088], in1=fbt[:], op=ALU.add), r=[zt, "fbt"], w=[("fl", 0, p)])
                    op("act", lambda a: a.activation(out=fl[:, 1, :], in_=fl[:, 0, :], func=AF.Exp, scale=-1.0), r=[("fl", 0, p)], w=[("fl", 1, p)])
                    op("act", lambda a: a.activation(out=fl[:, 2, :], in_=fl[:, 1, :], func=AF.Ln, bias=cst[:, 3:4]), r=[("fl", 1, p)], w=[("fl", 2, p)])
                    op("pe", lambda t: t.matmul(psc[:, 0:16], lhsT=mUi[:], rhs=fl[:, 2, :], start=True, stop=True), r=[("fl", 2, p)], w=[("psc", p % 2)])
                    op("pe", lambda t: t.matmul(psc[:, 16:32], lhsT=onesf[:], rhs=fl[:, 2, :], start=True, stop=True), r=[("fl", 2, p)], w=[("psc", p % 2)])
                    op("dve", lambda v: v.tensor_tensor(out=ctile[:], in0=carry[:], in1=psc[:, 0:16], op=ALU.subtract), r=["carry", ("psc", p % 2)], w=[("ctile", p)])
                    op("dve", lambda v: v.tensor_tensor(out=carry[:], in0=carry[:], in1=psc[:, 16:32], op=ALU.subtract), r=["carry", ("psc", p % 2)], w=["carry"])
                    op("dve", lambda v: v.tensor_scalar(out=nbt[:, :, ti], in0=ctile[:], scalar1=-1.0, scalar2=kpt[:, ti:ti + 1], op0=ALU.mult, op1=ALU.add), r=[("ctile", p), "kpt"], w=[("nbt", ti)])
                    if own:
                        oi = ti - NPT
                        op("pe", lambda t: t.transpose(out=psc[0:16, 128:256], in_=ctile[:], identity=identf[:]), r=[("ctile", p)], w=[("psc", p % 2)])
                        op("act", lambda a: a.copy(out=cT[:, oi * 128:(oi + 1) * 128], in_=psc[0:16, 128:256]), r=[("psc", p % 2)], w=[("cT", oi)])
                    yield
                    op("pool", lambda g: g.tensor_copy(out=vbs[p][:], in_=zfb[:, 2048:3072]), r=[zt], w=[("vb", p)])
                    dma("sp", VFX[ti * 128:(ti + 1) * 128, :], vbs[p][:], r=[("vb", p)], w=[("VFX", ti)])
                    yield
                    yield from rmsn(zfb[:, 1024:2048], kgt, knbs[p], zt, ("knb", p), sqs[p], sts[p], nrms[p], ("k", p))
                    for hh in range(2):
                        for i in range(8):
                            h = hh * 8 + i
                            op("pe", lambda t: t.transpose(out=psT[0][:, h, :], in_=knbs[p][:, h * 64:(h + 1) * 64], identity=ident[:]), r=[("knb", p), "ident"], w=["psT0"])
                    op("act", lambda a: a.copy(out=ktiles[p][:], in_=psT[0][:]), r=["psT0"], w=[("ktile", p)])
                    dma("sp", KTA[:, 0:64, ti * 128:(ti + 1) * 128].rearrange("h r t -> r h t"), ktiles[p][:], r=[("ktile", p)], w=[("KTA", ti)])
                    yield
                    if own:
                        oi = ti - NPT
                        yield from rmsn(zfb[:, 0:1024], qgt, qnbs[p], zt, ("qnb", p), sq2s[p], st2s[p], nrm2s[p], ("q", p))
                        for h in range(16):
                            op("pe", lambda t: t.transpose(out=psT[1][:, h, :], in_=qnbs[p][:, h * 64:(h + 1) * 64], identity=ident[:]), r=[("qnb", p), "ident"], w=["psT1"])
                        op("act", lambda a: a.copy(out=qtiles[p][:], in_=psT[1][:]), r=["psT1"], w=[("qtile", p)])
                        dma("sp", QTA[:, 0:64, oi * 128:(oi + 1) * 128].rearrange("h r t -> r h t"), qtiles[p][:], r=[("qtile", p)], w=[("QTA", oi)])
                        yield

                live = []
                nxt_t = 0
                while live or nxt_t < NT:
                    while len(live) < NP_ and nxt_t < NT:
                        live.append(cprep(nxt_t, nxt_t % NP_))
                        nxt_t += 1
                    for g_ in list(live):
                        try:
                            next(g_)
                        except StopIteration:
                            live.remove(g_)
                ctk = [("cT", i) for i in range(NOT_)]
                op("dve", lambda v: v.tensor_copy(out=chi[:, 0, :], in_=cT[:]), r=ctk, w=[("chi", 0)])
                op("dve", lambda v: v.tensor_tensor(out=r1[:], in0=cT[:], in1=chi[:, 0, :], op=ALU.subtract), r=ctk + [("chi", 0)], w=["r1"])
                op("dve", lambda v: v.tensor_copy(out=chi[:, 1, :], in_=r1[:]), r=["r1"], w=[("chi", 1)])
                op("dve", lambda v: v.tensor_tensor(out=r1[:], in0=r1[:], in1=chi[:, 1, :], op=ALU.subtract), r=["r1", ("chi", 1)], w=["r1"])
                op("dve", lambda v: v.tensor_copy(out=chi[:, 2, :], in_=r1[:]), r=["r1"], w=[("chi", 2)])
                dma("sp", QTA[:, 64:67, :], chi[:], r=[("chi", 0), ("chi", 1), ("chi", 2)], w=["QTA1"])
                kb.barrier()

            with ExitStack() as sc:
                T = lambda n, s, d=F32: sc.enter_context(SB(n, list(s), d))
                PS = lambda n, s, d=F32: sc.enter_context(PSM(n, list(s), d))
                QT = [T(f"QT{i}", [67, TO], BF16) for i in range(2)]
                KT = [T(f"KT{i}", [67, TA], BF16) for i in range(2)]
                Vh = [T(f"Vh{i}", [128, NT, 65], BF16) for i in range(2)]
                NSB = 5
                pt = [T(f"pt{i}", [128, 512], BF16) for i in range(NSB)]
                recrs = [T(f"recr{i}", [128, 512]) for i in range(2)]; recb = T("recb", [64, 512])
                deferred = []
                ybo = [T(f"ybo{i}", [64, 512], BF16) for i in range(2)]
                pss = [PS(f"pss{i}", [128, 512]) for i in range(NSB)]
                pso = [PS(f"pso{i}", [128, 512]) for i in range(2)]
                psr = PS("psr", [128, 512])
                for i in range(2):
                    op("pool", lambda g: g.memset(Vh[i][:, :, 64:65], 1.0), w=[("Vh1", i)])
                cnt = {"s": 0, "o": 0}

                def load_head(h):
                    b = h % 2
                    dma("sp", QT[b][:], QTA[h, :, :], w=[("QT", b)])
                    dma("sp", KT[b][:], KTA[h, :, :], w=[("KT", b)])
                    for q4 in range(3):
                        t0, t1_ = q4 * 11, (q4 + 1) * 11
                        dma("sp", Vh[b][:, t0:t1_, 0:64], VFX[t0 * 128:t1_ * 128, h * 64:(h + 1) * 64].rearrange("(t p) v -> p t v", p=128), w=[("Vh", b, q4)])

                steps = []
                for h in range(16):
                    for qsb in range(4):
                        d0 = NPT + 4 * qsb
                        for kbi in range(d0 + 4):
                            steps.append((h, qsb, kbi, d0))

                def emit_S(idx):
                    h, qsb, kbi, d0 = steps[idx]
                    b = h % 2
                    j0 = max(0, kbi - d0)
                    ncol = 512 - 128 * j0
                    si = idx % NSB
                    op("pe", lambda t: t.matmul(pss[si][:, 0:ncol], lhsT=KT[b][:, kbi * 128:(kbi + 1) * 128], rhs=QT[b][:, qsb * 512 + j0 * 128:(qsb + 1) * 512], start=True, stop=True), r=[("QT", b), ("KT", b)], w=[("pss", si)])

                def emit_rest(idx):
                    h, qsb, kbi, d0 = steps[idx]
                    b = h % 2
                    if qsb == 0 and kbi == 0 and h + 1 < 16:
                        load_head(h + 1)
                    vk = [("Vh", b, q4) for q4 in range(3)] + [("Vh1", b)]
                    j0 = max(0, kbi - d0)
                    ncol = 512 - 128 * j0
                    si = idx % NSB
                    oi = (h * 4 + qsb) % 2
                    op("act", lambda a: a.activation(out=pt[si][:, 0:ncol], in_=pss[si][:, 0:ncol], func=AF.Exp, bias=nbt[:, h, kbi:kbi + 1]), r=[("pss", si)], w=[("pt", si)])
                    if kbi >= d0:
                        op("dve", lambda g: g.tensor_tensor(out=pt[si][:, 0:128], in0=pt[si][:, 0:128], in1=mUi[:], op=ALU.mult), r=[("pt", si)], w=[("pt", si)])
                    op("pe", lambda t: t.matmul(pso[oi][0:65, j0 * 128:512], lhsT=Vh[b][:, kbi, :], rhs=pt[si][:, 0:ncol], start=(kbi == 0), stop=(kbi == d0 + 3)), r=[("pt", si)] + vk, w=[("pso", oi)])
                    if kbi == d0 + 3:
                        rb = recrs[oi]
                        op("dve", lambda v: v.reciprocal(out=rb[64:65, :], in_=pso[oi][64:65, :]), r=[("pso", oi)], w=[("recr", oi)])

                        def tail(h=h, qsb=qsb, oi=oi, rb=rb):
                            op("pe", lambda t: t.matmul(psr[0:64, :], lhsT=onesf[64:65, 0:64], rhs=rb[64:65, :], start=True, stop=True), r=[("recr", oi)], w=["psr"])
                            op("act", lambda a: a.copy(out=recb[:], in_=psr[0:64, :]), r=["psr"], w=["recb"])
                            op("dve", lambda v: v.tensor_tensor(out=ybo[oi][:], in0=pso[oi][0:64, :], in1=recb[:], op=ALU.mult), r=[("pso", oi), "recb"], w=[("ybo", oi)])
                            dma("sp", YBT[h * 64:(h + 1) * 64, qsb * 512:(qsb + 1) * 512], ybo[oi][:], r=[("ybo", oi)], w=[("YBT", h, qsb)])
                        deferred.append([8, tail])

                load_head(0)
                LOOK = 4
                for idx in range(min(LOOK, len(steps))):
                    emit_S(idx)
                for idx in range(len(steps)):
                    emit_rest(idx)
                    if idx + LOOK < len(steps):
                        emit_S(idx + LOOK)
                    for d_ in list(deferred):
                        d_[0] -= 1
                        if d_[0] <= 0:
                            d_[1]()
                            deferred.remove(d_)
                for d_ in deferred:
                    d_[1]()
                kb.barrier()

        MT = scr("MT", [D, TO], BF16)
        if "D" in STAGES:
            with ExitStack() as sd:
                T = lambda n, s, d=F32: sd.enter_context(SB(n, list(s), d))
                PS = lambda n, s, d=F32: sd.enter_context(PSM(n, list(s), d))
                wab = T("wab", [128, 8, D], BF16); wbb = T("wbb", [128, 8, D], BF16)
                wst = [T(f"wst{i}", [128, 2, D]) for i in range(2)]
                yaTs = T("yaTs", [128, 8, TO], BF16); ybTs = T("ybTs", [128, 8, TO], BF16)
                ybl = [T(f"ybl{i}", [128, RW], BF16) for i in range(2)]
                gat = [T(f"gat{i}", [128, 2, 512], BF16) for i in range(4)]
                m1 = [T(f"m1{i}", [128, 512]) for i in range(2)]; m2 = [T(f"m2{i}", [128, 512]) for i in range(2)]
                mo = [T(f"mo{i}", [128, 512], BF16) for i in range(2)]
                psa = [PS(f"psa{i}", [128, 512]) for i in range(2)]; psb_ = [PS(f"psb{i}", [128, 512]) for i in range(2)]
                pstr = [PS(f"pstr{i}", [128, 8, 128], BF16) for i in range(2)]
                n = 0
                for (wsrc, wdst, nm) in ((w_a, wab, "wab"), (w_b, wbb, "wbb")):
                    for i in range(4):
                        bb_ = n % 2; n += 1
                        dma("sp", wst[bb_][:], wsrc[i * 256:(i + 1) * 256, :].rearrange("(c p) n -> p c n", p=128), w=[("wst", bb_)])
                        if n % 2:
                            op("dve", lambda g: g.tensor_copy(out=wdst[:, 2 * i:2 * i + 2, :], in_=wst[bb_][:]), r=[("wst", bb_)], w=[(nm, i)])
                        else:
                            op("act", lambda g: g.copy(out=wdst[:, 2 * i:2 * i + 2, :], in_=wst[bb_][:]), r=[("wst", bb_)], w=[(nm, i)])
                dma("sp", yaTs[:], YAT.rearrange("(c p) t -> p c t", p=128), w=["yaTs"])
                dma("sp", ybTs[:], YBT.rearrange("(c p) t -> p c t", p=128), w=["ybTs"])
                wak = [("wab", i) for i in range(4)]; wbk = [("wbb", i) for i in range(4)]
                it = 0
                for dblk in range(16):
                    for tsb in range(4):
                        b = it % 2; gb = it % 4; it += 1
                        dma("sp", gat[gb][:, 0, :], GT[dblk * 128:(dblk + 1) * 128, tsb * 512:(tsb + 1) * 512], w=[("gat", gb, 0)])
                        dma("sp", gat[gb][:, 1, :], GT[2048 + dblk * 128:2048 + (dblk + 1) * 128, tsb * 512:(tsb + 1) * 512], w=[("gat", gb, 1)])
                        for kc in range(8):
                            op("pe", lambda t: t.matmul(psa[b][:, :], lhsT=wab[:, kc, dblk * 128:(dblk + 1) * 128], rhs=yaTs[:, kc, tsb * 512:(tsb + 1) * 512], start=(kc == 0), stop=(kc == 7)), r=wak + ["yaTs"], w=[("psa", b)])
                        for kc in range(8):
                            op("pe", lambda t: t.matmul(psb_[b][:, :], lhsT=wbb[:, kc, dblk * 128:(dblk + 1) * 128], rhs=ybTs[:, kc, tsb * 512:(tsb + 1) * 512], start=(kc == 0), stop=(kc == 7)), r=wbk + ["ybTs"], w=[("psb", b)])
                        op("dve", lambda v: v.tensor_tensor(out=m1[b][:], in0=psa[b][:, :], in1=gat[gb][:, 0, :], op=ALU.mult), r=[("psa", b), ("gat", gb, 0)], w=[("m1", b)])
                        op("dve", lambda v: v.tensor_tensor(out=m2[b][:], in0=psb_[b][:, :], in1=gat[gb][:, 1, :], op=ALU.mult), r=[("psb", b), ("gat", gb, 1)], w=[("m2", b)])
                        op("pool", lambda g: g.tensor_tensor(out=mo[b][:], in0=m1[b][:], in1=m2[b][:], op=ALU.add), r=[("m1", b), ("m2", b)], w=[("mo", b)])
                        dma("sp", MT[dblk * 128:(dblk + 1) * 128, tsb * 512:(tsb + 1) * 512], mo[b][:], r=[("mo", b)], w=[("MT", dblk, tsb)])
                kb.barrier()

            with ExitStack() as sd:
                T = lambda n, s, d=F32: sd.enter_context(SB(n, list(s), d))
                PS = lambda n, s, d=F32: sd.enter_context(PSM(n, list(s), d))
                wob = T("wob", [128, 16, D], BF16)
                wst = [T(f"wst{i}", [128, 2, D]) for i in range(2)]
                mTt = [T(f"mTt{i}", [128, 16, 128], BF16) for i in range(2)]
                xt = [T(f"xt{i}", [128, D]) for i in range(2)]
                h2t = [T(f"h2t{i}", [128, D]) for i in range(2)]
                g2t = T("g2t", [128, D]); xnb = T("xnb", [128, D], BF16); junk = T("junk", [128, D], BF16)
                st1 = T("st1", [128, 4]); xn2t = T("xn2t", [128, 16, 128], BF16)
                pso_ = [PS(f"pso{i}", [128, 512]) for i in range(4)]
                pstr = [PS(f"pstr{i}", [128, 8, 128], BF16) for i in range(2)]
                for i in range(8):
                    bb_ = i % 2
                    dma("sp", wst[bb_][:], w_o[i * 256:(i + 1) * 256, :].rearrange("(c p) n -> p c n", p=128), w=[("wst", bb_)])
                    if i % 2:
                        op("dve", lambda g: g.tensor_copy(out=wob[:, 2 * i:2 * i + 2, :], in_=wst[bb_][:]), r=[("wst", bb_)], w=[("wob", i)])
                    else:
                        op("act", lambda g: g.copy(out=wob[:, 2 * i:2 * i + 2, :], in_=wst[bb_][:]), r=[("wst", bb_)], w=[("wob", i)])
                wok = [("wob", i) for i in range(8)]
                dma("sp", g2t[:], g2bc[:, :], w=["g2t"])
                MT3 = MT.rearrange("(c p) t -> p c t", p=128)
                for ti in range(NOT_):
                    b = ti % 2
                    dma("sp", mTt[b][:], MT3[:, :, ti * 128:(ti + 1) * 128], w=[("mTt", b)])
                    dma("sp", xt[b][:], xall[TP + ti * 128:TP + (ti + 1) * 128, :], w=[("xt", b)])
                    for cb in range(4):
                        for kc in range(16):
                            op("pe", lambda t: t.matmul(pso_[cb][:, :], lhsT=mTt[b][:, kc, :], rhs=wob[:, kc, cb * 512:(cb + 1) * 512], start=(kc == 0), stop=(kc == 15)), r=[("mTt", b)] + wok, w=[("pso", cb)])
                        op("dve", lambda v: v.tensor_tensor(out=h2t[b][:, cb * 512:(cb + 1) * 512], in0=pso_[cb][:, :], in1=xt[b][:, cb * 512:(cb + 1) * 512], op=ALU.add), r=[("pso", cb), ("xt", b)], w=[("h2t", b, cb)])
                    hk = [("h2t", b, cb) for cb in range(4)]
                    dma("sp", H2[ti * 128:(ti + 1) * 128, :], h2t[b][:], r=hk, w=[("H2", ti)])
                    op("dve", lambda v: v.memset(st1[:, 0:1], 0.0), w=["ss"])
                    op("act", lambda a: a.activation(out=junk[:], in_=h2t[b][:], func=AF.Square, accum_out=st1[:, 0:1]), r=hk + ["ss"], w=["junk", "ss"])
                    op("act", lambda a: a.activation(out=st1[:, 1:2], in_=st1[:, 0:1], func=AF.Ln, bias=cst[:, 0:1], scale=1.0 / D), r=["ss"], w=["lnv"])
                    op("act", lambda a: a.activation(out=st1[:, 2:3], in_=st1[:, 1:2], func=AF.Exp, scale=-0.5), r=["lnv"], w=["rstd"])
                    op("dve", lambda v: v.scalar_tensor_tensor(out=xnb[:], in0=h2t[b][:], scalar=st1[:, 2:3], in1=g2t[:], op0=ALU.mult, op1=ALU.mult), r=hk + ["rstd", "g2t"], w=["xnb"])
                    for hh in range(2):
                        for c in range(8):
                            kc = hh * 8 + c
                            op("pe", lambda t: t.transpose(out=pstr[hh][:, c, :], in_=xnb[:, kc * 128:(kc + 1) * 128], identity=ident[:]), r=["xnb"], w=[("pstr", hh)])
                        op("act", lambda a: a.copy(out=xn2t[:, hh * 8:(hh + 1) * 8, :], in_=pstr[hh][:]), r=[("pstr", hh)], w=[("xn2t", hh)])
                    dma("sp", XN2T.rearrange("(c p) t -> p c t", p=128)[:, :, ti * 128:(ti + 1) * 128], xn2t[:], r=[("xn2t", 0), ("xn2t", 1)], w=[("XN2T", ti)])
                kb.barrier()

        if "E" in STAGES:
            with ExitStack() as se:
                T = lambda n, s, d=F32: se.enter_context(SB(n, list(s), d))
                PS = lambda n, s, d=F32: se.enter_context(PSM(n, list(s), d))
                xn2 = T("xn2", [128, 16, TO], BF16)
                wf = [T(f"wf{i}", [128, 16, 256]) for i in range(2)]
                wb = [T(f"wb{i}", [128, 16, 256], BF16) for i in range(2)]
                sg = [T(f"sg{i}", [128, 512]) for i in range(2)]
                ao = [T(f"ao{i}", [128, 512], BF16) for i in range(2)]
                psg = [PS(f"psg{i}", [128, 512]) for i in range(2)]; psu = [PS(f"psu{i}", [128, 512]) for i in range(2)]
                wst = [T(f"wst{i}", [128, 2, D]) for i in range(2)]
                wcv = [T(f"wcv{i}", [128, 2, D], BF16) for i in range(2)]

                def conv_wdn(i):
                    b = i % 2
                    dma("sp", wst[b][:], w_dn[i * 256:(i + 1) * 256, :].rearrange("(c p) n -> p c n", p=128), w=[("wst", b)])
                    op("dve", lambda v: v.tensor_copy(out=wcv[b][:], in_=wst[b][:]), r=[("wst", b)], w=[("wcv", b)])
                    dma("sp", WDNB[i * 256:(i + 1) * 256, :].rearrange("(c p) n -> p c n", p=128), wcv[b][:], r=[("wcv", b)], w=[("WDNB", i)])
                XN3 = XN2T.rearrange("(c p) t -> p c t", p=128)
                for q4 in range(4):
                    dma("sp", xn2[:, q4 * 4:(q4 + 1) * 4, :], XN3[:, q4 * 4:(q4 + 1) * 4, :], w=[("xn2", q4)])
                xk = [("xn2", q4) for q4 in range(4)]

                def load_gu(fb):
                    b = fb % 2
                    for q4 in range(4):
                        dma("sp", wf[b][:, q4 * 4:(q4 + 1) * 4, 0:128], w_gu[q4 * 512:(q4 + 1) * 512, fb * 128:(fb + 1) * 128].rearrange("(kc p) n -> p kc n", p=128), w=[("wf", b, q4, 0)])
                        dma("sp", wf[b][:, q4 * 4:(q4 + 1) * 4, 128:256], w_gu[q4 * 512:(q4 + 1) * 512, DFF + fb * 128:DFF + (fb + 1) * 128].rearrange("(kc p) n -> p kc n", p=128), w=[("wf", b, q4, 1)])
                    for q4 in range(4):
                        op("pool", lambda g: g.tensor_copy(out=wb[b][:, q4 * 4:(q4 + 1) * 4, :], in_=wf[b][:, q4 * 4:(q4 + 1) * 4, :]), r=[("wf", b, q4, 0), ("wf", b, q4, 1)], w=[("wb", b, q4)])

                load_gu(0)
                it = 0
                for fb in range(DFF // 128):
                    b = fb % 2
                    if fb + 1 < DFF // 128:
                        load_gu(fb + 1)
                    if fb % 2 == 0:
                        conv_wdn(fb // 2)
                    for tsb in range(4):
                        pi = it % 2; it += 1
                        for kc in range(16):
                            op("pe", lambda t: t.matmul(psg[pi][:, :], lhsT=wb[b][:, kc, 0:128], rhs=xn2[:, kc, tsb * 512:(tsb + 1) * 512], start=(kc == 0), stop=(kc == 15)), r=[("wb", b, kc // 4)] + xk, w=[("psg", pi)])
                        for kc in range(16):
                            op("pe", lambda t: t.matmul(psu[pi][:, :], lhsT=wb[b][:, kc, 128:256], rhs=xn2[:, kc, tsb * 512:(tsb + 1) * 512], start=(kc == 0), stop=(kc == 15)), r=[("wb", b, kc // 4)] + xk, w=[("psu", pi)])
                        op("act", lambda a: a.activation(out=sg[pi][:], in_=psg[pi][:, :], func=AF.Silu), r=[("psg", pi)], w=[("sg", pi)])
                        op("dve", lambda v: v.tensor_tensor(out=ao[pi][:], in0=sg[pi][:], in1=psu[pi][:, :], op=ALU.mult), r=[("sg", pi), ("psu", pi)], w=[("ao", pi)])
                        dma("sp", ACTT[fb * 128:(fb + 1) * 128, tsb * 512:(tsb + 1) * 512], ao[pi][:], r=[("ao", pi)], w=[("ACTT", fb, tsb)])
                kb.barrier()

        if "F" in STAGES:
            with ExitStack() as sf:
                T = lambda n, s, d=F32: sf.enter_context(SB(n, list(s), d))
                PS = lambda n, s, d=F32: sf.enter_context(PSM(n, list(s), d))
                NC_ = DFF // 128
                acts = T("acts", [128, NC_, 1024], BF16)
                wdb = [T(f"wdb{i}", [128, NC_, 512], BF16) for i in range(2)]
                h2s = [T(f"h2s{i}", [128, 512]) for i in range(4)]
                ot = [T(f"ot{i}", [128, 512]) for i in range(4)]
                psd = [PS(f"psd{i}", [128, 512]) for i in range(4)]
                AC3 = ACTT.rearrange("(c p) t -> p c t", p=128)
                WD3 = WDNB.rearrange("(c p) n -> p c n", p=128)
                it = 0
                def load_wd(cb, b):
                    for q4 in range(4):
                        dma("sp", wdb[b][:, q4 * 11:(q4 + 1) * 11, :], WD3[:, q4 * 11:(q4 + 1) * 11, cb * 512:(cb + 1) * 512], w=[("wdb", b, q4)])
                seq = [(th, cb) for th in range(2) for cb in range(4)]
                load_wd(seq[0][1], 0)
                for si, (th, cb) in enumerate(seq):
                    b = si % 2
                    if cb == 0:
                        for q4 in range(4):
                            dma("act", acts[:, q4 * 11:(q4 + 1) * 11, :], AC3[:, q4 * 11:(q4 + 1) * 11, th * 1024:(th + 1) * 1024], w=[("acts", q4)])
                    if si + 1 < len(seq):
                        load_wd(seq[si + 1][1], (si + 1) % 2)
                    for j in range(8):
                        pi = it % 4; it += 1
                        tok0 = th * 1024 + j * 128
                        dma("act", h2s[pi][:], H2[tok0:tok0 + 128, cb * 512:(cb + 1) * 512], w=[("h2s", pi)])
                        for c in range(NC_):
                            op("pe", lambda t: t.matmul(psd[pi][:, :], lhsT=acts[:, c, j * 128:(j + 1) * 128], rhs=wdb[b][:, c, :], start=(c == 0), stop=(c == NC_ - 1)), r=[("acts", c // 11), ("wdb", b, c // 11)], w=[("psd", pi)])
                        op("dve", lambda v: v.tensor_tensor(out=ot[pi][:], in0=psd[pi][:, :], in1=h2s[pi][:], op=ALU.add), r=[("psd", pi), ("h2s", pi)], w=[("ot", pi)])
                        dma("act", out[tok0:tok0 + 128, cb * 512:(cb + 1) * 512], ot[pi][:], r=[("ot", pi)], w=[("out", tok0, cb)])
                kb.barrier()

        kb.barrier()
    return nc


def prep_core_inputs(inp, core):
    b, half = core // 2, core % 2
    x = np.asarray(inp["x"], dtype=np.float32)
    meta = np.asarray(inp["meta_tokens"], dtype=np.float32)
    xall = np.zeros((TA, D), np.float32)
    keypen = np.zeros((128, NT), np.float32)
    if half == 0:
        xall[TP - 16:TP] = meta
        xall[TP:] = x[b, 0:2048]
        ndummy = TP - 16
    else:
        xall[112:128] = meta
        xall[128:TP] = x[b, 0:2048]
        xall[TP:] = x[b, 2048:4096]
        ndummy = 112
    kp = np.zeros(TA, np.float32)
    kp[:ndummy] = -30000.0
    keypen[:, :] = kp.reshape(NT, 128).T
    f = lambda k: np.asarray(inp[k], dtype=np.float32)[0]
    bc = lambda v, n=128: np.ascontiguousarray(np.broadcast_to(v[None, :], (n, v.shape[0])))
    mu = f("rwkv_mu")
    mu24 = np.ascontiguousarray(mu[0:3072].reshape(24, 128).T)
    mul4 = np.zeros((128, 4), np.float32)
    mul4[0:96, 0] = mu[3072:3168]
    mul4[0:96, 1] = mu[3168:3264]
    mul4[:, 2] = mu[3264:3392]
    mul4[:, 3] = mu[3392:3520]
    pp = lambda v: np.ascontiguousarray(v.reshape(8, 128).T)
    vecp = np.stack([pp(f("rwkv_a0")), pp(f("rwkv_k_k")), pp(f("rwkv_k_a")), pp(f("rwkv_r_k").reshape(-1))], axis=1)
    d = {
        "xall": xall, "keypen": keypen,
        "w_in": f("w_in"), "w_a": f("w_branch_a"), "w_b": f("w_branch_b"), "w_o": f("w_o"),
        "w_gu": f("w_gate_up"), "w_dn": f("w_down"),
        "g1bc": bc(f("norm1_g")), "g2bc": bc(f("norm2_g")),
        "mu24": mu24, "mul4": mul4, "w0bc": bc(f("rwkv_w0")),
        "w2": f("rwkv_w2"), "a2": f("rwkv_a2"), "g2": f("rwkv_g2"),
        "vecp": np.ascontiguousarray(vecp),
        "gnwbc": bc(f("rwkv_gn_w")), "gnbbc": bc(f("rwkv_gn_b")),
        "qgbc": bc(np.tile(f("fox_q_norm_g"), 16)), "kgbc": bc(np.tile(f("fox_k_norm_g"), 16)),
        "fbbc": bc(f("fox_f_bias")),
    }
    return d


_NC_CACHE = {}


def kernel(**inputs):
    if "nc" not in _NC_CACHE:
        _NC_CACHE["nc"] = build_program()
    nc = _NC_CACHE["nc"]
    shared = None
    in_maps = []
    ncores = int(os.environ.get("KCORES", "8"))
    for c in range(ncores):
        in_maps.append(prep_core_inputs(inputs, c))
    res = run_bass_kernel_spmd(nc, in_maps, core_ids=list(range(ncores)))
    B = inputs["x"].shape[0]
    outp = np.zeros((B, 4096, D), np.float32)
    for c in range(ncores):
        b, half = c // 2, c % 2
        outp[b, half * 2048:(half + 1) * 2048] = res.results[c]["out"]
    if DEBUG:
        kernel.last = res
    return outp
```

```python
import os
import numpy as np
from contextlib import ExitStack
import concourse.bass as bass
import concourse.mybir as mybir
from concourse.bass_utils import run_bass_kernel_spmd

F32 = mybir.dt.float32
BF16 = mybir.dt.bfloat16
AF = mybir.ActivationFunctionType
ALU = mybir.AluOpType
AX = mybir.AxisListType

D = 2048
NPT, NOT_, NT = 17, 16, 33
TP, TO, TA = NPT * 128, NOT_ * 128, NT * 128
RW = 1024
RC = 3520
FC = 3088
NIN = 10704
DFF = 5632
DECAY_C = 0.6065306597126334
DEBUG = os.environ.get("KDEBUG", "") != ""
STAGES = os.environ.get("KSTAGES", "ABCDEF")


class KB:
    def __init__(self, nc, es):
        self.nc = nc
        self.eng = {"pe": nc.tensor, "act": nc.scalar, "dve": nc.vector, "pool": nc.gpsimd, "sp": nc.sync}
        self.sem = {e: es.enter_context(nc.semaphore("s_" + e)) for e in ("pe", "act", "dve", "pool")}
        self.cnt = {e: 0 for e in self.sem}
        self.seen = {}
        self.nd = 12
        self.dsem = {q: [es.enter_context(nc.semaphore(f"d_{q}{i}")) for i in range(self.nd)] for q in ("sp", "pool", "act")}
        self.dcnt = {q: 0 for q in self.dsem}
        self.dlast = {q: [None] * self.nd for q in self.dsem}
        self.res = {}

    def ensure(self, e, tok):
        if tok is None:
            return
        if tok[0] == "c":
            _, pe_, idx = tok
            if pe_ == e and e == "pe":
                return
            k = (e, pe_)
            if self.seen.get(k, 0) >= idx:
                return
            self.eng[e].wait_ge(self.sem[pe_], idx)
            self.seen[k] = idx
        else:
            _, q, slot, val = tok
            k = (e, q, slot)
            if self.seen.get(k, 0) >= val:
                return
            self.eng[e].wait_ge(self.dsem[q][slot], val)
            self.seen[k] = val

    def _deps(self, e, r, w):
        for k in r:
            st = self.res.get(k)
            if st:
                self.ensure(e, st["w"])
        for k in w:
            st = self.res.get(k)
            if st:
                self.ensure(e, st["w"])
                for t in st["r"].values():
                    self.ensure(e, t)

    def _upd(self, tok, rk, r, w):
        for k in w:
            self.res[k] = {"w": tok, "r": {}}
        for k in r:
            st = self.res.setdefault(k, {"w": None, "r": {}})
            st["r"][rk] = tok

    def op(self, e, fn, r=(), w=()):
        self._deps(e, r, w)
        inst = fn(self.eng[e])
        self.cnt[e] += 1
        inst.then_inc(self.sem[e], 1)
        tok = ("c", e, self.cnt[e])
        self._upd(tok, e, r, w)
        return tok

    def dma(self, q, out, in_, r=(), w=(), slow=False):
        self._deps(q, r, w)
        n = self.dcnt[q]
        slot = n % self.nd
        self.ensure(q, self.dlast[q][slot])
        val = 16 * (n // self.nd + 1)
        if slow:
            self.eng[q].dma_start(out=out, in_=in_, allow_slow_non_contiguous=True).then_inc(self.dsem[q][slot], 16)
        else:
            self.eng[q].dma_start(out=out, in_=in_).then_inc(self.dsem[q][slot], 16)
        tok = ("d", q, slot, val)
        self.dlast[q][slot] = tok
        self.dcnt[q] = n + 1
        self._upd(tok, ("d", q, slot), r, w)
        return tok

    def barrier(self):
        for e in ("pe", "act", "dve", "pool", "sp"):
            for c in self.cnt:
                if c != e and self.cnt[c] > 0:
                    self.ensure(e, ("c", c, self.cnt[c]))
            if e in self.cnt and e != "pe" and self.cnt[e] > 0:
                self.ensure(e, ("c", e, self.cnt[e]))
            for q in self.dsem:
                for t in self.dlast[q]:
                    self.ensure(e, t)
        self.res = {}


def bcast_last(ap, n):
    dims = [list(d) for d in ap.ap]
    if dims[-1][1] == 1 and len(dims) == 3:
        dims = dims[:2]
    return bass.AP(ap.tensor, ap.offset, dims + [[0, n]])


def build_program():
    nc = bass.Bass("TRN2", target_bir_lowering=False)
    _u = {"n": 0}
    def _un(n):
        _u["n"] += 1
        return "%s_%d" % (n, _u["n"])
    SB = lambda n, s, d: nc.sbuf_tensor(_un(n), s, d)
    PSM = lambda n, s, d: nc.psum_tensor(_un(n), s, d)
    dt_in = lambda n, s, d=F32: nc.dram_tensor(n, list(s), d, kind="ExternalInput").ap()
    okind = "ExternalOutput" if DEBUG else None
    DBGOUT = os.environ.get("KDBGOUT", "").split(",")
    def scr(n, s, d):
        if DEBUG and n in DBGOUT:
            return nc.dram_tensor(n, list(s), d, kind="ExternalOutput").ap()
        return nc.dram_tensor(n, list(s), d).ap()

    xall = dt_in("xall", [TA, D])
    keypen = dt_in("keypen", [128, NT])
    w_in = dt_in("w_in", [D, NIN])
    w_a = dt_in("w_a", [RW, D])
    w_b = dt_in("w_b", [RW, D])
    w_o = dt_in("w_o", [D, D])
    w_gu = dt_in("w_gu", [D, 2 * DFF])
    w_dn = dt_in("w_dn", [DFF, D])
    g1bc = dt_in("g1bc", [128, D])
    g2bc = dt_in("g2bc", [128, D])
    mu24 = dt_in("mu24", [128, 24])
    mul4 = dt_in("mul4", [128, 4])
    w0bc = dt_in("w0bc", [128, RW])
    w2 = dt_in("w2", [96, RW])
    a2 = dt_in("a2", [96, RW])
    g2 = dt_in("g2", [256, RW])
    vecp = dt_in("vecp", [128, 4, 8])
    gnwbc = dt_in("gnwbc", [128, RW])
    gnbbc = dt_in("gnbbc", [128, RW])
    qgbc = dt_in("qgbc", [128, RW])
    kgbc = dt_in("kgbc", [128, RW])
    fbbc = dt_in("fbbc", [128, 16])
    out = nc.dram_tensor("out", [TO, D], F32, kind="ExternalOutput").ap()

    ZR = scr("ZR", [3584, TA + 1], F32)
    ZFX = scr("ZFX", [TA, FC], F32)
    GT = scr("GT", [4096, TO], BF16)
    YAT = scr("YAT", [RW, TO], BF16)
    QTA = scr("QTA", [16, 67, TO], BF16)
    KTA = scr("KTA", [16, 67, TA], BF16)
    VFX = scr("VFX", [TA, RW], BF16)
    NB = scr("NB", [16, 128, NT], F32)
    YBT = scr("YBT", [RW, TO], BF16)
    H2 = scr("H2", [TO, D], F32)
    XN2T = scr("XN2T", [D, TO], BF16)
    ACTT = scr("ACTT", [DFF, TO], BF16)
    WDNB = scr("WDNB", [DFF, D], BF16)

    with ExitStack() as es:
        kb = KB(nc, es)
        es.enter_context(nc.Block())
        op, dma = kb.op, kb.dma

        ident = es.enter_context(SB("ident", [128, 128], BF16))
        identf = es.enter_context(SB("identf", [128, 128], F32))
        onesf = es.enter_context(SB("onesf", [128, 128], F32))
        zerof = es.enter_context(SB("zerof", [128, 32], F32))
        cst = es.enter_context(SB("cst", [128, 8], F32))
        mUs = es.enter_context(SB("mUs", [128, 128], F32))
        mUi = es.enter_context(SB("mUi", [128, 128], F32))
        mLs = es.enter_context(SB("mLs", [128, 128], F32))
        op("pool", lambda g: g.memset(identf[:], 1.0), w=["identf"])
        op("pool", lambda g: g.affine_select(out=identf[:], in_=identf[:], pattern=[[-1, 128]], compare_op=ALU.is_equal, fill=0.0, base=0, channel_multiplier=1), r=["identf"], w=["identf"])
        op("pool", lambda g: g.memset(onesf[:], 1.0), w=["onesf"])
        op("pool", lambda g: g.memset(zerof[:], 0.0), w=["zerof"])
        op("pool", lambda g: g.affine_select(out=mUs[:], in_=onesf[:], pattern=[[1, 128]], compare_op=ALU.is_gt, fill=0.0, base=0, channel_multiplier=-1), r=["onesf"], w=["mUs"])
        op("pool", lambda g: g.affine_select(out=mUi[:], in_=onesf[:], pattern=[[1, 128]], compare_op=ALU.is_ge, fill=0.0, base=0, channel_multiplier=-1), r=["onesf"], w=["mUi"])
        op("pool", lambda g: g.affine_select(out=mLs[:], in_=onesf[:], pattern=[[-1, 128]], compare_op=ALU.is_gt, fill=0.0, base=0, channel_multiplier=1), r=["onesf"], w=["mLs"])
        op("dve", lambda v: v.tensor_copy(out=ident[:], in_=identf[:]), r=["identf"], w=["ident"])
        for j, val in enumerate((1e-6, 64e-5, 1e-24, 1.0)):
            op("pool", lambda g, j=j, val=val: g.memset(cst[:, j:j + 1], val), w=[("cst", j)])
        kb.barrier()

        if "A" in STAGES:
            with ExitStack() as sa:
                xnT = sa.enter_context(SB("xnT", [128, 16, TP], BF16))
                g1t = sa.enter_context(SB("g1t", [128, D], F32))
                xt = [sa.enter_context(SB(f"xt{i}", [128, D], F32)) for i in range(2)]
                xnbs = [sa.enter_context(SB(f"xnb{i}", [128, D], BF16)) for i in range(2)]
                junk = sa.enter_context(SB("junk", [128, D], BF16))
                st1s = [sa.enter_context(SB(f"st1{i}", [128, 4], F32)) for i in range(2)]
                wf = [sa.enter_context(SB(f"wf{i}", [128, 16, 256], F32)) for i in range(2)]
                wb = [sa.enter_context(SB(f"wb{i}", [128, 16, 256], BF16)) for i in range(2)]
                og = [sa.enter_context(SB(f"og{i}", [128, 512], F32)) for i in range(4)]
                ogb = [sa.enter_context(SB(f"ogb{i}", [128, 512], BF16)) for i in range(2)]
                pst = [sa.enter_context(PSM(f"pst{i}", [128, 8, 128], BF16)) for i in range(2)]
                psg = [sa.enter_context(PSM(f"psg{i}", [128, 512], F32)) for i in range(4)]

                dma("sp", g1t[:], g1bc[:, :], w=["g1t"])
                dma("sp", ZR[:, 0:1].rearrange("(b p) o -> p b o", p=128), zerof[:, 0:28].rearrange("p (b o) -> p b o", o=1), r=["zerof"], w=[("ZRc0",)], slow=True)

                def norm_pass(tok0, ntiles):
                    for i in range(ntiles):
                        b = i % 2
                        xnb = xnbs[b]; st1 = st1s[b]
                        dma("sp", xt[b][:], xall[tok0 + i * 128: tok0 + (i + 1) * 128, :], w=[("xt", b)])
                        op("dve", lambda v: v.memset(st1[:, 0:1], 0.0), w=[("ss", b)])
                        op("act", lambda a: a.activation(out=junk[:], in_=xt[b][:], func=AF.Square, accum_out=st1[:, 0:1]), r=[("xt", b), ("ss", b)], w=[("ss", b)])
                        op("act", lambda a: a.activation(out=st1[:, 1:2], in_=st1[:, 0:1], func=AF.Ln, bias=cst[:, 0:1], scale=1.0 / D), r=[("ss", b)], w=[("lnv", b)])
                        op("act", lambda a: a.activation(out=st1[:, 2:3], in_=st1[:, 1:2], func=AF.Exp, scale=-0.5), r=[("lnv", b)], w=[("rstd", b)])
                        op("dve", lambda v: v.scalar_tensor_tensor(out=xnb[:], in0=xt[b][:], scalar=st1[:, 2:3], in1=g1t[:], op0=ALU.mult, op1=ALU.mult), r=[("xt", b), ("rstd", b), "g1t"], w=[("xnb", b)])
                        for hh in range(2):
                            for c in range(8):
                                kc = hh * 8 + c
                                op("pe", lambda t: t.transpose(out=pst[hh][:, c, :], in_=xnb[:, kc * 128:(kc + 1) * 128], identity=ident[:]), r=[("xnb", b), "ident"], w=[("pst", hh)])
                            e = "act" if hh == 0 else "dve"
                            if e == "act":
                                op("act", lambda a: a.copy(out=xnT[:, hh * 8:(hh + 1) * 8, i * 128:(i + 1) * 128], in_=pst[hh][:]), r=[("pst", hh)], w=[("xnT", i)])
                            else:
                                op("dve", lambda v: v.tensor_copy(out=xnT[:, hh * 8:(hh + 1) * 8, i * 128:(i + 1) * 128], in_=pst[hh][:]), r=[("pst", hh)], w=[("xnT", i)])

                state = {"ld": 0, "pg": 0, "og": 0, "ogb": 0, "ev": 0}

                def load_w(c0, w):
                    b = state["ld"] % 2
                    state["ld"] += 1
                    for q4 in range(4):
                        dma("sp", wf[b][:, q4 * 4:(q4 + 1) * 4, 0:w], w_in[q4 * 512:(q4 + 1) * 512, c0:c0 + w].rearrange("(kc p) n -> p kc n", p=128), w=[("wf", b, q4)])
                    for q4 in range(4):
                        op("pool", lambda g: g.tensor_copy(out=wb[b][:, q4 * 4:(q4 + 1) * 4, 0:w], in_=wf[b][:, q4 * 4:(q4 + 1) * 4, 0:w]), r=[("wf", b, q4)], w=[("wb", b, q4)])
                    return b

                def evac_copy(dst, src, rk, wk):
                    state["ev"] += 1
                    if state["ev"] % 2:
                        op("act", lambda a: a.copy(out=dst, in_=src), r=rk, w=wk)
                    else:
                        op("dve", lambda v: v.tensor_copy(out=dst, in_=src), r=rk, w=wk)

                def fm_job(b, c0, w, t0, t1, ntile_used, kind):
                    for sub in range(0, w, 128):
                        m = min(128, w - sub)
                        for ts in range(t0, t1, 512):
                            n = min(512, t1 - ts)
                            pi = state["pg"] % 4
                            state["pg"] += 1
                            for kc in range(16):
                                op("pe", lambda t: t.matmul(psg[pi][0:m, 0:n], lhsT=wb[b][:, kc, sub:sub + m], rhs=xnT[:, kc, ts:ts + n], start=(kc == 0), stop=(kc == 15)),
                                   r=[("wb", b, kc // 4)] + [("xnT", ti) for ti in range(ts // 128, (ts + n + 127) // 128)], w=[("psg", pi)])
                            col = c0 + sub
                            if kind == "rw":
                                oi = state["og"] % 4
                                state["og"] += 1
                                evac_copy(og[oi][0:m, 0:n], psg[pi][0:m, 0:n], [("psg", pi)], [("og", oi)])
                                dma("sp", ZR[col:col + m, 1 + ntile_used + ts: 1 + ntile_used + ts + n], og[oi][0:m, 0:n], r=[("og", oi)], w=[("ZR", col, ntile_used + ts)])
                            else:
                                oi = state["ogb"] % 2
                                state["ogb"] += 1
                                op("act", lambda a: a.activation(out=ogb[oi][0:m, 0:n], in_=psg[pi][0:m, 0:n], func=AF.Sigmoid), r=[("psg", pi)], w=[("ogb", oi)])
                                dma("sp", GT[col - 6608:col - 6608 + m, ts:ts + n], ogb[oi][0:m, 0:n], r=[("ogb", oi)], w=[("GT", col, ts)])

                def tm_job(b, c0, w, ntiles, tokbase):
                    for i in range(ntiles):
                        pi = state["pg"] % 4
                        state["pg"] += 1
                        for kc in range(16):
                            op("pe", lambda t: t.matmul(psg[pi][:, 0:w], lhsT=xnT[:, kc, i * 128:(i + 1) * 128], rhs=wb[b][:, kc, 0:w], start=(kc == 0), stop=(kc == 15)),
                               r=[("wb", b, kc // 4), ("xnT", i)], w=[("psg", pi)])
                        oi = state["og"] % 4
                        state["og"] += 1
                        evac_copy(og[oi][:, 0:w], psg[pi][:, 0:w], [("psg", pi)], [("og", oi)])
                        dma("sp", ZFX[tokbase + i * 128: tokbase + (i + 1) * 128, c0 - RC:c0 - RC + w], og[oi][:, 0:w], r=[("og", oi)], w=[("ZFX", c0, tokbase + i)])

                def run_jobs(jobs):
                    nxt = load_w(jobs[0][0], jobs[0][1])
                    for ji, (c0, w, fn) in enumerate(jobs):
                        cur = nxt
                        if ji + 1 < len(jobs):
                            nxt = load_w(jobs[ji + 1][0], jobs[ji + 1][1])
                        fn(cur)

                def col_blocks(lo, hi):
                    return [(c, min(256, hi - c)) for c in range(lo, hi, 256)]

                norm_pass(0, NPT)
                jobs = []
                for (c0, w) in col_blocks(1024, 3264):
                    jobs.append((c0, w, lambda b, c0=c0, w=w: fm_job(b, c0, w, 0, TP, 0, "rw")))
                for (c0, w) in col_blocks(0, 1024) + col_blocks(3264, 3520):
                    jobs.append((c0, w, lambda b, c0=c0, w=w: fm_job(b, c0, w, TP - 128, TP, 0, "rw")))
                for (c0, w) in col_blocks(4544, 6608):
                    jobs.append((c0, w, lambda b, c0=c0, w=w: tm_job(b, c0, w, NPT, 0)))
                run_jobs(jobs)
                norm_pass(TP, NOT_)
                jobs = []
                for (c0, w) in col_blocks(0, 3520):
                    jobs.append((c0, w, lambda b, c0=c0, w=w: fm_job(b, c0, w, 0, TO, TP, "rw")))
                for (c0, w) in col_blocks(3520, 6608):
                    jobs.append((c0, w, lambda b, c0=c0, w=w: tm_job(b, c0, w, NOT_, TP)))
                for (c0, w) in col_blocks(6608, NIN):
                    jobs.append((c0, w, lambda b, c0=c0, w=w: fm_job(b, c0, w, 0, TO, TP, "gt")))
                run_jobs(jobs)
                kb.barrier()

        if "B" in STAGES:
            with ExitStack() as sb:
                T = lambda n, s, d=F32: sb.enter_context(SB(n, list(s), d))
                PS = lambda n, s, d=F32: sb.enter_context(PSM(n, list(s), d))
                mask4 = T("mask4", [128, 512])
                maskL4 = T("maskL4", [128, 512])
                identb4 = T("identb4", [128, 512], BF16)
                tri2 = T("tri2", [128, 256])
                bones = T("bones", [128, 128], BF16)
                hsel = T("hsel", [128, 2], BF16)
                mu24t = T("mu24t", [128, 24]); mul4t = T("mul4t", [128, 4])
                w0t = T("w0t", [128, RW]); vp = T("vp", [128, 4, 8])
                gnwt = T("gnwt", [128, RW]); gnbt = T("gnbt", [128, RW])
                w2b = T("w2b", [96, RW], BF16); a2b = T("a2b", [96, RW], BF16); g2b = T("g2b", [128, 2, RW], BF16)
                zin = T("zin", [128, 24, 129]); zl = T("zl", [128, 4, 129])
                zs = T("zs", [128, 24, 128]); zls = T("zls", [128, 4, 128])
                lt2 = [T(f"lt{i}", [128, 4, 128], BF16) for i in range(2)]
                gam2 = [T(f"gam{i}", [128, 8]) for i in range(2)]
                sgT = T("sgT", [128, RW])
                e1 = T("e1", [128, 8, 128]); e2 = T("e2", [128, 8, 128]); e3 = T("e3", [128, 8, 128])
                alpha = T("alpha", [128, 8, 128])
                kkr = T("kkr", [128, 8, 128]); sqb = T("sqb", [128, 8, 128], BF16)
                rn = T("rn", [128, 8, 128]); kk = T("kk", [128, 8, 128])
                t1 = T("t1", [128, 8, 128]); kf = T("kf", [128, 8, 128]); bb = T("bb", [128, 8, 128])
                AR2 = [T(f"AR{i}", [128, 8, 2, 128], BF16) for i in range(2)]
                KB2 = T("KB2", [128, 8, 2, 128], BF16)
                prod2 = [T(f"prod{i}", [128, 8, 128], BF16) for i in range(2)]
                vsb = T("vsb", [128, 8, 128], BF16)
                Vtok2 = [T(f"Vtok{i}", [128, RW], BF16) for i in range(2)]
                KBtok2 = [T(f"KBtok{i}", [128, 2, RW], BF16) for i in range(2)]
                A4 = T("A4", [128, 16, 512], BF16)
                Nj = [T(f"Nj{i}", [128, 16, 128], BF16) for i in range(2)]
                Mj = [T(f"Mj{i}", [128, 16, 128], BF16) for i in range(2)]
                Qj = [T(f"Qj{i}", [128, 16, 128], BF16) for i in range(2)]
                Hf = T("Hf", [128, 8, 64]); Hb = T("Hb", [128, 8, 64], BF16)
                Xb = T("Xb", [128, RW], BF16); Ub = T("Ub", [128, RW], BF16)
                ysb = T("ysb", [128, 16, 64]); ysq = T("ysq", [128, 16, 64])
                gst = T("gst", [128, 6, 16])
                bon = T("bon", [128, 16])
                yab = T("yab", [128, RW], BF16)
                yaT = T("yaT", [128, 8, 128], BF16)
                psA = [PS(f"psA{i}", [128, 512]) for i in range(4)]
                psB = [PS(f"psB{i}", [128, 512]) for i in range(2)]
                psTb = [PS(f"psTb{i}", [128, 8, 128], BF16) for i in range(2)]

                for j in range(4):
                    src = mUs if j % 2 == 0 else mUi
                    op("dve", lambda v: v.tensor_copy(out=mask4[:, j * 128:(j + 1) * 128], in_=src[:]), w=[("mask4", j)])
                    op("dve", lambda v: v.tensor_copy(out=maskL4[:, j * 128:(j + 1) * 128], in_=mLs[:]), w=[("maskL4", j)])
                    op("dve", lambda v: v.tensor_copy(out=identb4[:, j * 128:(j + 1) * 128], in_=identf[:]), w=[("identb4", j)])
                op("dve", lambda v: v.tensor_scalar(out=tri2[:, 0:128], in0=mUi[:], scalar1=-DECAY_C, scalar2=None, op0=ALU.mult), w=[("tri2", 0)])
                op("dve", lambda v: v.tensor_scalar(out=tri2[:, 128:256], in0=mUs[:], scalar1=-DECAY_C, scalar2=None, op0=ALU.mult), w=[("tri2", 1)])
                op("pool", lambda g: g.memset(bones[:], 0.0), w=["bones"])
                op("pool", lambda g: g.memset(bones[0:64, 0:64], 1.0), r=["bones"], w=["bones"])
                op("pool", lambda g: g.memset(bones[64:128, 64:128], 1.0), r=["bones"], w=["bones"])
                op("pool", lambda g: g.memset(hsel[:], 0.0), w=["hsel"])
                op("pool", lambda g: g.memset(hsel[0:64, 0:1], 1.0), r=["hsel"], w=["hsel"])
                op("pool", lambda g: g.memset(hsel[64:128, 1:2], 1.0), r=["hsel"], w=["hsel"])
                dma("sp", mu24t[:], mu24[:, :], w=["mu24t"]); dma("sp", mul4t[:], mul4[:, :], w=["mul4t"])
                dma("sp", w0t[:], w0bc[:, :], w=["w0t"]); dma("sp", vp[:], vecp[:, :, :], w=["vp"])
                dma("sp", gnwt[:], gnwbc[:, :], w=["gnwt"]); dma("sp", gnbt[:], gnbbc[:, :], w=["gnbt"])
                ldf = zs[:, 0:16, :].rearrange("p (c a) t -> p c (a t)", c=2)
                dma("sp", ldf[0:96, 0, :], w2[:, :], w=[("zs", 0), ("zs", 8)])
                op("dve", lambda v: v.tensor_copy(out=w2b[:], in_=ldf[0:96, 0, :]), r=[("zs", 0), ("zs", 8)], w=["w2b"])
                dma("sp", ldf[0:96, 1, :], a2[:, :], r=[("zs", 0), ("zs", 8)], w=[("zs", 0), ("zs", 8)])
                op("dve", lambda v: v.tensor_copy(out=a2b[:], in_=ldf[0:96, 1, :]), r=[("zs", 0), ("zs", 8)], w=["a2b"])
                dma("sp", ldf[:, :, :], g2.rearrange("(c p) n -> p c n", p=128), r=[("zs", 0), ("zs", 8)], w=[("zs", 0), ("zs", 8)])
                op("dve", lambda v: v.tensor_copy(out=g2b[:], in_=ldf[:, :, :]), r=[("zs", 0), ("zs", 8)], w=["g2b"])
                for i in range(2):
                    op("pool", lambda g: g.memset(AR2[i][:], 0.0), w=[("ARa", i), ("ARr", i)])
                op("dve", lambda v: v.memset(Hf[:], 0.0), w=["Hf"])
                op("dve", lambda v: v.memset(Hb[:], 0.0), w=["Hb"])
                ZR3 = ZR.rearrange("(b p) t -> p b t", p=128)

                BT = [int(t) for t in os.environ.get("KBTILES", "").split(",") if t] or list(range(NT))
                hb = lambda h: (h // 2, 64 * (h % 2))
                ek = [("e1", b_) for b_ in range(8)]; e2k = [("e2", b_) for b_ in range(8)]; e3k = [("e3", b_) for b_ in range(8)]
                alk = [("alpha", b_) for b_ in range(8)]
                kkrk = [("kkr", b_) for b_ in range(8)]
                t1k = [("t1", b_) for b_ in range(8)]

                def prep(ti, p):
                    own = ti >= NPT
                    AR, lt, gam, prod, Vtok, KBtok = AR2[p], lt2[p], gam2[p], prod2[p], Vtok2[p], KBtok2[p]
                    c0 = ti * 128
                    blo = 0 if own else 8
                    dma("sp", zin[:, blo:24, :], ZR3[:, blo:24, c0:c0 + 129], w=["zin"])
                    nl = 4 if own else 2
                    dma("sp", zl[0:96, 0, :], ZR[3072:3168, c0:c0 + 129], w=["zl0"])
                    dma("sp", zl[0:96, 1, :], ZR[3168:3264, c0:c0 + 129], w=["zl1"])
                    if own:
                        dma("sp", zl[:, 2:4, :], ZR[3264:3520, c0:c0 + 129].rearrange("(b p) t -> p b t", p=128), w=["zl2"])
                    yield
                    for (g0, g1) in ((8, 16), (16, 24), (0, 8)):
                        if g1 <= blo:
                            continue
                        zk = ("zs", g0)
                        op("pool", lambda g: g.tensor_tensor(out=zs[:, g0:g1, :], in0=zin[:, g0:g1, 0:128], in1=zin[:, g0:g1, 1:129], op=ALU.subtract), r=["zin"], w=[zk])
                        op("pool", lambda g: g.tensor_tensor(out=zs[:, g0:g1, :], in0=zs[:, g0:g1, :], in1=bcast_last(mu24t[:, g0:g1], 128), op=ALU.mult), r=[zk, "mu24t"], w=[zk])
                        op("pool", lambda g: g.tensor_tensor(out=zs[:, g0:g1, :], in0=zs[:, g0:g1, :], in1=zin[:, g0:g1, 1:129], op=ALU.add), r=[zk, "zin"], w=[zk])
                        yield
                    zK, zV, zR = ("zs", 8), ("zs", 16), ("zs", 0)
                    for j in range(nl):
                        pr = 96 if j < 2 else 128
                        zk = "zl%d" % min(j, 2)
                        op("dve", lambda v: v.tensor_tensor(out=zls[0:pr, j, :], in0=zl[0:pr, j, 0:128], in1=zl[0:pr, j, 1:129], op=ALU.subtract), r=[zk], w=[("zls", j)])
                        op("dve", lambda v: v.scalar_tensor_tensor(out=zls[0:pr, j, :], in0=zls[0:pr, j, :], scalar=mul4t[0:pr, j:j + 1], in1=zl[0:pr, j, 1:129], op0=ALU.mult, op1=ALU.add), r=[("zls", j), zk, "mul4t"], w=[("zls", j)])
                    yield
                    op("act", lambda a: a.activation(out=lt[0:96, 0, :], in_=zls[0:96, 0, :], func=AF.Tanh), r=[("zls", 0)], w=[("lt", 0, p)])
                    op("act", lambda a: a.copy(out=lt[0:96, 1, :], in_=zls[0:96, 1, :]), r=[("zls", 1)], w=[("lt", 1, p)])
                    if own:
                        op("act", lambda a: a.activation(out=lt[:, 2:4, :], in_=zls[:, 2:4, :], func=AF.Sigmoid), r=[("zls", 2), ("zls", 3)], w=[("lt", 2, p)])
                    yield
                    for hh in range(2):
                        op("pe", lambda t: t.matmul(psB[hh][:, :], lhsT=lt[0:96, 0, :], rhs=w2b[:, hh * 512:(hh + 1) * 512], start=True, stop=True), r=[("lt", 0, p), "w2b"], w=[("psB", hh)])
                        op("dve", lambda v: v.tensor_tensor(out=sgT[:, hh * 512:(hh + 1) * 512], in0=psB[hh][:, :], in1=w0t[:, hh * 512:(hh + 1) * 512], op=ALU.add), r=[("psB", hh), "w0t"], w=[("sgT", hh)])
                        op("act", lambda a: a.activation(out=sgT[:, hh * 512:(hh + 1) * 512], in_=sgT[:, hh * 512:(hh + 1) * 512], func=AF.Sigmoid), r=[("sgT", hh)], w=[("sgT", hh)])
                        yield
                    for hh in range(2):
                        for c in range(4):
                            blk = hh * 4 + c
                            op("pe", lambda t: t.matmul(psB[hh][:, c * 128:(c + 1) * 128], lhsT=a2b[:, blk * 128:(blk + 1) * 128], rhs=lt[0:96, 1, :], start=True, stop=True), r=[("lt", 1, p), "a2b"], w=[("psB", hh)])
                        for c in range(4):
                            blk = hh * 4 + c
                            op("act", lambda a: a.activation(out=alpha[:, blk, :], in_=psB[hh][:, c * 128:(c + 1) * 128], func=AF.Sigmoid, bias=vp[:, 0, blk:blk + 1]), r=[("psB", hh), "vp"], w=[("alpha", blk)])
                        yield
                    for blk in range(8):
                        pi = blk % 2
                        op("pe", lambda t: t.matmul(psB[pi][:, 0:256], lhsT=sgT[:, blk * 128:(blk + 1) * 128], rhs=tri2[:, :], start=True, stop=True), r=[("sgT", blk // 4), ("tri2", 0), ("tri2", 1)], w=[("psB", pi)])
                        op("act", lambda a: a.activation(out=e1[:, blk, :], in_=psB[pi][:, 0:128], func=AF.Exp), r=[("psB", pi)], w=[("e1", blk)])
                        op("act", lambda a: a.activation(out=e2[:, blk, :], in_=psB[pi][:, 0:128], func=AF.Exp, scale=-1.0), r=[("psB", pi)], w=[("e2", blk)])
                        op("act", lambda a: a.activation(out=e3[:, blk, :], in_=psB[pi][:, 128:256], func=AF.Exp), r=[("psB", pi)], w=[("e3", blk)])
                        yield
                    op("act", lambda a: a.copy(out=gam[:], in_=e1[:, :, 127]), r=ek, w=[("gam", p)])
                    for blk in range(8):
                        op("dve", lambda v: v.tensor_scalar(out=kkr[:, blk, :], in0=zs[:, 8 + blk, :], scalar1=vp[:, 1, blk:blk + 1], scalar2=None, op0=ALU.mult), r=[zK, "vp"], w=[("kkr", blk)])
                    op("dve", lambda v: v.tensor_tensor(out=sqb[:], in0=kkr[:], in1=kkr[:], op=ALU.mult), r=kkrk, w=["sqb"])
                    yield
                    for hh in range(2):
                        for c in range(4):
                            blk = hh * 4 + c
                            op("pe", lambda t: t.matmul(psB[hh][:, c * 128:(c + 1) * 128], lhsT=bones[:], rhs=sqb[:, blk, :], start=True, stop=True), r=["sqb", "bones"], w=[("psB", hh)])
                        op("act", lambda a: a.activation(out=rn[:, hh * 4:(hh + 1) * 4, :], in_=psB[hh][:, :].rearrange("p (c t) -> p c t", c=4), func=AF.Ln, bias=cst[:, 2:3]), r=[("psB", hh)], w=[("rn", hh)])
                        op("act", lambda a: a.activation(out=rn[:, hh * 4:(hh + 1) * 4, :], in_=rn[:, hh * 4:(hh + 1) * 4, :], func=AF.Exp, scale=-0.5), r=[("rn", hh)], w=[("rn", hh)])
                        yield
                    op("dve", lambda v: v.tensor_tensor(out=kk[:], in0=kkr[:], in1=rn[:], op=ALU.mult), r=kkrk + [("rn", 0), ("rn", 1)], w=["kk"])
                    for blk in range(8):
                        op("dve", lambda v: v.tensor_scalar(out=t1[:, blk, :], in0=alpha[:, blk, :], scalar1=-1.0, scalar2=vp[:, 2, blk:blk + 1], op0=ALU.add, op1=ALU.mult), r=[("alpha", blk), "vp"], w=[("t1", blk)])
                    yield
                    op("dve", lambda v: v.scalar_tensor_tensor(out=kf[:], in0=t1[:], scalar=1.0, in1=zs[:, 8:16, :], op0=ALU.add, op1=ALU.mult), r=t1k + [zK], w=["kf"])
                    op("pool", lambda g: g.tensor_tensor(out=bb[:], in0=kk[:], in1=alpha[:], op=ALU.mult), r=["kk"] + alk, w=["bb"])
                    yield
                    op("dve", lambda v: v.scalar_tensor_tensor(out=AR[:, :, 0, :], in0=kk[:], scalar=-1.0, in1=e3[:], op0=ALU.mult, op1=ALU.mult), r=["kk"] + e3k, w=[("ARa", p)])
                    if own:
                        op("pool", lambda g: g.tensor_tensor(out=AR[:, :, 1, :], in0=zs[:, 0:8, :], in1=e1[:], op=ALU.mult), r=[zR] + ek, w=[("ARr", p)])
                    yield
                    op("dve", lambda v: v.tensor_tensor(out=KB2[:, :, 0, :], in0=kf[:], in1=e2[:], op=ALU.mult), r=["kf"] + e2k, w=["KBk"])
                    op("pool", lambda g: g.tensor_tensor(out=KB2[:, :, 1, :], in0=bb[:], in1=e2[:], op=ALU.mult), r=["bb"] + e2k, w=["KBb"])
                    op("act", lambda a: a.copy(out=vsb[:], in_=zs[:, 16:24, :]), r=[zV], w=["vsb"])
                    yield
                    for c in range(8):
                        op("pe", lambda t: t.transpose(out=psTb[0][:, c, :], in_=vsb[:, c, :], identity=ident[:]), r=["vsb", "ident"], w=[("psTb", 0)])
                    op("act", lambda a: a.copy(out=Vtok[:], in_=psTb[0][:].rearrange("p c t -> p (c t)")), r=[("psTb", 0)], w=[("Vtok", p)])
                    yield
                    for j in range(2):
                        for c in range(8):
                            op("pe", lambda t: t.transpose(out=psTb[1][:, c, :], in_=KB2[:, c, j, :], identity=ident[:]), r=["KBk", "KBb", "ident"], w=[("psTb", 1)])
                        op("dve", lambda v: v.tensor_copy(out=KBtok[:, j, :], in_=psTb[1][:].rearrange("p c t -> p (c t)")), r=[("psTb", 1)], w=[("KBtok", j, p)])
                        yield
                    if own:
                        for blk in range(8):
                            op("dve", lambda v: v.scalar_tensor_tensor(out=prod[:, blk, :], in0=zs[:, blk, :], scalar=vp[:, 3, blk:blk + 1], in1=kf[:, blk, :], op0=ALU.mult, op1=ALU.mult), r=[zR, "kf", "vp"], w=[("prod", blk, p)])
                        yield

                def mm(ti, p):
                    AR = AR2[p]
                    for h in range(16):
                        blk, pb = hb(h)
                        pi = h % 4
                        rd = [("ARa", p), "KBk", "KBb", ("ARr", p)]
                        op("pe", lambda t: t.matmul(psA[pi][:, 0:256], lhsT=KB2[pb:pb + 64, blk, 1, :], rhs=AR[pb:pb + 64, blk, :, :].rearrange("p a t -> p (a t)"), start=True, stop=True), r=rd, w=[("psA", pi)])
                        op("pe", lambda t: t.matmul(psA[pi][:, 256:512], lhsT=KB2[pb:pb + 64, blk, 0, :], rhs=AR[pb:pb + 64, blk, :, :].rearrange("p a t -> p (a t)"), start=True, stop=True), r=rd, w=[("psA", pi)])
                        op("dve", lambda v: v.tensor_tensor(out=A4[:, h, :], in0=psA[pi][:, :], in1=mask4[:], op=ALU.mult), r=[("psA", pi)] + [("mask4", j) for j in range(4)], w=[("A4", h)])
                    for hg in range(4):
                        hk = [("A4", hg * 4 + i) for i in range(4)]
                        op("pool", lambda g: g.tensor_copy(out=Mj[0][:, hg * 4:(hg + 1) * 4, :], in_=A4[:, hg * 4:(hg + 1) * 4, 0:128]), r=hk, w=[("M0", hg)])
                        op("pool", lambda g: g.tensor_tensor(out=Qj[0][:, hg * 4:(hg + 1) * 4, :], in0=A4[:, hg * 4:(hg + 1) * 4, 0:128], in1=identb4[:].rearrange("p (h t) -> p h t", h=4), op=ALU.add), r=hk + [("identb4", j) for j in range(4)], w=[("Q0", hg)])
                    for g8 in range(2):
                        for par in range(2):
                            for i in range(4):
                                h = g8 * 8 + 2 * i + par
                                blk, pb = hb(h)
                                op("pe", lambda t: t.matmul(psA[2 + par][:, i * 128:(i + 1) * 128], lhsT=AR[pb:pb + 64, blk, 0, :], rhs=KB2[pb:pb + 64, blk, 1, :], start=True, stop=True), r=[("ARa", p), "KBb"], w=[("psA", 2 + par)])
                        for par in range(2):
                            op("dve", lambda v: v.tensor_tensor(out=Nj[0][:, g8 * 8 + par:g8 * 8 + 8:2, :], in0=psA[2 + par][:, :].rearrange("p (h t) -> p h t", h=4), in1=maskL4[:].rearrange("p (h t) -> p h t", h=4), op=ALU.mult), r=[("psA", 2 + par)] + [("maskL4", j) for j in range(4)], w=[("N0", 2 * g8), ("N0", 2 * g8 + 1)])

                def inverse(ti):
                    bank = [0]
                    def nb():
                        bank[0] += 1
                        return bank[0] % 4
                    for lvl in range(6):
                        ci, ni = lvl % 2, (lvl + 1) % 2
                        for hg in range(4):
                            pi = nb()
                            for i in range(4):
                                h = hg * 4 + i
                                op("pe", lambda t: t.matmul(psA[pi][:, i * 128:(i + 1) * 128], lhsT=Mj[ci][:, h, :], rhs=Nj[ci][:, h, :], start=True, stop=True), r=[("M%d" % ci, hg), ("N%d" % ci, hg)], w=[("psA", pi)])
                            op("act", lambda a: a.copy(out=Nj[ni][:, hg * 4:(hg + 1) * 4, :].rearrange("p h t -> p (h t)"), in_=psA[pi][:, :]), r=[("psA", pi)], w=[("N%d" % ni, hg)])
                            yield
                        if lvl < 5:
                            for hg in range(4):
                                pi = nb()
                                for i in range(4):
                                    h = hg * 4 + i
                                    op("pe", lambda t: t.matmul(psA[pi][:, i * 128:(i + 1) * 128], lhsT=Nj[ci][:, h, :], rhs=Mj[ci][:, h, :], start=True, stop=True), r=[("M%d" % ci, hg), ("N%d" % ci, hg)], w=[("psA", pi)])
                                op("act", lambda a: a.copy(out=Mj[ni][:, hg * 4:(hg + 1) * 4, :].rearrange("p h t -> p (h t)"), in_=psA[pi][:, :]), r=[("psA", pi)], w=[("M%d" % ni, hg)])
                                yield
                        for hg in range(4):
                            pi = nb()
                            for i in range(4):
                                h = hg * 4 + i
                                op("pe", lambda t: t.matmul(psA[pi][:, i * 128:(i + 1) * 128], lhsT=Nj[ni][:, h, :], rhs=Qj[ci][:, h, :], start=True, stop=True), r=[("N%d" % ni, hg), ("Q%d" % ci, hg)], w=[("psA", pi)])
                            op("dve", lambda v: v.tensor_tensor(out=Qj[ni][:, hg * 4:(hg + 1) * 4, :].rearrange("p h t -> p (h t)"), in0=psA[pi][:, :], in1=Qj[ci][:, hg * 4:(hg + 1) * 4, :].rearrange("p h t -> p (h t)"), op=ALU.add), r=[("psA", pi), ("Q%d" % ci, hg)], w=[("Q%d" % ni, hg)])
                            yield

                def chain(ti, p):
                    own = ti >= NPT
                    AR, lt, gam, prod, Vtok, KBtok = AR2[p], lt2[p], gam2[p], prod2[p], Vtok2[p], KBtok2[p]
                    QT = Qj[0]
                    vtk = ("Vtok", p)
                    for hh in range(2):
                        for i in range(8):
                            h = hh * 8 + i
                            blk, pb = hb(h)
                            op("pe", lambda t: t.matmul(psB[hh][:, i * 64:(i + 1) * 64], lhsT=AR[pb:pb + 64, blk, 0, :], rhs=Hb[pb:pb + 64, blk, :], start=True, stop=False), r=[("ARa", p), "Hb"], w=[("psB", hh)])
                            op("pe", lambda t: t.matmul(psB[hh][:, i * 64:(i + 1) * 64], lhsT=A4[:, h, 256:384], rhs=Vtok[:, h * 64:(h + 1) * 64], start=False, stop=True), r=[("A4", h), vtk], w=[("psB", hh)])
                        if hh == 0:
                            op("act", lambda a: a.copy(out=Xb[:, 0:512], in_=psB[0][:, :]), r=[("psB", 0)], w=[("Xb", 0)])
                        else:
                            op("dve", lambda v: v.tensor_copy(out=Xb[:, 512:1024], in_=psB[1][:, :]), r=[("psB", 1)], w=[("Xb", 1)])
                    for hh in range(2):
                        for i in range(8):
                            h = hh * 8 + i
                            op("pe", lambda t: t.matmul(psB[hh][:, i * 64:(i + 1) * 64], lhsT=QT[:, h, :], rhs=Xb[:, h * 64:(h + 1) * 64], start=True, stop=True), r=[("Q0", h // 4), ("Xb", hh)], w=[("psB", hh)])
                        if hh == 0:
                            op("act", lambda a: a.copy(out=Ub[:, 0:512], in_=psB[0][:, :]), r=[("psB", 0)], w=[("Ub", 0)])
                        else:
                            op("dve", lambda v: v.tensor_copy(out=Ub[:, 512:1024], in_=psB[1][:, :]), r=[("psB", 1)], w=[("Ub", 1)])
                    if own:
                        for hh in range(2):
                            pi = 2 + hh
                            for i in range(8):
                                h = hh * 8 + i
                                blk, pb = hb(h)
                                op("pe", lambda t: t.matmul(psA[pi][:, i * 64:(i + 1) * 64], lhsT=AR[pb:pb + 64, blk, 1, :], rhs=Hb[pb:pb + 64, blk, :], start=True, stop=False), r=[("ARr", p), "Hb"], w=[("psA", pi)])
                                op("pe", lambda t: t.matmul(psA[pi][:, i * 64:(i + 1) * 64], lhsT=A4[:, h, 128:256], rhs=Ub[:, h * 64:(h + 1) * 64], start=False, stop=False), r=[("A4", h), ("Ub", hh)], w=[("psA", pi)])
                                op("pe", lambda t: t.matmul(psA[pi][:, i * 64:(i + 1) * 64], lhsT=A4[:, h, 384:512], rhs=Vtok[:, h * 64:(h + 1) * 64], start=False, stop=True), r=[("A4", h), vtk], w=[("psA", pi)])
                            op("act", lambda a: a.copy(out=ysb[:, hh * 8:(hh + 1) * 8, :].rearrange("p h v -> p (h v)"), in_=psA[pi][:, :]), r=[("psA", pi)], w=[("ysb", hh)])
                    for blk in range(8):
                        for half in range(2):
                            h = blk * 2 + half
                            pb = 64 * half
                            op("pe", lambda t: t.matmul(psA[0][pb:pb + 64, blk * 64:(blk + 1) * 64], lhsT=KBtok[:, 0, blk * 128 + pb: blk * 128 + pb + 64], rhs=Vtok[:, h * 64:(h + 1) * 64], start=True, stop=False), r=[("KBtok", 0, p), vtk], w=[("psA", 0)])
                            op("pe", lambda t: t.matmul(psA[0][pb:pb + 64, blk * 64:(blk + 1) * 64], lhsT=KBtok[:, 1, blk * 128 + pb: blk * 128 + pb + 64], rhs=Ub[:, h * 64:(h + 1) * 64], start=False, stop=True), r=[("KBtok", 1, p), ("Ub", h // 8)], w=[("psA", 0)])
                    op("dve", lambda v: v.tensor_tensor(out=Hf[:].rearrange("p b v -> p (b v)"), in0=psA[0][:, :], in1=Hf[:].rearrange("p b v -> p (b v)"), op=ALU.add), r=[("psA", 0), "Hf"], w=["Hf"])
                    op("dve", lambda v: v.tensor_tensor(out=Hf[:], in0=Hf[:], in1=bcast_last(gam[:, :], 64), op=ALU.mult), r=["Hf", ("gam", p)], w=["Hf"])
                    op("act", lambda a: a.copy(out=Hb[:], in_=Hf[:]), r=["Hf"], w=["Hb"])
                    if own:
                        oi = ti - NPT
                        yk = [("ysb", 0), ("ysb", 1)]
                        for hh in range(2):
                            for kc in range(2):
                                op("pe", lambda t: t.matmul(psB[hh][:, :], lhsT=lt[:, 2 + kc, :], rhs=g2b[:, kc, hh * 512:(hh + 1) * 512], start=(kc == 0), stop=(kc == 1)), r=[("lt", 2, p), "g2b"], w=[("psB", hh)])
                        for blk in range(8):
                            op("pe", lambda t: t.matmul(psA[1][:, blk * 2:(blk + 1) * 2], lhsT=prod[:, blk, :], rhs=hsel[:], start=True, stop=True), r=[("prod", blk, p), "hsel"], w=[("psA", 1)])
                        op("act", lambda a: a.copy(out=bon[:], in_=psA[1][:, 0:16]), r=[("psA", 1)], w=["bon"])
                        op("dve", lambda v: v.tensor_reduce(out=gst[:, 0, :], in_=ysb[:], axis=AX.X, op=ALU.add), r=yk, w=[("gst", 0)])
                        op("pool", lambda g: g.tensor_tensor(out=ysq[:], in0=ysb[:], in1=ysb[:], op=ALU.mult), r=yk, w=["ysq"])
                        op("dve", lambda v: v.tensor_reduce(out=gst[:, 1, :], in_=ysq[:], axis=AX.X, op=ALU.add), r=["ysq"], w=[("gst", 1)])
                        op("dve", lambda v: v.tensor_scalar(out=gst[:, 2, :], in0=gst[:, 0, :], scalar1=1.0 / 64, scalar2=None, op0=ALU.mult), r=[("gst", 0)], w=[("gst", 2)])
                        op("dve", lambda v: v.tensor_tensor(out=gst[:, 3, :], in0=gst[:, 2, :], in1=gst[:, 2, :], op=ALU.mult), r=[("gst", 2)], w=[("gst", 3)])
                        op("dve", lambda v: v.scalar_tensor_tensor(out=gst[:, 4, :], in0=gst[:, 1, :], scalar=1.0 / 64, in1=gst[:, 3, :], op0=ALU.mult, op1=ALU.subtract), r=[("gst", 1), ("gst", 3)], w=[("gst", 4)])
                        op("act", lambda a: a.activation(out=gst[:, 5, :], in_=gst[:, 4, :], func=AF.Ln, bias=cst[:, 1:2]), r=[("gst", 4)], w=[("gst", 5)])
                        op("act", lambda a: a.activation(out=gst[:, 5, :], in_=gst[:, 5, :], func=AF.Exp, scale=-0.5), r=[("gst", 5)], w=[("gst", 5)])
                        op("dve", lambda v: v.tensor_tensor(out=ysq[:], in0=ysb[:], in1=bcast_last(gst[:, 2, :], 64), op=ALU.subtract), r=yk + [("gst", 2), "ysq"], w=["ysq"])
                        op("dve", lambda v: v.tensor_tensor(out=ysq[:], in0=ysq[:], in1=bcast_last(gst[:, 5, :], 64), op=ALU.mult), r=["ysq", ("gst", 5)], w=["ysq"])
                        ysq2 = ysq[:].rearrange("p h v -> p (h v)")
                        op("dve", lambda v: v.tensor_tensor(out=ysq2, in0=ysq2, in1=gnwt[:], op=ALU.mult), r=["ysq", "gnwt"], w=["ysq"])
                        op("dve", lambda v: v.tensor_tensor(out=ysq2, in0=ysq2, in1=gnbt[:], op=ALU.add), r=["ysq", "gnbt"], w=["ysq"])
                        op("dve", lambda g: g.tensor_tensor(out=ysb[:], in0=Vtok[:].rearrange("p (h v) -> p h v", h=16), in1=bcast_last(bon[:, :], 64), op=ALU.mult), r=[vtk, "bon"] + yk, w=yk)
                        op("dve", lambda v: v.tensor_tensor(out=ysq[:], in0=ysq[:], in1=ysb[:], op=ALU.add), r=["ysq"] + yk, w=["ysq"])
                        for hh in range(2):
                            op("dve", lambda v: v.tensor_tensor(out=yab[:, hh * 512:(hh + 1) * 512], in0=ysq[:, hh * 8:(hh + 1) * 8, :].rearrange("p h v -> p (h v)"), in1=psB[hh][:, :], op=ALU.mult), r=["ysq", ("psB", hh)], w=[("yab", hh)])
                        for c in range(8):
                            op("pe", lambda t: t.transpose(out=psTb[0][:, c, :], in_=yab[:, c * 128:(c + 1) * 128], identity=ident[:]), r=[("yab", c // 4), "ident"], w=[("psTb", 0)])
                        op("act", lambda a: a.copy(out=yaT[:], in_=psTb[0][:]), r=[("psTb", 0)], w=["yaT"])
                        dma("sp", YAT.rearrange("(c p) t -> p c t", p=128)[:, :, oi * 128:(oi + 1) * 128], yaT[:], r=["yaT"], w=[("YAT", oi)])

                def run_rr(gens):
                    gens = list(gens)
                    while gens:
                        for g_ in list(gens):
                            try:
                                next(g_)
                            except StopIteration:
                                gens.remove(g_)

                run_rr([prep(BT[0], 0)])
                for k_, ti in enumerate(BT):
                    p = k_ % 2
                    mm(ti, p)
                    gens = [inverse(ti)]
                    if k_ + 1 < len(BT):
                        gens.append(prep(BT[k_ + 1], 1 - p))
                    run_rr(gens)
                    chain(ti, p)
                kb.barrier()


        nbt = es.enter_context(SB("nbt", [128, 16, NT], F32))
        if "C" in STAGES:
            with ExitStack() as sc:
                T = lambda n, s, d=F32: sc.enter_context(SB(n, list(s), d))
                PS = lambda n, s, d=F32: sc.enter_context(PSM(n, list(s), d))
                qgt = T("qgt", [128, RW]); kgt = T("kgt", [128, RW]); fbt = T("fbt", [128, 16]); kpt = T("kpt", [128, NT])
                carry = T("carry", [128, 16])
                cT = T("cT", [16, TO]); r1 = T("r1", [16, TO]); chi = T("chi", [16, 3, TO], BF16)
                onesb = T("onesb", [16, 3, 1408], BF16)
                psT = [PS(f"psT{i}", [64, 16, 128], BF16) for i in range(2)]
                dma("sp", qgt[:], qgbc[:, :], w=["qgt"]); dma("sp", kgt[:], kgbc[:, :], w=["kgt"])
                dma("sp", fbt[:], fbbc[:, :], w=["fbt"]); dma("sp", kpt[:], keypen[:, :], w=["kpt"])
                op("dve", lambda v: v.tensor_scalar(out=qgt[:], in0=qgt[:], scalar1=0.125, scalar2=None, op0=ALU.mult), r=["qgt"], w=["qgt"])
                op("dve", lambda v: v.memset(carry[:], 0.0), w=["carry"])
                op("pool", lambda g: g.memset(onesb[:], 1.0), w=["onesb"])
                for c3 in range(3):
                    dma("sp", KTA[:, 64:67, c3 * 1408:(c3 + 1) * 1408], onesb[:], r=["onesb"], w=[("KTA1", c3)])

                NP_ = 3
                sqs = [T(f"sq{i}", [128, 16, 64]) for i in range(NP_)]; sts = [T(f"st{i}", [128, 4, 16]) for i in range(NP_)]
                nrms = [T(f"nrm{i}", [128, 16, 64]) for i in range(NP_)]
                sq2s = [T(f"sqq{i}", [128, 16, 64]) for i in range(NP_)]; st2s = [T(f"stq{i}", [128, 4, 16]) for i in range(NP_)]
                nrm2s = [T(f"nrmq{i}", [128, 16, 64]) for i in range(NP_)]
                zfs = [T(f"zfp{i}", [128, FC]) for i in range(NP_)]
                knbs = [T(f"knb{i}", [128, RW], BF16) for i in range(NP_)]; qnbs = [T(f"qnb{i}", [128, RW], BF16) for i in range(NP_)]
                vbs = [T(f"vb{i}", [128, RW], BF16) for i in range(NP_)]
                ktiles = [T(f"ktile{i}", [64, 16, 128], BF16) for i in range(NP_)]; qtiles = [T(f"qtile{i}", [64, 16, 128], BF16) for i in range(NP_)]
                fls = [T(f"fl{i}", [128, 4, 16]) for i in range(NP_)]; ctiles = [T(f"ctile{i}", [128, 16]) for i in range(NP_)]
                pscs = [PS(f"pscp{i}", [128, 512]) for i in range(2)]

                def rmsn(src, gt, dst, tag, dkey, sq, st, nrm, kk_):
                    s3 = src.rearrange("p (h v) -> p h v", h=16)
                    op("pool", lambda g: g.tensor_tensor(out=sq[:], in0=s3, in1=s3, op=ALU.mult), r=[tag], w=[("sq",) + kk_])
                    yield
                    op("dve", lambda v: v.tensor_reduce(out=st[:, 0, :], in_=sq[:], axis=AX.X, op=ALU.add), r=[("sq",) + kk_], w=[("st", 0) + kk_])
                    op("act", lambda a: a.activation(out=st[:, 1, :], in_=st[:, 0, :], func=AF.Ln, bias=cst[:, 0:1], scale=1.0 / 64), r=[("st", 0) + kk_], w=[("st", 1) + kk_])
                    op("act", lambda a: a.activation(out=st[:, 2, :], in_=st[:, 1, :], func=AF.Exp, scale=-0.5), r=[("st", 1) + kk_], w=[("st", 2) + kk_])
                    yield
                    op("dve", lambda v: v.tensor_tensor(out=nrm[:], in0=s3, in1=bcast_last(st[:, 2, :], 64), op=ALU.mult), r=[tag, ("st", 2) + kk_], w=[("nrm",) + kk_])
                    op("dve", lambda v: v.tensor_tensor(out=dst[:], in0=nrm[:].rearrange("p h v -> p (h v)"), in1=gt[:], op=ALU.mult), r=[("nrm",) + kk_, "qgt", "kgt"], w=[dkey])
                    yield

                def cprep(ti, p):
                    own = ti >= NPT
                    zfb = zfs[p]
                    zt = ("zf", p)
                    c_lo = 0 if own else 1024
                    dma("sp", zfb[:, c_lo:FC], ZFX[ti * 128:(ti + 1) * 128, c_lo:FC], w=[zt])
                    yield
                    fl = fls[p]; ctile = ctiles[p]; psc = pscs[p % 2]
                    op("dve", lambda v: v.tensor_tensor(out=fl[:, 0, :], in0=zfb[:, 3072:3088], in1=fbt[:], op=ALU.add), r=[zt, "fbt"], w=[("fl", 0, p)])
                    op("act", lambda a: a.activation(out=fl[:, 1, :], in_=fl[:, 0, :], func=AF.Exp, scale=-1.0), r=[("fl", 0, p)], w=[("fl", 1, p)])
                    op("act", lambda a: a.activation(out=fl[:, 2, :], in_=fl[:, 1, :], func=AF.Ln, bias=cst[:, 3:4]), r=[("fl", 1, p)], w=[("fl", 2, p)])
                    op("pe", lambda t: t.matmul(psc[:, 0:16], lhsT=mUi[:], rhs=fl[:, 2, :], start=True, stop=True), r=[("fl", 2, p)], w=[("psc", p % 2)])
                    op("pe", lambda t: t.matmul(psc[:, 16:32], lhsT=onesf[:], rhs=fl[:, 2, :], start=True, stop=True), r=[("fl", 2, p)], w=[("psc", p % 2)])
                    op("dve", lambda v: v.tensor_tensor(out=ctile[:], in0=carry[:], in1=psc[:, 0:16], op=ALU.subtract), r=["carry", ("psc", p % 2)], w=[("ctile", p)])
                    op("dve", lambda v: v.tensor_tensor(out=carry[:], in0=carry[:], in1=psc[:, 16:32], op=ALU.subtract), r=["carry", ("psc", p % 2)], w=["carry"])
                    op("dve", lambda v: v.tensor_scalar(out=nbt[:, :, ti], in0=ctile[:], scalar1=-1.0, scalar2=kpt[:, ti:ti + 1], op0=ALU.mult, op1=ALU.add), r=[("ctile", p), "kpt"], w=[("nbt", ti)])
                    if own:
                        oi = ti - NPT
                        op("pe", lambda t: t.transpose(out=psc[0:16, 128:256], in_=ctile[:], identity=identf[:]), r=[("ctile", p)], w=[("psc", p % 2)])
                        op("act", lambda a: a.copy(out=cT[:, oi * 128:(oi + 1) * 128], in_=psc[0:16, 128:256]), r=[("psc", p % 2)], w=[("cT", oi)])
                    yield
                    op("pool", lambda g: g.tensor_copy(out=vbs[p][:], in_=zfb[:, 2048:3072]), r=[zt], w=[("vb", p)])
                    dma("sp", VFX[ti * 128:(ti + 1) * 128, :], vbs[p][:], r=[("vb", p)], w=[("VFX", ti)])
                    yield
                    yield from rmsn(zfb[:, 1024:2048], kgt, knbs[p], zt, ("knb", p), sqs[p], sts[p], nrms[p], ("k", p))
                    for hh in range(2):
                        for i in range(8):
                            h = hh * 8 + i
                            op("pe", lambda t: t.transpose(out=psT[0][:, h, :], in_=knbs[p][:, h * 64:(h + 1) * 64], identity=ident[:]), r=[("knb", p), "ident"], w=["psT0"])
                    op("act", lambda a: a.copy(out=ktiles[p][:], in_=psT[0][:]), r=["psT0"], w=[("ktile", p)])
                    dma("sp", KTA[:, 0:64, ti * 128:(ti + 1) * 128].rearrange("h r t -> r h t"), ktiles[p][:], r=[("ktile", p)], w=[("KTA", ti)])
                    yield
                    if own:
                        oi = ti - NPT
                        yield from rmsn(zfb[:, 0:1024], qgt, qnbs[p], zt, ("qnb", p), sq2s[p], st2s[p], nrm2s[p], ("q", p))
                        for h in range(16):
                            op("pe", lambda t: t.transpose(out=psT[1][:, h, :], in_=qnbs[p][:, h * 64:(h + 1) * 64], identity=ident[:]), r=[("qnb", p), "ident"], w=["psT1"])
                        op("act", lambda a: a.copy(out=qtiles[p][:], in_=psT[1][:]), r=["psT1"], w=[("qtile", p)])
                        dma("sp", QTA[:, 0:64, oi * 128:(oi + 1) * 128].rearrange("h r t -> r h t"), qtiles[p][:], r=[("qtile", p)], w=[("QTA", oi)])
                        yield

                live = []
                nxt_t = 0
                while live or nxt_t < NT:
                    while len(live) < NP_ and nxt_t < NT:
                        live.append(cprep(nxt_t, nxt_t % NP_))
                        nxt_t += 1
                    for g_ in list(live):
                        try:
                            next(g_)
                        except StopIteration:
                            live.remove(g_)
                ctk = [("cT", i) for i in range(NOT_)]
                op("dve", lambda v: v.tensor_copy(out=chi[:, 0, :], in_=cT[:]), r=ctk, w=[("chi", 0)])
                op("dve", lambda v: v.tensor_tensor(out=r1[:], in0=cT[:], in1=chi[:, 0, :], op=ALU.subtract), r=ctk + [("chi", 0)], w=["r1"])
                op("dve", lambda v: v.tensor_copy(out=chi[:, 1, :], in_=r1[:]), r=["r1"], w=[("chi", 1)])
                op("dve", lambda v: v.tensor_tensor(out=r1[:], in0=r1[:], in1=chi[:, 1, :], op=ALU.subtract), r=["r1", ("chi", 1)], w=["r1"])
                op("dve", lambda v: v.tensor_copy(out=chi[:, 2, :], in_=r1[:]), r=["r1"], w=[("chi", 2)])
                dma("sp", QTA[:, 64:67, :], chi[:], r=[("chi", 0), ("chi", 1), ("chi", 2)], w=["QTA1"])
                kb.barrier()

            with ExitStack() as sc:
                T = lambda n, s, d=F32: sc.enter_context(SB(n, list(s), d))
                PS = lambda n, s, d=F32: sc.enter_context(PSM(n, list(s), d))
                QT = [T(f"QT{i}", [67, TO], BF16) for i in range(2)]
                KT = [T(f"KT{i}", [67, TA], BF16) for i in range(2)]
                Vh = [T(f"Vh{i}", [128, NT, 65], BF16) for i in range(2)]
                NSB = 5
                pt = [T(f"pt{i}", [128, 512], BF16) for i in range(NSB)]
                recrs = [T(f"recr{i}", [128, 512]) for i in range(2)]; recb = T("recb", [64, 512])
                deferred = []
                ybo = [T(f"ybo{i}", [64, 512], BF16) for i in range(2)]
                pss = [PS(f"pss{i}", [128, 512]) for i in range(NSB)]
                pso = [PS(f"pso{i}", [128, 512]) for i in range(2)]
                psr = PS("psr", [128, 512])
                for i in range(2):
                    op("pool", lambda g: g.memset(Vh[i][:, :, 64:65], 1.0), w=[("Vh1", i)])
                cnt = {"s": 0, "o": 0}

                def load_head(h):
                    b = h % 2
                    dma("sp", QT[b][:], QTA[h, :, :], w=[("QT", b)])
                    dma("sp", KT[b][:], KTA[h, :, :], w=[("KT", b)])
                    for q4 in range(3):
                        t0, t1_ = q4 * 11, (q4 + 1) * 11
                        dma("sp", Vh[b][:, t0:t1_, 0:64], VFX[t0 * 128:t1_ * 128, h * 64:(h + 1) * 64].rearrange("(t p) v -> p t v", p=128), w=[("Vh", b, q4)])

                steps = []
                for h in range(16):
                    for qsb in range(4):
                        d0 = NPT + 4 * qsb
                        for kbi in range(d0 + 4):
                            steps.append((h, qsb, kbi, d0))

                def emit_S(idx):
                    h, qsb, kbi, d0 = steps[idx]
                    b = h % 2
                    j0 = max(0, kbi - d0)
                    ncol = 512 - 128 * j0
                    si = idx % NSB
                    op("pe", lambda t: t.matmul(pss[si][:, 0:ncol], lhsT=KT[b][:, kbi * 128:(kbi + 1) * 128], rhs=QT[b][:, qsb * 512 + j0 * 128:(qsb + 1) * 512], start=True, stop=True), r=[("QT", b), ("KT", b)], w=[("pss", si)])

                def emit_rest(idx):
                    h, qsb, kbi, d0 = steps[idx]
                    b = h % 2
                    if qsb == 0 and kbi == 0 and h + 1 < 16:
                        load_head(h + 1)
                    vk = [("Vh", b, q4) for q4 in range(3)] + [("Vh1", b)]
                    j0 = max(0, kbi - d0)
                    ncol = 512 - 128 * j0
                    si = idx % NSB
                    oi = (h * 4 + qsb) % 2
                    op("act", lambda a: a.activation(out=pt[si][:, 0:ncol], in_=pss[si][:, 0:ncol], func=AF.Exp, bias=nbt[:, h, kbi:kbi + 1]), r=[("pss", si)], w=[("pt", si)])
                    if kbi >= d0:
                        op("dve", lambda g: g.tensor_tensor(out=pt[si][:, 0:128], in0=pt[si][:, 0:128], in1=mUi[:], op=ALU.mult), r=[("pt", si)], w=[("pt", si)])
                    op("pe", lambda t: t.matmul(pso[oi][0:65, j0 * 128:512], lhsT=Vh[b][:, kbi, :], rhs=pt[si][:, 0:ncol], start=(kbi == 0), stop=(kbi == d0 + 3)), r=[("pt", si)] + vk, w=[("pso", oi)])
                    if kbi == d0 + 3:
                        rb = recrs[oi]
                        op("dve", lambda v: v.reciprocal(out=rb[64:65, :], in_=pso[oi][64:65, :]), r=[("pso", oi)], w=[("recr", oi)])

                        def tail(h=h, qsb=qsb, oi=oi, rb=rb):
                            op("pe", lambda t: t.matmul(psr[0:64, :], lhsT=onesf[64:65, 0:64], rhs=rb[64:65, :], start=True, stop=True), r=[("recr", oi)], w=["psr"])
                            op("act", lambda a: a.copy(out=recb[:], in_=psr[0:64, :]), r=["psr"], w=["recb"])
                            op("dve", lambda v: v.tensor_tensor(out=ybo[oi][:], in0=pso[oi][0:64, :], in1=recb[:], op=ALU.mult), r=[("pso", oi), "recb"], w=[("ybo", oi)])
                            dma("sp", YBT[h * 64:(h + 1) * 64, qsb * 512:(qsb + 1) * 512], ybo[oi][:], r=[("ybo", oi)], w=[("YBT", h, qsb)])
                        deferred.append([8, tail])

                load_head(0)
                LOOK = 4
                for idx in range(min(LOOK, len(steps))):
                    emit_S(idx)
                for idx in range(len(steps)):
                    emit_rest(idx)
                    if idx + LOOK < len(steps):
                        emit_S(idx + LOOK)
                    for d_ in list(deferred):
                        d_[0] -= 1
                        if d_[0] <= 0:
                            d_[1]()
                            deferred.remove(d_)
                for d_ in deferred:
                    d_[1]()
                kb.barrier()

        MT = scr("MT", [D, TO], BF16)
        if "D" in STAGES:
            with ExitStack() as sd:
                T = lambda n, s, d=F32: sd.enter_context(SB(n, list(s), d))
                PS = lambda n, s, d=F32: sd.enter_context(PSM(n, list(s), d))
                wab = T("wab", [128, 8, D], BF16); wbb = T("wbb", [128, 8, D], BF16)
                wst = [T(f"wst{i}", [128, 2, D]) for i in range(2)]
                yaTs = T("yaTs", [128, 8, TO], BF16); ybTs = T("ybTs", [128, 8, TO], BF16)
                ybl = [T(f"ybl{i}", [128, RW], BF16) for i in range(2)]
                gat = [T(f"gat{i}", [128, 2, 512], BF16) for i in range(4)]
                m1 = [T(f"m1{i}", [128, 512]) for i in range(3)]; m2 = [T(f"m2{i}", [128, 512]) for i in range(3)]
                mo = [T(f"mo{i}", [128, 512], BF16) for i in range(3)]
                psa = [PS(f"psa{i}", [128, 512]) for i in range(3)]; psb_ = [PS(f"psb{i}", [128, 512]) for i in range(3)]
                n = 0
                for (wsrc, wdst, nm) in ((w_a, wab, "wab"), (w_b, wbb, "wbb")):
                    for i in range(4):
                        bb_ = n % 2; n += 1
                        dma("sp", wst[bb_][:], wsrc[i * 256:(i + 1) * 256, :].rearrange("(c p) n -> p c n", p=128), w=[("wst", bb_)])
                        if n % 2:
                            op("dve", lambda g: g.tensor_copy(out=wdst[:, 2 * i:2 * i + 2, :], in_=wst[bb_][:]), r=[("wst", bb_)], w=[(nm, i)])
                        else:
                            op("act", lambda g: g.copy(out=wdst[:, 2 * i:2 * i + 2, :], in_=wst[bb_][:]), r=[("wst", bb_)], w=[(nm, i)])
                dma("sp", yaTs[:], YAT.rearrange("(c p) t -> p c t", p=128), w=["yaTs"])
                dma("sp", ybTs[:], YBT.rearrange("(c p) t -> p c t", p=128), w=["ybTs"])
                wak = [("wab", i) for i in range(4)]; wbk = [("wbb", i) for i in range(4)]
                it = 0
                for dblk in range(16):
                    for tsb in range(4):
                        b = it % 3; gb = it % 4; it += 1
                        dma("sp", gat[gb][:, 0, :], GT[dblk * 128:(dblk + 1) * 128, tsb * 512:(tsb + 1) * 512], w=[("gat", gb, 0)])
                        dma("sp", gat[gb][:, 1, :], GT[2048 + dblk * 128:2048 + (dblk + 1) * 128, tsb * 512:(tsb + 1) * 512], w=[("gat", gb, 1)])
                        for kc in range(8):
                            op("pe", lambda t: t.matmul(psa[b][:, :], lhsT=wab[:, kc, dblk * 128:(dblk + 1) * 128], rhs=yaTs[:, kc, tsb * 512:(tsb + 1) * 512], start=(kc == 0), stop=(kc == 7)), r=wak + ["yaTs"], w=[("psa", b)])
                        for kc in range(8):
                            op("pe", lambda t: t.matmul(psb_[b][:, :], lhsT=wbb[:, kc, dblk * 128:(dblk + 1) * 128], rhs=ybTs[:, kc, tsb * 512:(tsb + 1) * 512], start=(kc == 0), stop=(kc == 7)), r=wbk + ["ybTs"], w=[("psb", b)])
                        op("dve", lambda v: v.tensor_tensor(out=m1[b][:], in0=psa[b][:, :], in1=gat[gb][:, 0, :], op=ALU.mult), r=[("psa", b), ("gat", gb, 0)], w=[("m1", b)])
                        op("dve", lambda v: v.tensor_tensor(out=m2[b][:], in0=psb_[b][:, :], in1=gat[gb][:, 1, :], op=ALU.mult), r=[("psb", b), ("gat", gb, 1)], w=[("m2", b)])
                        op("pool", lambda g: g.tensor_tensor(out=mo[b][:], in0=m1[b][:], in1=m2[b][:], op=ALU.add), r=[("m1", b), ("m2", b)], w=[("mo", b)])
                        dma("sp", MT[dblk * 128:(dblk + 1) * 128, tsb * 512:(tsb + 1) * 512], mo[b][:], r=[("mo", b)], w=[("MT", dblk, tsb)])
                kb.barrier()

            with ExitStack() as sd:
                T = lambda n, s, d=F32: sd.enter_context(SB(n, list(s), d))
                PS = lambda n, s, d=F32: sd.enter_context(PSM(n, list(s), d))
                wob = T("wob", [128, 16, D], BF16)
                wst = [T(f"wst{i}", [128, 2, D]) for i in range(2)]
                mTt = [T(f"mTt{i}", [128, 16, 128], BF16) for i in range(2)]
                xt = [T(f"xt{i}", [128, D]) for i in range(2)]
                h2t = [T(f"h2t{i}", [128, D]) for i in range(2)]
                g2t = T("g2t", [128, D]); xnb = T("xnb", [128, D], BF16); junk = T("junk", [128, D], BF16)
                st1 = T("st1", [128, 4]); xn2t = T("xn2t", [128, 16, 128], BF16)
                pso_ = [PS(f"pso{i}", [128, 512]) for i in range(4)]
                pstr = [PS(f"pstr{i}", [128, 8, 128], BF16) for i in range(2)]
                for i in range(8):
                    bb_ = i % 2
                    dma("sp", wst[bb_][:], w_o[i * 256:(i + 1) * 256, :].rearrange("(c p) n -> p c n", p=128), w=[("wst", bb_)])
                    if i % 2:
                        op("dve", lambda g: g.tensor_copy(out=wob[:, 2 * i:2 * i + 2, :], in_=wst[bb_][:]), r=[("wst", bb_)], w=[("wob", i)])
                    else:
                        op("act", lambda g: g.copy(out=wob[:, 2 * i:2 * i + 2, :], in_=wst[bb_][:]), r=[("wst", bb_)], w=[("wob", i)])
                wok = [("wob", i) for i in range(8)]
                dma("sp", g2t[:], g2bc[:, :], w=["g2t"])
                MT3 = MT.rearrange("(c p) t -> p c t", p=128)
                xnbs = [xnb, T("xnb1", [128, D], BF16)]
                st1s = [st1, T("st11", [128, 4])]

                def d2_mm(ti):
                    b = ti % 2
                    xb_, s1 = xnbs[b], st1s[b]
                    dma("sp", mTt[b][:], MT3[:, :, ti * 128:(ti + 1) * 128], w=[("mTt", b)])
                    dma("sp", xt[b][:], xall[TP + ti * 128:TP + (ti + 1) * 128, :], w=[("xt", b)])
                    for cb in range(4):
                        for kc in range(16):
                            op("pe", lambda t: t.matmul(pso_[cb][:, :], lhsT=mTt[b][:, kc, :], rhs=wob[:, kc, cb * 512:(cb + 1) * 512], start=(kc == 0), stop=(kc == 15)), r=[("mTt", b)] + wok, w=[("pso", cb)])
                        op("dve", lambda v: v.tensor_tensor(out=h2t[b][:, cb * 512:(cb + 1) * 512], in0=pso_[cb][:, :], in1=xt[b][:, cb * 512:(cb + 1) * 512], op=ALU.add), r=[("pso", cb), ("xt", b)], w=[("h2t", b, cb)])
                    hk = [("h2t", b, cb) for cb in range(4)]
                    dma("sp", H2[ti * 128:(ti + 1) * 128, :], h2t[b][:], r=hk, w=[("H2", ti)])
                    op("dve", lambda v: v.memset(s1[:, 0:1], 0.0), w=[("ss", b)])
                    op("act", lambda a: a.activation(out=junk[:], in_=h2t[b][:], func=AF.Square, accum_out=s1[:, 0:1]), r=hk + [("ss", b)], w=[("ss", b)])
                    op("act", lambda a: a.activation(out=s1[:, 1:2], in_=s1[:, 0:1], func=AF.Ln, bias=cst[:, 0:1], scale=1.0 / D), r=[("ss", b)], w=[("lnv", b)])
                    op("act", lambda a: a.activation(out=s1[:, 2:3], in_=s1[:, 1:2], func=AF.Exp, scale=-0.5), r=[("lnv", b)], w=[("rstd", b)])
                    op("dve", lambda v: v.scalar_tensor_tensor(out=xb_[:], in0=h2t[b][:], scalar=s1[:, 2:3], in1=g2t[:], op0=ALU.mult, op1=ALU.mult), r=hk + [("rstd", b), "g2t"], w=[("xnb", b)])

                def d2_tr(ti):
                    b = ti % 2
                    xb_ = xnbs[b]
                    for hh in range(2):
                        for c in range(8):
                            kc = hh * 8 + c
                            op("pe", lambda t: t.transpose(out=pstr[hh][:, c, :], in_=xb_[:, kc * 128:(kc + 1) * 128], identity=ident[:]), r=[("xnb", b)], w=[("pstr", hh)])
                        op("act", lambda a: a.copy(out=xn2t[:, hh * 8:(hh + 1) * 8, :], in_=pstr[hh][:]), r=[("pstr", hh)], w=[("xn2t", hh)])
                    dma("sp", XN2T.rearrange("(c p) t -> p c t", p=128)[:, :, ti * 128:(ti + 1) * 128], xn2t[:], r=[("xn2t", 0), ("xn2t", 1)], w=[("XN2T", ti)])

                d2_mm(0)
                for ti in range(1, NOT_):
                    d2_mm(ti)
                    d2_tr(ti - 1)
                d2_tr(NOT_ - 1)
                kb.barrier()

        if "E" in STAGES:
            with ExitStack() as se:
                T = lambda n, s, d=F32: se.enter_context(SB(n, list(s), d))
                PS = lambda n, s, d=F32: se.enter_context(PSM(n, list(s), d))
                xn2 = T("xn2", [128, 16, TO], BF16)
                wf = [T(f"wf{i}", [128, 16, 256]) for i in range(2)]
                wb = [T(f"wb{i}", [128, 16, 256], BF16) for i in range(2)]
                sg = [T(f"sg{i}", [128, 512]) for i in range(2)]
                ao = [T(f"ao{i}", [128, 512], BF16) for i in range(2)]
                psg = [PS(f"psg{i}", [128, 512]) for i in range(2)]; psu = [PS(f"psu{i}", [128, 512]) for i in range(2)]
                wst = [T(f"wst{i}", [128, 2, D]) for i in range(2)]
                wcv = [T(f"wcv{i}", [128, 2, D], BF16) for i in range(2)]

                def conv_wdn(i):
                    b = i % 2
                    dma("sp", wst[b][:], w_dn[i * 256:(i + 1) * 256, :].rearrange("(c p) n -> p c n", p=128), w=[("wst", b)])
                    op("dve", lambda v: v.tensor_copy(out=wcv[b][:], in_=wst[b][:]), r=[("wst", b)], w=[("wcv", b)])
                    dma("sp", WDNB[i * 256:(i + 1) * 256, :].rearrange("(c p) n -> p c n", p=128), wcv[b][:], r=[("wcv", b)], w=[("WDNB", i)])
                XN3 = XN2T.rearrange("(c p) t -> p c t", p=128)
                for q4 in range(4):
                    dma("sp", xn2[:, q4 * 4:(q4 + 1) * 4, :], XN3[:, q4 * 4:(q4 + 1) * 4, :], w=[("xn2", q4)])
                xk = [("xn2", q4) for q4 in range(4)]

                def load_gu(fb):
                    b = fb % 2
                    for q4 in range(4):
                        dma("sp", wf[b][:, q4 * 4:(q4 + 1) * 4, 0:128], w_gu[q4 * 512:(q4 + 1) * 512, fb * 128:(fb + 1) * 128].rearrange("(kc p) n -> p kc n", p=128), w=[("wf", b, q4, 0)])
                        dma("sp", wf[b][:, q4 * 4:(q4 + 1) * 4, 128:256], w_gu[q4 * 512:(q4 + 1) * 512, DFF + fb * 128:DFF + (fb + 1) * 128].rearrange("(kc p) n -> p kc n", p=128), w=[("wf", b, q4, 1)])
                    for q4 in range(4):
                        op("pool", lambda g: g.tensor_copy(out=wb[b][:, q4 * 4:(q4 + 1) * 4, :], in_=wf[b][:, q4 * 4:(q4 + 1) * 4, :]), r=[("wf", b, q4, 0), ("wf", b, q4, 1)], w=[("wb", b, q4)])

                load_gu(0)
                it = 0
                for fb in range(DFF // 128):
                    b = fb % 2
                    if fb + 1 < DFF // 128:
                        load_gu(fb + 1)
                    if fb % 2 == 0:
                        conv_wdn(fb // 2)
                    for tsb in range(4):
                        pi = it % 2; it += 1
                        for kc in range(16):
                            op("pe", lambda t: t.matmul(psg[pi][:, :], lhsT=wb[b][:, kc, 0:128], rhs=xn2[:, kc, tsb * 512:(tsb + 1) * 512], start=(kc == 0), stop=(kc == 15)), r=[("wb", b, kc // 4)] + xk, w=[("psg", pi)])
                        for kc in range(16):
                            op("pe", lambda t: t.matmul(psu[pi][:, :], lhsT=wb[b][:, kc, 128:256], rhs=xn2[:, kc, tsb * 512:(tsb + 1) * 512], start=(kc == 0), stop=(kc == 15)), r=[("wb", b, kc // 4)] + xk, w=[("psu", pi)])
                        op("act", lambda a: a.activation(out=sg[pi][:], in_=psg[pi][:, :], func=AF.Silu), r=[("psg", pi)], w=[("sg", pi)])
                        op("dve", lambda v: v.tensor_tensor(out=ao[pi][:], in0=sg[pi][:], in1=psu[pi][:, :], op=ALU.mult), r=[("sg", pi), ("psu", pi)], w=[("ao", pi)])
                        dma("sp", ACTT[fb * 128:(fb + 1) * 128, tsb * 512:(tsb + 1) * 512], ao[pi][:], r=[("ao", pi)], w=[("ACTT", fb, tsb)])
                kb.barrier()

        if "F" in STAGES:
            with ExitStack() as sf:
                T = lambda n, s, d=F32: sf.enter_context(SB(n, list(s), d))
                PS = lambda n, s, d=F32: sf.enter_context(PSM(n, list(s), d))
                NC_ = DFF // 128
                acts = T("acts", [128, NC_, 1024], BF16)
                wdb = [T(f"wdb{i}", [128, NC_, 512], BF16) for i in range(2)]
                h2s = [T(f"h2s{i}", [128, 512]) for i in range(4)]
                ot = [T(f"ot{i}", [128, 512]) for i in range(4)]
                psd = [PS(f"psd{i}", [128, 512]) for i in range(4)]
                AC3 = ACTT.rearrange("(c p) t -> p c t", p=128)
                WD3 = WDNB.rearrange("(c p) n -> p c n", p=128)
                it = 0
                def load_wd(cb, b):
                    for q4 in range(4):
                        dma("sp", wdb[b][:, q4 * 11:(q4 + 1) * 11, :], WD3[:, q4 * 11:(q4 + 1) * 11, cb * 512:(cb + 1) * 512], w=[("wdb", b, q4)])
                seq = [(th, cb) for th in range(2) for cb in range(4)]
                load_wd(seq[0][1], 0)
                for si, (th, cb) in enumerate(seq):
                    b = si % 2
                    if cb == 0:
                        for q4 in range(4):
                            dma("act", acts[:, q4 * 11:(q4 + 1) * 11, :], AC3[:, q4 * 11:(q4 + 1) * 11, th * 1024:(th + 1) * 1024], w=[("acts", q4)])
                    if si + 1 < len(seq):
                        load_wd(seq[si + 1][1], (si + 1) % 2)
                    for j in range(8):
                        pi = it % 4; it += 1
                        tok0 = th * 1024 + j * 128
                        dma("act", h2s[pi][:], H2[tok0:tok0 + 128, cb * 512:(cb + 1) * 512], w=[("h2s", pi)])
                        for c in range(NC_):
                            op("pe", lambda t: t.matmul(psd[pi][:, :], lhsT=acts[:, c, j * 128:(j + 1) * 128], rhs=wdb[b][:, c, :], start=(c == 0), stop=(c == NC_ - 1)), r=[("acts", c // 11), ("wdb", b, c // 11)], w=[("psd", pi)])
                        op("dve", lambda v: v.tensor_tensor(out=ot[pi][:], in0=psd[pi][:, :], in1=h2s[pi][:], op=ALU.add), r=[("psd", pi), ("h2s", pi)], w=[("ot", pi)])
                        dma("act", out[tok0:tok0 + 128, cb * 512:(cb + 1) * 512], ot[pi][:], r=[("ot", pi)], w=[("out", tok0, cb)])
                kb.barrier()

        kb.barrier()
    return nc


def prep_core_inputs(inp, core):
    b, half = core // 2, core % 2
    x = np.asarray(inp["x"], dtype=np.float32)
    meta = np.asarray(inp["meta_tokens"], dtype=np.float32)
    xall = np.zeros((TA, D), np.float32)
    keypen = np.zeros((128, NT), np.float32)
    if half == 0:
        xall[TP - 16:TP] = meta
        xall[TP:] = x[b, 0:2048]
        ndummy = TP - 16
    else:
        xall[112:128] = meta
        xall[128:TP] = x[b, 0:2048]
        xall[TP:] = x[b, 2048:4096]
        ndummy = 112
    kp = np.zeros(TA, np.float32)
    kp[:ndummy] = -30000.0
    keypen[:, :] = kp.reshape(NT, 128).T
    f = lambda k: np.asarray(inp[k], dtype=np.float32)[0]
    bc = lambda v, n=128: np.ascontiguousarray(np.broadcast_to(v[None, :], (n, v.shape[0])))
    mu = f("rwkv_mu")
    mu24 = np.ascontiguousarray(mu[0:3072].reshape(24, 128).T)
    mul4 = np.zeros((128, 4), np.float32)
    mul4[0:96, 0] = mu[3072:3168]
    mul4[0:96, 1] = mu[3168:3264]
    mul4[:, 2] = mu[3264:3392]
    mul4[:, 3] = mu[3392:3520]
    pp = lambda v: np.ascontiguousarray(v.reshape(8, 128).T)
    vecp = np.stack([pp(f("rwkv_a0")), pp(f("rwkv_k_k")), pp(f("rwkv_k_a")), pp(f("rwkv_r_k").reshape(-1))], axis=1)
    d = {
        "xall": xall, "keypen": keypen,
        "w_in": f("w_in"), "w_a": f("w_branch_a"), "w_b": f("w_branch_b"), "w_o": f("w_o"),
        "w_gu": f("w_gate_up"), "w_dn": f("w_down"),
        "g1bc": bc(f("norm1_g")), "g2bc": bc(f("norm2_g")),
        "mu24": mu24, "mul4": mul4, "w0bc": bc(f("rwkv_w0")),
        "w2": f("rwkv_w2"), "a2": f("rwkv_a2"), "g2": f("rwkv_g2"),
        "vecp": np.ascontiguousarray(vecp),
        "gnwbc": bc(f("rwkv_gn_w")), "gnbbc": bc(f("rwkv_gn_b")),
        "qgbc": bc(np.tile(f("fox_q_norm_g"), 16)), "kgbc": bc(np.tile(f("fox_k_norm_g"), 16)),
        "fbbc": bc(f("fox_f_bias")),
    }
    return d


_NC_CACHE = {}


def kernel(**inputs):
    if "nc" not in _NC_CACHE:
        _NC_CACHE["nc"] = build_program()
    nc = _NC_CACHE["nc"]
    shared = None
    in_maps = []
    ncores = int(os.environ.get("KCORES", "8"))
    for c in range(ncores):
        in_maps.append(prep_core_inputs(inputs, c))
    res = run_bass_kernel_spmd(nc, in_maps, core_ids=list(range(ncores)))
    B = inputs["x"].shape[0]
    outp = np.zeros((B, 4096, D), np.float32)
    for c in range(ncores):
        b, half = c // 2, c % 2
        outp[b, half * 2048:(half + 1) * 2048] = res.results[c]["out"]
    if DEBUG:
        kernel.last = res
    return outp
```

```python
import os
import numpy as np
from contextlib import ExitStack
import concourse.bass as bass
import concourse.mybir as mybir
from concourse.bass_utils import run_bass_kernel_spmd

F32 = mybir.dt.float32
BF16 = mybir.dt.bfloat16
AF = mybir.ActivationFunctionType
ALU = mybir.AluOpType
AX = mybir.AxisListType

D = 2048
NPT, NOT_, NT = 17, 16, 33
TP, TO, TA = NPT * 128, NOT_ * 128, NT * 128
RW = 1024
RC = 3520
FC = 3088
NIN = 10704
DFF = 5632
DECAY_C = 0.6065306597126334
DEBUG = os.environ.get("KDEBUG", "") != ""
STAGES = os.environ.get("KSTAGES", "ABCDEF")


class KB:
    def __init__(self, nc, es):
        self.nc = nc
        self.eng = {"pe": nc.tensor, "act": nc.scalar, "dve": nc.vector, "pool": nc.gpsimd, "sp": nc.sync}
        self.sem = {e: es.enter_context(nc.semaphore("s_" + e)) for e in ("pe", "act", "dve", "pool")}
        self.cnt = {e: 0 for e in self.sem}
        self.seen = {}
        self.nd = 12
        self.dsem = {q: [es.enter_context(nc.semaphore(f"d_{q}{i}")) for i in range(self.nd)] for q in ("sp", "pool", "act")}
        self.dcnt = {q: 0 for q in self.dsem}
        self.dlast = {q: [None] * self.nd for q in self.dsem}
        self.res = {}

    def ensure(self, e, tok):
        if tok is None:
            return
        if tok[0] == "c":
            _, pe_, idx = tok
            if pe_ == e and e == "pe":
                return
            k = (e, pe_)
            if self.seen.get(k, 0) >= idx:
                return
            self.eng[e].wait_ge(self.sem[pe_], idx)
            self.seen[k] = idx
        else:
            _, q, slot, val = tok
            k = (e, q, slot)
            if self.seen.get(k, 0) >= val:
                return
            self.eng[e].wait_ge(self.dsem[q][slot], val)
            self.seen[k] = val

    def _deps(self, e, r, w):
        for k in r:
            st = self.res.get(k)
            if st:
                self.ensure(e, st["w"])
        for k in w:
            st = self.res.get(k)
            if st:
                self.ensure(e, st["w"])
                for t in st["r"].values():
                    self.ensure(e, t)

    def _upd(self, tok, rk, r, w):
        for k in w:
            self.res[k] = {"w": tok, "r": {}}
        for k in r:
            st = self.res.setdefault(k, {"w": None, "r": {}})
            st["r"][rk] = tok

    def op(self, e, fn, r=(), w=()):
        self._deps(e, r, w)
        inst = fn(self.eng[e])
        self.cnt[e] += 1
        inst.then_inc(self.sem[e], 1)
        tok = ("c", e, self.cnt[e])
        self._upd(tok, e, r, w)
        return tok

    def dma(self, q, out, in_, r=(), w=(), slow=False):
        self._deps(q, r, w)
        n = self.dcnt[q]
        slot = n % self.nd
        self.ensure(q, self.dlast[q][slot])
        val = 16 * (n // self.nd + 1)
        if slow:
            self.eng[q].dma_start(out=out, in_=in_, allow_slow_non_contiguous=True).then_inc(self.dsem[q][slot], 16)
        else:
            self.eng[q].dma_start(out=out, in_=in_).then_inc(self.dsem[q][slot], 16)
        tok = ("d", q, slot, val)
        self.dlast[q][slot] = tok
        self.dcnt[q] = n + 1
        self._upd(tok, ("d", q, slot), r, w)
        return tok

    def barrier(self):
        for e in ("pe", "act", "dve", "pool", "sp"):
            for c in self.cnt:
                if c != e and self.cnt[c] > 0:
                    self.ensure(e, ("c", c, self.cnt[c]))
            if e in self.cnt and e != "pe" and self.cnt[e] > 0:
                self.ensure(e, ("c", e, self.cnt[e]))
            for q in self.dsem:
                for t in self.dlast[q]:
                    self.ensure(e, t)
        self.res = {}


def bcast_last(ap, n):
    dims = [list(d) for d in ap.ap]
    if dims[-1][1] == 1 and len(dims) == 3:
        dims = dims[:2]
    return bass.AP(ap.tensor, ap.offset, dims + [[0, n]])


def build_program():
    nc = bass.Bass("TRN2", target_bir_lowering=False)
    _u = {"n": 0}
    def _un(n):
        _u["n"] += 1
        return "%s_%d" % (n, _u["n"])
    SB = lambda n, s, d: nc.sbuf_tensor(_un(n), s, d)
    PSM = lambda n, s, d: nc.psum_tensor(_un(n), s, d)
    dt_in = lambda n, s, d=F32: nc.dram_tensor(n, list(s), d, kind="ExternalInput").ap()
    okind = "ExternalOutput" if DEBUG else None
    DBGOUT = os.environ.get("KDBGOUT", "").split(",")
    def scr(n, s, d):
        if DEBUG and n in DBGOUT:
            return nc.dram_tensor(n, list(s), d, kind="ExternalOutput").ap()
        return nc.dram_tensor(n, list(s), d).ap()

    xall = dt_in("xall", [TA, D])
    keypen = dt_in("keypen", [128, NT])
    w_in = dt_in("w_in", [D, NIN])
    w_a = dt_in("w_a", [RW, D])
    w_b = dt_in("w_b", [RW, D])
    w_o = dt_in("w_o", [D, D])
    w_gu = dt_in("w_gu", [D, 2 * DFF])
    w_dn = dt_in("w_dn", [DFF, D])
    g1bc = dt_in("g1bc", [128, D])
    g2bc = dt_in("g2bc", [128, D])
    mu24 = dt_in("mu24", [128, 24])
    mul4 = dt_in("mul4", [128, 4])
    w0bc = dt_in("w0bc", [128, RW])
    w2 = dt_in("w2", [96, RW])
    a2 = dt_in("a2", [96, RW])
    g2 = dt_in("g2", [256, RW])
    vecp = dt_in("vecp", [128, 4, 8])
    gnwbc = dt_in("gnwbc", [128, RW])
    gnbbc = dt_in("gnbbc", [128, RW])
    qgbc = dt_in("qgbc", [128, RW])
    kgbc = dt_in("kgbc", [128, RW])
    fbbc = dt_in("fbbc", [128, 16])
    out = nc.dram_tensor("out", [TO, D], F32, kind="ExternalOutput").ap()

    ZR = scr("ZR", [3584, TA + 1], F32)
    ZFX = scr("ZFX", [TA, FC], F32)
    GT = scr("GT", [4096, TO], BF16)
    YAT = scr("YAT", [RW, TO], BF16)
    QTA = scr("QTA", [16, 67, TO], BF16)
    KTA = scr("KTA", [16, 67, TA], BF16)
    VFX = scr("VFX", [TA, RW], BF16)
    NB = scr("NB", [16, 128, NT], F32)
    YBT = scr("YBT", [RW, TO], BF16)
    H2 = scr("H2", [TO, D], F32)
    XN2T = scr("XN2T", [D, TO], BF16)
    ACTT = scr("ACTT", [DFF, TO], BF16)
    WDNB = scr("WDNB", [DFF, D], BF16)

    with ExitStack() as es:
        kb = KB(nc, es)
        es.enter_context(nc.Block())
        op, dma = kb.op, kb.dma

        ident = es.enter_context(SB("ident", [128, 128], BF16))
        identf = es.enter_context(SB("identf", [128, 128], F32))
        onesf = es.enter_context(SB("onesf", [128, 128], F32))
        zerof = es.enter_context(SB("zerof", [128, 32], F32))
        cst = es.enter_context(SB("cst", [128, 8], F32))
        mUs = es.enter_context(SB("mUs", [128, 128], F32))
        mUi = es.enter_context(SB("mUi", [128, 128], F32))
        mLs = es.enter_context(SB("mLs", [128, 128], F32))
        op("pool", lambda g: g.memset(identf[:], 1.0), w=["identf"])
        op("pool", lambda g: g.affine_select(out=identf[:], in_=identf[:], pattern=[[-1, 128]], compare_op=ALU.is_equal, fill=0.0, base=0, channel_multiplier=1), r=["identf"], w=["identf"])
        op("pool", lambda g: g.memset(onesf[:], 1.0), w=["onesf"])
        op("pool", lambda g: g.memset(zerof[:], 0.0), w=["zerof"])
        op("pool", lambda g: g.affine_select(out=mUs[:], in_=onesf[:], pattern=[[1, 128]], compare_op=ALU.is_gt, fill=0.0, base=0, channel_multiplier=-1), r=["onesf"], w=["mUs"])
        op("pool", lambda g: g.affine_select(out=mUi[:], in_=onesf[:], pattern=[[1, 128]], compare_op=ALU.is_ge, fill=0.0, base=0, channel_multiplier=-1), r=["onesf"], w=["mUi"])
        op("pool", lambda g: g.affine_select(out=mLs[:], in_=onesf[:], pattern=[[-1, 128]], compare_op=ALU.is_gt, fill=0.0, base=0, channel_multiplier=1), r=["onesf"], w=["mLs"])
        op("dve", lambda v: v.tensor_copy(out=ident[:], in_=identf[:]), r=["identf"], w=["ident"])
        for j, val in enumerate((1e-6, 64e-5, 1e-24, 1.0)):
            op("pool", lambda g, j=j, val=val: g.memset(cst[:, j:j + 1], val), w=[("cst", j)])
        kb.barrier()

        if "A" in STAGES:
            with ExitStack() as sa:
                xnT = sa.enter_context(SB("xnT", [128, 16, TP], BF16))
                g1t = sa.enter_context(SB("g1t", [128, D], F32))
                xt = [sa.enter_context(SB(f"xt{i}", [128, D], F32)) for i in range(2)]
                xnbs = [sa.enter_context(SB(f"xnb{i}", [128, D], BF16)) for i in range(2)]
                junk = sa.enter_context(SB("junk", [128, D], BF16))
                st1s = [sa.enter_context(SB(f"st1{i}", [128, 4], F32)) for i in range(2)]
                wf = [sa.enter_context(SB(f"wf{i}", [128, 16, 256], F32)) for i in range(2)]
                wb = [sa.enter_context(SB(f"wb{i}", [128, 16, 256], BF16)) for i in range(2)]
                og = [sa.enter_context(SB(f"og{i}", [128, 512], F32)) for i in range(4)]
                ogb = [sa.enter_context(SB(f"ogb{i}", [128, 512], BF16)) for i in range(2)]
                pst = [sa.enter_context(PSM(f"pst{i}", [128, 8, 128], BF16)) for i in range(2)]
                psg = [sa.enter_context(PSM(f"psg{i}", [128, 512], F32)) for i in range(4)]

                dma("sp", g1t[:], g1bc[:, :], w=["g1t"])
                dma("sp", ZR[:, 0:1].rearrange("(b p) o -> p b o", p=128), zerof[:, 0:28].rearrange("p (b o) -> p b o", o=1), r=["zerof"], w=[("ZRc0",)], slow=True)

                nst = sa.enter_context(SB("nst", [128, 3, NPT], F32))

                def norm_pass(tok0, ntiles):
                    op("dve", lambda v: v.memset(nst[:, 0, :], 0.0), w=[("nss", i) for i in range(ntiles)])
                    for i in range(ntiles):
                        b = i % 2
                        dma("sp", xt[b][:], xall[tok0 + i * 128: tok0 + (i + 1) * 128, :], w=[("xt", b)])
                        op("act", lambda a: a.activation(out=junk[:], in_=xt[b][:], func=AF.Square, accum_out=nst[:, 0, i:i + 1]), r=[("xt", b), ("nss", i)], w=[("nss", i)])
                    allk = [("nss", i) for i in range(ntiles)]
                    op("act", lambda a: a.activation(out=nst[:, 1, 0:ntiles], in_=nst[:, 0, 0:ntiles], func=AF.Ln, bias=cst[:, 0:1], scale=1.0 / D), r=allk, w=["nln"])
                    op("act", lambda a: a.activation(out=nst[:, 2, 0:ntiles], in_=nst[:, 1, 0:ntiles], func=AF.Exp, scale=-0.5), r=["nln"], w=["nrs"])
                    for i in range(ntiles):
                        b = i % 2
                        xnb = xnbs[b]
                        dma("sp", xt[b][:], xall[tok0 + i * 128: tok0 + (i + 1) * 128, :], w=[("xt", b)])
                        op("dve", lambda v: v.scalar_tensor_tensor(out=xnb[:], in0=xt[b][:], scalar=nst[:, 2, i:i + 1], in1=g1t[:], op0=ALU.mult, op1=ALU.mult), r=[("xt", b), "nrs", "g1t"], w=[("xnb", b)])
                        for hh in range(2):
                            for c in range(8):
                                kc = hh * 8 + c
                                op("pe", lambda t: t.transpose(out=pst[hh][:, c, :], in_=xnb[:, kc * 128:(kc + 1) * 128], identity=ident[:]), r=[("xnb", b), "ident"], w=[("pst", hh)])
                            if hh == 0:
                                op("act", lambda a: a.copy(out=xnT[:, hh * 8:(hh + 1) * 8, i * 128:(i + 1) * 128], in_=pst[hh][:]), r=[("pst", hh)], w=[("xnT", i, hh)])
                            else:
                                op("dve", lambda v: v.tensor_copy(out=xnT[:, hh * 8:(hh + 1) * 8, i * 128:(i + 1) * 128], in_=pst[hh][:]), r=[("pst", hh)], w=[("xnT", i, hh)])

                state = {"ld": 0, "pg": 0, "og": 0, "ogb": 0, "ev": 0}

                def load_w(c0, w):
                    b = state["ld"] % 2
                    state["ld"] += 1
                    for q4 in range(4):
                        dma("sp", wf[b][:, q4 * 4:(q4 + 1) * 4, 0:w], w_in[q4 * 512:(q4 + 1) * 512, c0:c0 + w].rearrange("(kc p) n -> p kc n", p=128), w=[("wf", b, q4)])
                    for q4 in range(4):
                        op("pool", lambda g: g.tensor_copy(out=wb[b][:, q4 * 4:(q4 + 1) * 4, 0:w], in_=wf[b][:, q4 * 4:(q4 + 1) * 4, 0:w]), r=[("wf", b, q4)], w=[("wb", b, q4)])
                    return b

                def evac_copy(dst, src, rk, wk):
                    state["ev"] += 1
                    if state["ev"] % 2:
                        op("act", lambda a: a.copy(out=dst, in_=src), r=rk, w=wk)
                    else:
                        op("dve", lambda v: v.tensor_copy(out=dst, in_=src), r=rk, w=wk)

                def fm_job(b, c0, w, t0, t1, ntile_used, kind):
                    for sub in range(0, w, 128):
                        m = min(128, w - sub)
                        for ts in range(t0, t1, 512):
                            n = min(512, t1 - ts)
                            pi = state["pg"] % 4
                            state["pg"] += 1
                            for kc in range(16):
                                op("pe", lambda t: t.matmul(psg[pi][0:m, 0:n], lhsT=wb[b][:, kc, sub:sub + m], rhs=xnT[:, kc, ts:ts + n], start=(kc == 0), stop=(kc == 15)),
                                   r=[("wb", b, kc // 4)] + [("xnT", ti, kc // 8) for ti in range(ts // 128, (ts + n + 127) // 128)], w=[("psg", pi)])
                            col = c0 + sub
                            if kind == "rw":
                                oi = state["og"] % 4
                                state["og"] += 1
                                evac_copy(og[oi][0:m, 0:n], psg[pi][0:m, 0:n], [("psg", pi)], [("og", oi)])
                                dma("sp", ZR[col:col + m, 1 + ntile_used + ts: 1 + ntile_used + ts + n], og[oi][0:m, 0:n], r=[("og", oi)], w=[("ZR", col, ntile_used + ts)])
                            else:
                                oi = state["ogb"] % 2
                                state["ogb"] += 1
                                op("act", lambda a: a.activation(out=ogb[oi][0:m, 0:n], in_=psg[pi][0:m, 0:n], func=AF.Sigmoid), r=[("psg", pi)], w=[("ogb", oi)])
                                dma("sp", GT[col - 6608:col - 6608 + m, ts:ts + n], ogb[oi][0:m, 0:n], r=[("ogb", oi)], w=[("GT", col, ts)])

                def tm_job(b, c0, w, ntiles, tokbase):
                    for i in range(ntiles):
                        pi = state["pg"] % 4
                        state["pg"] += 1
                        for kc in range(16):
                            op("pe", lambda t: t.matmul(psg[pi][:, 0:w], lhsT=xnT[:, kc, i * 128:(i + 1) * 128], rhs=wb[b][:, kc, 0:w], start=(kc == 0), stop=(kc == 15)),
                               r=[("wb", b, kc // 4), ("xnT", i, kc // 8)], w=[("psg", pi)])
                        oi = state["og"] % 4
                        state["og"] += 1
                        evac_copy(og[oi][:, 0:w], psg[pi][:, 0:w], [("psg", pi)], [("og", oi)])
                        dma("sp", ZFX[tokbase + i * 128: tokbase + (i + 1) * 128, c0 - RC:c0 - RC + w], og[oi][:, 0:w], r=[("og", oi)], w=[("ZFX", c0, tokbase + i)])

                def run_jobs(jobs):
                    nxt = load_w(jobs[0][0], jobs[0][1])
                    for ji, (c0, w, fn) in enumerate(jobs):
                        cur = nxt
                        if ji + 1 < len(jobs):
                            nxt = load_w(jobs[ji + 1][0], jobs[ji + 1][1])
                        fn(cur)

                def col_blocks(lo, hi):
                    return [(c, min(256, hi - c)) for c in range(lo, hi, 256)]

                norm_pass(0, NPT)
                jobs = []
                for (c0, w) in col_blocks(1024, 3264):
                    jobs.append((c0, w, lambda b, c0=c0, w=w: fm_job(b, c0, w, 0, TP, 0, "rw")))
                for (c0, w) in col_blocks(0, 1024) + col_blocks(3264, 3520):
                    jobs.append((c0, w, lambda b, c0=c0, w=w: fm_job(b, c0, w, TP - 128, TP, 0, "rw")))
                for (c0, w) in col_blocks(4544, 6608):
                    jobs.append((c0, w, lambda b, c0=c0, w=w: tm_job(b, c0, w, NPT, 0)))
                run_jobs(jobs)
                norm_pass(TP, NOT_)
                jobs = []
                for (c0, w) in col_blocks(0, 3520):
                    jobs.append((c0, w, lambda b, c0=c0, w=w: fm_job(b, c0, w, 0, TO, TP, "rw")))
                for (c0, w) in col_blocks(3520, 6608):
                    jobs.append((c0, w, lambda b, c0=c0, w=w: tm_job(b, c0, w, NOT_, TP)))
                for (c0, w) in col_blocks(6608, NIN):
                    jobs.append((c0, w, lambda b, c0=c0, w=w: fm_job(b, c0, w, 0, TO, TP, "gt")))
                run_jobs(jobs)
                kb.barrier()

        if "B" in STAGES:
            with ExitStack() as sb:
                T = lambda n, s, d=F32: sb.enter_context(SB(n, list(s), d))
                PS = lambda n, s, d=F32: sb.enter_context(PSM(n, list(s), d))
                mask4 = T("mask4", [128, 512])
                maskL4 = T("maskL4", [128, 512])
                identb4 = T("identb4", [128, 512], BF16)
                tri2 = T("tri2", [128, 256])
                bones = T("bones", [128, 128], BF16)
                hsel = T("hsel", [128, 2], BF16)
                mu24t = T("mu24t", [128, 24]); mul4t = T("mul4t", [128, 4])
                w0t = T("w0t", [128, RW]); vp = T("vp", [128, 4, 8])
                gnwt = T("gnwt", [128, RW]); gnbt = T("gnbt", [128, RW])
                w2b = T("w2b", [96, RW], BF16); a2b = T("a2b", [96, RW], BF16); g2b = T("g2b", [128, 2, RW], BF16)
                zin = T("zin", [128, 24, 129]); zl = T("zl", [128, 4, 129])
                zs = T("zs", [128, 24, 128]); zls = T("zls", [128, 4, 128])
                lt2 = [T(f"lt{i}", [128, 4, 128], BF16) for i in range(2)]
                gam2 = [T(f"gam{i}", [128, 8]) for i in range(2)]
                sgT = T("sgT", [128, RW])
                e1 = T("e1", [128, 8, 128]); e2 = T("e2", [128, 8, 128]); e3 = T("e3", [128, 8, 128])
                alpha = T("alpha", [128, 8, 128])
                kkr = T("kkr", [128, 8, 128]); sqb = T("sqb", [128, 8, 128], BF16)
                rn = T("rn", [128, 8, 128]); kk = T("kk", [128, 8, 128])
                t1 = T("t1", [128, 8, 128]); kf = T("kf", [128, 8, 128]); bb = T("bb", [128, 8, 128])
                AR2 = [T(f"AR{i}", [128, 8, 2, 128], BF16) for i in range(2)]
                KB2 = T("KB2", [128, 8, 2, 128], BF16)
                prod2 = [T(f"prod{i}", [128, 8, 128], BF16) for i in range(2)]
                vsb = T("vsb", [128, 8, 128], BF16)
                Vtok2 = [T(f"Vtok{i}", [128, RW], BF16) for i in range(2)]
                KBtok2 = [T(f"KBtok{i}", [128, 2, RW], BF16) for i in range(2)]
                A4 = T("A4", [128, 16, 512], BF16)
                Nj = [T(f"Nj{i}", [128, 16, 128], BF16) for i in range(2)]
                Mj = [T(f"Mj{i}", [128, 16, 128], BF16) for i in range(2)]
                Qj = [T(f"Qj{i}", [128, 16, 128], BF16) for i in range(2)]
                Hf = T("Hf", [128, 8, 64]); Hb = T("Hb", [128, 8, 64], BF16)
                Xb = T("Xb", [128, RW], BF16); Ub = T("Ub", [128, RW], BF16)
                ysb = T("ysb", [128, 16, 64]); ysq = T("ysq", [128, 16, 64])
                gst = T("gst", [128, 6, 16])
                bon = T("bon", [128, 16])
                yab = T("yab", [128, RW], BF16)
                yaT = T("yaT", [128, 8, 128], BF16)
                psA = [PS(f"psA{i}", [128, 512]) for i in range(4)]
                psB = [PS(f"psB{i}", [128, 512]) for i in range(2)]
                psTb = [PS(f"psTb{i}", [128, 8, 128], BF16) for i in range(2)]

                for j in range(4):
                    src = mUs if j % 2 == 0 else mUi
                    op("dve", lambda v: v.tensor_copy(out=mask4[:, j * 128:(j + 1) * 128], in_=src[:]), w=[("mask4", j)])
                    op("dve", lambda v: v.tensor_copy(out=maskL4[:, j * 128:(j + 1) * 128], in_=mLs[:]), w=[("maskL4", j)])
                    op("dve", lambda v: v.tensor_copy(out=identb4[:, j * 128:(j + 1) * 128], in_=identf[:]), w=[("identb4", j)])
                op("dve", lambda v: v.tensor_scalar(out=tri2[:, 0:128], in0=mUi[:], scalar1=-DECAY_C, scalar2=None, op0=ALU.mult), w=[("tri2", 0)])
                op("dve", lambda v: v.tensor_scalar(out=tri2[:, 128:256], in0=mUs[:], scalar1=-DECAY_C, scalar2=None, op0=ALU.mult), w=[("tri2", 1)])
                op("pool", lambda g: g.memset(bones[:], 0.0), w=["bones"])
                op("pool", lambda g: g.memset(bones[0:64, 0:64], 1.0), r=["bones"], w=["bones"])
                op("pool", lambda g: g.memset(bones[64:128, 64:128], 1.0), r=["bones"], w=["bones"])
                op("pool", lambda g: g.memset(hsel[:], 0.0), w=["hsel"])
                op("pool", lambda g: g.memset(hsel[0:64, 0:1], 1.0), r=["hsel"], w=["hsel"])
                op("pool", lambda g: g.memset(hsel[64:128, 1:2], 1.0), r=["hsel"], w=["hsel"])
                dma("sp", mu24t[:], mu24[:, :], w=["mu24t"]); dma("sp", mul4t[:], mul4[:, :], w=["mul4t"])
                dma("sp", w0t[:], w0bc[:, :], w=["w0t"]); dma("sp", vp[:], vecp[:, :, :], w=["vp"])
                dma("sp", gnwt[:], gnwbc[:, :], w=["gnwt"]); dma("sp", gnbt[:], gnbbc[:, :], w=["gnbt"])
                ldf = zs[:, 0:16, :].rearrange("p (c a) t -> p c (a t)", c=2)
                dma("sp", ldf[0:96, 0, :], w2[:, :], w=[("zs", 0), ("zs", 8)])
                op("dve", lambda v: v.tensor_copy(out=w2b[:], in_=ldf[0:96, 0, :]), r=[("zs", 0), ("zs", 8)], w=["w2b"])
                dma("sp", ldf[0:96, 1, :], a2[:, :], r=[("zs", 0), ("zs", 8)], w=[("zs", 0), ("zs", 8)])
                op("dve", lambda v: v.tensor_copy(out=a2b[:], in_=ldf[0:96, 1, :]), r=[("zs", 0), ("zs", 8)], w=["a2b"])
                dma("sp", ldf[:, :, :], g2.rearrange("(c p) n -> p c n", p=128), r=[("zs", 0), ("zs", 8)], w=[("zs", 0), ("zs", 8)])
                op("dve", lambda v: v.tensor_copy(out=g2b[:], in_=ldf[:, :, :]), r=[("zs", 0), ("zs", 8)], w=["g2b"])
                for i in range(2):
                    op("pool", lambda g: g.memset(AR2[i][:], 0.0), w=[("ARa", i), ("ARr", i)])
                op("dve", lambda v: v.memset(Hf[:], 0.0), w=["Hf"])
                op("dve", lambda v: v.memset(Hb[:], 0.0), w=["Hb"])
                ZR3 = ZR.rearrange("(b p) t -> p b t", p=128)

                BT = [int(t) for t in os.environ.get("KBTILES", "").split(",") if t] or list(range(NT))
                hb = lambda h: (h // 2, 64 * (h % 2))
                ek = [("e1", b_) for b_ in range(8)]; e2k = [("e2", b_) for b_ in range(8)]; e3k = [("e3", b_) for b_ in range(8)]
                alk = [("alpha", b_) for b_ in range(8)]
                kkrk = [("kkr", b_) for b_ in range(8)]
                t1k = [("t1", b_) for b_ in range(8)]

                def prep(ti, p):
                    own = ti >= NPT
                    AR, lt, gam, prod, Vtok, KBtok = AR2[p], lt2[p], gam2[p], prod2[p], Vtok2[p], KBtok2[p]
                    c0 = ti * 128
                    blo = 0 if own else 8
                    dma("sp", zin[:, blo:24, :], ZR3[:, blo:24, c0:c0 + 129], w=["zin"])
                    nl = 4 if own else 2
                    dma("sp", zl[0:96, 0, :], ZR[3072:3168, c0:c0 + 129], w=["zl0"])
                    dma("sp", zl[0:96, 1, :], ZR[3168:3264, c0:c0 + 129], w=["zl1"])
                    if own:
                        dma("sp", zl[:, 2:4, :], ZR[3264:3520, c0:c0 + 129].rearrange("(b p) t -> p b t", p=128), w=["zl2"])
                    yield
                    for (g0, g1) in ((8, 16), (16, 24), (0, 8)):
                        if g1 <= blo:
                            continue
                        zk = ("zs", g0)
                        op("pool", lambda g: g.tensor_tensor(out=zs[:, g0:g1, :], in0=zin[:, g0:g1, 0:128], in1=zin[:, g0:g1, 1:129], op=ALU.subtract), r=["zin"], w=[zk])
                        op("pool", lambda g: g.tensor_tensor(out=zs[:, g0:g1, :], in0=zs[:, g0:g1, :], in1=bcast_last(mu24t[:, g0:g1], 128), op=ALU.mult), r=[zk, "mu24t"], w=[zk])
                        op("pool", lambda g: g.tensor_tensor(out=zs[:, g0:g1, :], in0=zs[:, g0:g1, :], in1=zin[:, g0:g1, 1:129], op=ALU.add), r=[zk, "zin"], w=[zk])
                        yield
                    zK, zV, zR = ("zs", 8), ("zs", 16), ("zs", 0)
                    for j in range(nl):
                        pr = 96 if j < 2 else 128
                        zk = "zl%d" % min(j, 2)
                        op("dve", lambda v: v.tensor_tensor(out=zls[0:pr, j, :], in0=zl[0:pr, j, 0:128], in1=zl[0:pr, j, 1:129], op=ALU.subtract), r=[zk], w=[("zls", j)])
                        op("dve", lambda v: v.scalar_tensor_tensor(out=zls[0:pr, j, :], in0=zls[0:pr, j, :], scalar=mul4t[0:pr, j:j + 1], in1=zl[0:pr, j, 1:129], op0=ALU.mult, op1=ALU.add), r=[("zls", j), zk, "mul4t"], w=[("zls", j)])
                    yield
                    op("act", lambda a: a.activation(out=lt[0:96, 0, :], in_=zls[0:96, 0, :], func=AF.Tanh), r=[("zls", 0)], w=[("lt", 0, p)])
                    op("act", lambda a: a.copy(out=lt[0:96, 1, :], in_=zls[0:96, 1, :]), r=[("zls", 1)], w=[("lt", 1, p)])
                    if own:
                        op("act", lambda a: a.activation(out=lt[:, 2:4, :], in_=zls[:, 2:4, :], func=AF.Sigmoid), r=[("zls", 2), ("zls", 3)], w=[("lt", 2, p)])
                    yield
                    for hh in range(2):
                        op("pe", lambda t: t.matmul(psB[hh][:, :], lhsT=lt[0:96, 0, :], rhs=w2b[:, hh * 512:(hh + 1) * 512], start=True, stop=True), r=[("lt", 0, p), "w2b"], w=[("psB", hh)])
                        op("dve", lambda v: v.tensor_tensor(out=sgT[:, hh * 512:(hh + 1) * 512], in0=psB[hh][:, :], in1=w0t[:, hh * 512:(hh + 1) * 512], op=ALU.add), r=[("psB", hh), "w0t"], w=[("sgT", hh)])
                        op("act", lambda a: a.activation(out=sgT[:, hh * 512:(hh + 1) * 512], in_=sgT[:, hh * 512:(hh + 1) * 512], func=AF.Sigmoid), r=[("sgT", hh)], w=[("sgT", hh)])
                        yield
                    for hh in range(2):
                        for c in range(4):
                            blk = hh * 4 + c
                            op("pe", lambda t: t.matmul(psB[hh][:, c * 128:(c + 1) * 128], lhsT=a2b[:, blk * 128:(blk + 1) * 128], rhs=lt[0:96, 1, :], start=True, stop=True), r=[("lt", 1, p), "a2b"], w=[("psB", hh)])
                        for c in range(4):
                            blk = hh * 4 + c
                            op("act", lambda a: a.activation(out=alpha[:, blk, :], in_=psB[hh][:, c * 128:(c + 1) * 128], func=AF.Sigmoid, bias=vp[:, 0, blk:blk + 1]), r=[("psB", hh), "vp"], w=[("alpha", blk)])
                        yield
                    for blk in range(8):
                        pi = blk % 2
                        op("pe", lambda t: t.matmul(psB[pi][:, 0:256], lhsT=sgT[:, blk * 128:(blk + 1) * 128], rhs=tri2[:, :], start=True, stop=True), r=[("sgT", blk // 4), ("tri2", 0), ("tri2", 1)], w=[("psB", pi)])
                        op("act", lambda a: a.activation(out=e1[:, blk, :], in_=psB[pi][:, 0:128], func=AF.Exp), r=[("psB", pi)], w=[("e1", blk)])
                        op("act", lambda a: a.activation(out=e2[:, blk, :], in_=psB[pi][:, 0:128], func=AF.Exp, scale=-1.0), r=[("psB", pi)], w=[("e2", blk)])
                        op("act", lambda a: a.activation(out=e3[:, blk, :], in_=psB[pi][:, 128:256], func=AF.Exp), r=[("psB", pi)], w=[("e3", blk)])
                        yield
                    op("act", lambda a: a.copy(out=gam[:], in_=e1[:, :, 127]), r=ek, w=[("gam", p)])
                    for blk in range(8):
                        op("dve", lambda v: v.tensor_scalar(out=kkr[:, blk, :], in0=zs[:, 8 + blk, :], scalar1=vp[:, 1, blk:blk + 1], scalar2=None, op0=ALU.mult), r=[zK, "vp"], w=[("kkr", blk)])
                    op("dve", lambda v: v.tensor_tensor(out=sqb[:], in0=kkr[:], in1=kkr[:], op=ALU.mult), r=kkrk, w=["sqb"])
                    yield
                    for hh in range(2):
                        for c in range(4):
                            blk = hh * 4 + c
                            op("pe", lambda t: t.matmul(psB[hh][:, c * 128:(c + 1) * 128], lhsT=bones[:], rhs=sqb[:, blk, :], start=True, stop=True), r=["sqb", "bones"], w=[("psB", hh)])
                        op("act", lambda a: a.activation(out=rn[:, hh * 4:(hh + 1) * 4, :], in_=psB[hh][:, :].rearrange("p (c t) -> p c t", c=4), func=AF.Ln, bias=cst[:, 2:3]), r=[("psB", hh)], w=[("rn", hh)])
                        op("act", lambda a: a.activation(out=rn[:, hh * 4:(hh + 1) * 4, :], in_=rn[:, hh * 4:(hh + 1) * 4, :], func=AF.Exp, scale=-0.5), r=[("rn", hh)], w=[("rn", hh)])
                        yield
                    op("dve", lambda v: v.tensor_tensor(out=kk[:], in0=kkr[:], in1=rn[:], op=ALU.mult), r=kkrk + [("rn", 0), ("rn", 1)], w=["kk"])
                    for blk in range(8):
                        op("dve", lambda v: v.tensor_scalar(out=t1[:, blk, :], in0=alpha[:, blk, :], scalar1=-1.0, scalar2=vp[:, 2, blk:blk + 1], op0=ALU.add, op1=ALU.mult), r=[("alpha", blk), "vp"], w=[("t1", blk)])
                    yield
                    op("dve", lambda v: v.scalar_tensor_tensor(out=kf[:], in0=t1[:], scalar=1.0, in1=zs[:, 8:16, :], op0=ALU.add, op1=ALU.mult), r=t1k + [zK], w=["kf"])
                    op("pool", lambda g: g.tensor_tensor(out=bb[:], in0=kk[:], in1=alpha[:], op=ALU.mult), r=["kk"] + alk, w=["bb"])
                    yield
                    op("dve", lambda v: v.scalar_tensor_tensor(out=AR[:, :, 0, :], in0=kk[:], scalar=-1.0, in1=e3[:], op0=ALU.mult, op1=ALU.mult), r=["kk"] + e3k, w=[("ARa", p)])
                    if own:
                        op("pool", lambda g: g.tensor_tensor(out=AR[:, :, 1, :], in0=zs[:, 0:8, :], in1=e1[:], op=ALU.mult), r=[zR] + ek, w=[("ARr", p)])
                    yield
                    op("dve", lambda v: v.tensor_tensor(out=KB2[:, :, 0, :], in0=kf[:], in1=e2[:], op=ALU.mult), r=["kf"] + e2k, w=["KBk"])
                    op("pool", lambda g: g.tensor_tensor(out=KB2[:, :, 1, :], in0=bb[:], in1=e2[:], op=ALU.mult), r=["bb"] + e2k, w=["KBb"])
                    op("act", lambda a: a.copy(out=vsb[:], in_=zs[:, 16:24, :]), r=[zV], w=["vsb"])
                    yield
                    for c in range(8):
                        op("pe", lambda t: t.transpose(out=psTb[0][:, c, :], in_=vsb[:, c, :], identity=ident[:]), r=["vsb", "ident"], w=[("psTb", 0)])
                    op("act", lambda a: a.copy(out=Vtok[:], in_=psTb[0][:].rearrange("p c t -> p (c t)")), r=[("psTb", 0)], w=[("Vtok", p)])
                    yield
                    for j in range(2):
                        for c in range(8):
                            op("pe", lambda t: t.transpose(out=psTb[1][:, c, :], in_=KB2[:, c, j, :], identity=ident[:]), r=["KBk", "KBb", "ident"], w=[("psTb", 1)])
                        op("dve", lambda v: v.tensor_copy(out=KBtok[:, j, :], in_=psTb[1][:].rearrange("p c t -> p (c t)")), r=[("psTb", 1)], w=[("KBtok", j, p)])
                        yield
                    if own:
                        for blk in range(8):
                            op("dve", lambda v: v.scalar_tensor_tensor(out=prod[:, blk, :], in0=zs[:, blk, :], scalar=vp[:, 3, blk:blk + 1], in1=kf[:, blk, :], op0=ALU.mult, op1=ALU.mult), r=[zR, "kf", "vp"], w=[("prod", blk, p)])
                        yield

                def mm(ti, p):
                    AR = AR2[p]
                    for h in range(16):
                        blk, pb = hb(h)
                        pi = h % 4
                        rd = [("ARa", p), "KBk", "KBb", ("ARr", p)]
                        op("pe", lambda t: t.matmul(psA[pi][:, 0:256], lhsT=KB2[pb:pb + 64, blk, 1, :], rhs=AR[pb:pb + 64, blk, :, :].rearrange("p a t -> p (a t)"), start=True, stop=True), r=rd, w=[("psA", pi)])
                        op("pe", lambda t: t.matmul(psA[pi][:, 256:512], lhsT=KB2[pb:pb + 64, blk, 0, :], rhs=AR[pb:pb + 64, blk, :, :].rearrange("p a t -> p (a t)"), start=True, stop=True), r=rd, w=[("psA", pi)])
                        op("dve", lambda v: v.tensor_tensor(out=A4[:, h, :], in0=psA[pi][:, :], in1=mask4[:], op=ALU.mult), r=[("psA", pi)] + [("mask4", j) for j in range(4)], w=[("A4", h)])
                    for hg in range(4):
                        hk = [("A4", hg * 4 + i) for i in range(4)]
                        op("pool", lambda g: g.tensor_copy(out=Mj[0][:, hg * 4:(hg + 1) * 4, :], in_=A4[:, hg * 4:(hg + 1) * 4, 0:128]), r=hk, w=[("M0", hg)])
                        op("pool", lambda g: g.tensor_tensor(out=Qj[0][:, hg * 4:(hg + 1) * 4, :], in0=A4[:, hg * 4:(hg + 1) * 4, 0:128], in1=identb4[:].rearrange("p (h t) -> p h t", h=4), op=ALU.add), r=hk + [("identb4", j) for j in range(4)], w=[("Q0", hg)])
                    for g8 in range(2):
                        for par in range(2):
                            for i in range(4):
                                h = g8 * 8 + 2 * i + par
                                blk, pb = hb(h)
                                op("pe", lambda t: t.matmul(psA[2 + par][:, i * 128:(i + 1) * 128], lhsT=AR[pb:pb + 64, blk, 0, :], rhs=KB2[pb:pb + 64, blk, 1, :], start=True, stop=True), r=[("ARa", p), "KBb"], w=[("psA", 2 + par)])
                        for par in range(2):
                            op("dve", lambda v: v.tensor_tensor(out=Nj[0][:, g8 * 8 + par:g8 * 8 + 8:2, :], in0=psA[2 + par][:, :].rearrange("p (h t) -> p h t", h=4), in1=maskL4[:].rearrange("p (h t) -> p h t", h=4), op=ALU.mult), r=[("psA", 2 + par)] + [("maskL4", j) for j in range(4)], w=[("N0", 2 * g8), ("N0", 2 * g8 + 1)])

                def inverse(ti):
                    bank = [0]
                    def nb():
                        bank[0] += 1
                        return bank[0] % 4
                    for lvl in range(6):
                        ci, ni = lvl % 2, (lvl + 1) % 2
                        for hg in range(4):
                            pi = nb()
                            for i in range(4):
                                h = hg * 4 + i
                                op("pe", lambda t: t.matmul(psA[pi][:, i * 128:(i + 1) * 128], lhsT=Mj[ci][:, h, :], rhs=Nj[ci][:, h, :], start=True, stop=True), r=[("M%d" % ci, hg), ("N%d" % ci, hg)], w=[("psA", pi)])
                            op("act", lambda a: a.copy(out=Nj[ni][:, hg * 4:(hg + 1) * 4, :].rearrange("p h t -> p (h t)"), in_=psA[pi][:, :]), r=[("psA", pi)], w=[("N%d" % ni, hg)])
                            yield
                        if lvl < 5:
                            for hg in range(4):
                                pi = nb()
                                for i in range(4):
                                    h = hg * 4 + i
                                    op("pe", lambda t: t.matmul(psA[pi][:, i * 128:(i + 1) * 128], lhsT=Nj[ci][:, h, :], rhs=Mj[ci][:, h, :], start=True, stop=True), r=[("M%d" % ci, hg), ("N%d" % ci, hg)], w=[("psA", pi)])
                                op("act", lambda a: a.copy(out=Mj[ni][:, hg * 4:(hg + 1) * 4, :].rearrange("p h t -> p (h t)"), in_=psA[pi][:, :]), r=[("psA", pi)], w=[("M%d" % ni, hg)])
                                yield
                        for hg in range(4):
                            pi = nb()
                            for i in range(4):
                                h = hg * 4 + i
                                op("pe", lambda t: t.matmul(psA[pi][:, i * 128:(i + 1) * 128], lhsT=Nj[ni][:, h, :], rhs=Qj[ci][:, h, :], start=True, stop=True), r=[("N%d" % ni, hg), ("Q%d" % ci, hg)], w=[("psA", pi)])
                            op("dve", lambda v: v.tensor_tensor(out=Qj[ni][:, hg * 4:(hg + 1) * 4, :].rearrange("p h t -> p (h t)"), in0=psA[pi][:, :], in1=Qj[ci][:, hg * 4:(hg + 1) * 4, :].rearrange("p h t -> p (h t)"), op=ALU.add), r=[("psA", pi), ("Q%d" % ci, hg)], w=[("Q%d" % ni, hg)])
                            yield

                def chain(ti, p):
                    own = ti >= NPT
                    AR, lt, gam, prod, Vtok, KBtok = AR2[p], lt2[p], gam2[p], prod2[p], Vtok2[p], KBtok2[p]
                    QT = Qj[0]
                    vtk = ("Vtok", p)
                    for hh in range(2):
                        for i in range(8):
                            h = hh * 8 + i
                            blk, pb = hb(h)
                            op("pe", lambda t: t.matmul(psB[hh][:, i * 64:(i + 1) * 64], lhsT=AR[pb:pb + 64, blk, 0, :], rhs=Hb[pb:pb + 64, blk, :], start=True, stop=False), r=[("ARa", p), "Hb"], w=[("psB", hh)])
                            op("pe", lambda t: t.matmul(psB[hh][:, i * 64:(i + 1) * 64], lhsT=A4[:, h, 256:384], rhs=Vtok[:, h * 64:(h + 1) * 64], start=False, stop=True), r=[("A4", h), vtk], w=[("psB", hh)])
                        if hh == 0:
                            op("act", lambda a: a.copy(out=Xb[:, 0:512], in_=psB[0][:, :]), r=[("psB", 0)], w=[("Xb", 0)])
                        else:
                            op("dve", lambda v: v.tensor_copy(out=Xb[:, 512:1024], in_=psB[1][:, :]), r=[("psB", 1)], w=[("Xb", 1)])
                    for hh in range(2):
                        for i in range(8):
                            h = hh * 8 + i
                            op("pe", lambda t: t.matmul(psB[hh][:, i * 64:(i + 1) * 64], lhsT=QT[:, h, :], rhs=Xb[:, h * 64:(h + 1) * 64], start=True, stop=True), r=[("Q0", h // 4), ("Xb", hh)], w=[("psB", hh)])
                        if hh == 0:
                            op("act", lambda a: a.copy(out=Ub[:, 0:512], in_=psB[0][:, :]), r=[("psB", 0)], w=[("Ub", 0)])
                        else:
                            op("dve", lambda v: v.tensor_copy(out=Ub[:, 512:1024], in_=psB[1][:, :]), r=[("psB", 1)], w=[("Ub", 1)])
                    if own:
                        for hh in range(2):
                            pi = 2 + hh
                            for i in range(8):
                                h = hh * 8 + i
                                blk, pb = hb(h)
                                op("pe", lambda t: t.matmul(psA[pi][:, i * 64:(i + 1) * 64], lhsT=AR[pb:pb + 64, blk, 1, :], rhs=Hb[pb:pb + 64, blk, :], start=True, stop=False), r=[("ARr", p), "Hb"], w=[("psA", pi)])
                                op("pe", lambda t: t.matmul(psA[pi][:, i * 64:(i + 1) * 64], lhsT=A4[:, h, 128:256], rhs=Ub[:, h * 64:(h + 1) * 64], start=False, stop=False), r=[("A4", h), ("Ub", hh)], w=[("psA", pi)])
                                op("pe", lambda t: t.matmul(psA[pi][:, i * 64:(i + 1) * 64], lhsT=A4[:, h, 384:512], rhs=Vtok[:, h * 64:(h + 1) * 64], start=False, stop=True), r=[("A4", h), vtk], w=[("psA", pi)])
                            op("act", lambda a: a.copy(out=ysb[:, hh * 8:(hh + 1) * 8, :].rearrange("p h v -> p (h v)"), in_=psA[pi][:, :]), r=[("psA", pi)], w=[("ysb", hh)])
                    for blk in range(8):
                        for half in range(2):
                            h = blk * 2 + half
                            pb = 64 * half
                            op("pe", lambda t: t.matmul(psA[0][pb:pb + 64, blk * 64:(blk + 1) * 64], lhsT=KBtok[:, 0, blk * 128 + pb: blk * 128 + pb + 64], rhs=Vtok[:, h * 64:(h + 1) * 64], start=True, stop=False), r=[("KBtok", 0, p), vtk], w=[("psA", 0)])
                            op("pe", lambda t: t.matmul(psA[0][pb:pb + 64, blk * 64:(blk + 1) * 64], lhsT=KBtok[:, 1, blk * 128 + pb: blk * 128 + pb + 64], rhs=Ub[:, h * 64:(h + 1) * 64], start=False, stop=True), r=[("KBtok", 1, p), ("Ub", h // 8)], w=[("psA", 0)])
                    op("dve", lambda v: v.tensor_tensor(out=Hf[:].rearrange("p b v -> p (b v)"), in0=psA[0][:, :], in1=Hf[:].rearrange("p b v -> p (b v)"), op=ALU.add), r=[("psA", 0), "Hf"], w=["Hf"])
                    op("dve", lambda v: v.tensor_tensor(out=Hf[:], in0=Hf[:], in1=bcast_last(gam[:, :], 64), op=ALU.mult), r=["Hf", ("gam", p)], w=["Hf"])
                    op("act", lambda a: a.copy(out=Hb[:], in_=Hf[:]), r=["Hf"], w=["Hb"])
                    if own:
                        oi = ti - NPT
                        yk = [("ysb", 0), ("ysb", 1)]
                        for hh in range(2):
                            for kc in range(2):
                                op("pe", lambda t: t.matmul(psB[hh][:, :], lhsT=lt[:, 2 + kc, :], rhs=g2b[:, kc, hh * 512:(hh + 1) * 512], start=(kc == 0), stop=(kc == 1)), r=[("lt", 2, p), "g2b"], w=[("psB", hh)])
                        for blk in range(8):
                            op("pe", lambda t: t.matmul(psA[1][:, blk * 2:(blk + 1) * 2], lhsT=prod[:, blk, :], rhs=hsel[:], start=True, stop=True), r=[("prod", blk, p), "hsel"], w=[("psA", 1)])
                        op("act", lambda a: a.copy(out=bon[:], in_=psA[1][:, 0:16]), r=[("psA", 1)], w=["bon"])
                        op("dve", lambda v: v.tensor_reduce(out=gst[:, 0, :], in_=ysb[:], axis=AX.X, op=ALU.add), r=yk, w=[("gst", 0)])
                        op("pool", lambda g: g.tensor_tensor(out=ysq[:], in0=ysb[:], in1=ysb[:], op=ALU.mult), r=yk, w=["ysq"])
                        op("dve", lambda v: v.tensor_reduce(out=gst[:, 1, :], in_=ysq[:], axis=AX.X, op=ALU.add), r=["ysq"], w=[("gst", 1)])
                        op("dve", lambda v: v.tensor_scalar(out=gst[:, 2, :], in0=gst[:, 0, :], scalar1=1.0 / 64, scalar2=None, op0=ALU.mult), r=[("gst", 0)], w=[("gst", 2)])
                        op("dve", lambda v: v.tensor_tensor(out=gst[:, 3, :], in0=gst[:, 2, :], in1=gst[:, 2, :], op=ALU.mult), r=[("gst", 2)], w=[("gst", 3)])
                        op("dve", lambda v: v.scalar_tensor_tensor(out=gst[:, 4, :], in0=gst[:, 1, :], scalar=1.0 / 64, in1=gst[:, 3, :], op0=ALU.mult, op1=ALU.subtract), r=[("gst", 1), ("gst", 3)], w=[("gst", 4)])
                        op("act", lambda a: a.activation(out=gst[:, 5, :], in_=gst[:, 4, :], func=AF.Ln, bias=cst[:, 1:2]), r=[("gst", 4)], w=[("gst", 5)])
                        op("act", lambda a: a.activation(out=gst[:, 5, :], in_=gst[:, 5, :], func=AF.Exp, scale=-0.5), r=[("gst", 5)], w=[("gst", 5)])
                        op("dve", lambda v: v.tensor_tensor(out=ysq[:], in0=ysb[:], in1=bcast_last(gst[:, 2, :], 64), op=ALU.subtract), r=yk + [("gst", 2), "ysq"], w=["ysq"])
                        op("dve", lambda v: v.tensor_tensor(out=ysq[:], in0=ysq[:], in1=bcast_last(gst[:, 5, :], 64), op=ALU.mult), r=["ysq", ("gst", 5)], w=["ysq"])
                        ysq2 = ysq[:].rearrange("p h v -> p (h v)")
                        op("dve", lambda v: v.tensor_tensor(out=ysq2, in0=ysq2, in1=gnwt[:], op=ALU.mult), r=["ysq", "gnwt"], w=["ysq"])
                        op("dve", lambda v: v.tensor_tensor(out=ysq2, in0=ysq2, in1=gnbt[:], op=ALU.add), r=["ysq", "gnbt"], w=["ysq"])
                        op("dve", lambda g: g.tensor_tensor(out=ysb[:], in0=Vtok[:].rearrange("p (h v) -> p h v", h=16), in1=bcast_last(bon[:, :], 64), op=ALU.mult), r=[vtk, "bon"] + yk, w=yk)
                        op("dve", lambda v: v.tensor_tensor(out=ysq[:], in0=ysq[:], in1=ysb[:], op=ALU.add), r=["ysq"] + yk, w=["ysq"])
                        for hh in range(2):
                            op("dve", lambda v: v.tensor_tensor(out=yab[:, hh * 512:(hh + 1) * 512], in0=ysq[:, hh * 8:(hh + 1) * 8, :].rearrange("p h v -> p (h v)"), in1=psB[hh][:, :], op=ALU.mult), r=["ysq", ("psB", hh)], w=[("yab", hh)])
                        for c in range(8):
                            op("pe", lambda t: t.transpose(out=psTb[0][:, c, :], in_=yab[:, c * 128:(c + 1) * 128], identity=ident[:]), r=[("yab", c // 4), "ident"], w=[("psTb", 0)])
                        op("act", lambda a: a.copy(out=yaT[:], in_=psTb[0][:]), r=[("psTb", 0)], w=["yaT"])
                        dma("sp", YAT.rearrange("(c p) t -> p c t", p=128)[:, :, oi * 128:(oi + 1) * 128], yaT[:], r=["yaT"], w=[("YAT", oi)])

                def run_rr(gens):
                    gens = list(gens)
                    while gens:
                        for g_ in list(gens):
                            try:
                                next(g_)
                            except StopIteration:
                                gens.remove(g_)

                run_rr([prep(BT[0], 0)])
                for k_, ti in enumerate(BT):
                    p = k_ % 2
                    mm(ti, p)
                    gens = [inverse(ti)]
                    if k_ + 1 < len(BT):
                        gens.append(prep(BT[k_ + 1], 1 - p))
                    run_rr(gens)
                    chain(ti, p)
                kb.barrier()


        nbt = es.enter_context(SB("nbt", [128, 16, NT], F32))
        if "C" in STAGES:
            with ExitStack() as sc:
                T = lambda n, s, d=F32: sc.enter_context(SB(n, list(s), d))
                PS = lambda n, s, d=F32: sc.enter_context(PSM(n, list(s), d))
                qgt = T("qgt", [128, RW]); kgt = T("kgt", [128, RW]); fbt = T("fbt", [128, 16]); kpt = T("kpt", [128, NT])
                carry = T("carry", [128, 16])
                cT = T("cT", [16, TO]); r1 = T("r1", [16, TO]); chi = T("chi", [16, 3, TO], BF16)
                onesb = T("onesb", [16, 3, 1408], BF16)
                psT = [PS(f"psT{i}", [64, 16, 128], BF16) for i in range(2)]
                dma("sp", qgt[:], qgbc[:, :], w=["qgt"]); dma("sp", kgt[:], kgbc[:, :], w=["kgt"])
                dma("sp", fbt[:], fbbc[:, :], w=["fbt"]); dma("sp", kpt[:], keypen[:, :], w=["kpt"])
                op("dve", lambda v: v.tensor_scalar(out=qgt[:], in0=qgt[:], scalar1=0.125, scalar2=None, op0=ALU.mult), r=["qgt"], w=["qgt"])
                op("dve", lambda v: v.memset(carry[:], 0.0), w=["carry"])
                op("pool", lambda g: g.memset(onesb[:], 1.0), w=["onesb"])
                for c3 in range(3):
                    dma("sp", KTA[:, 64:67, c3 * 1408:(c3 + 1) * 1408], onesb[:], r=["onesb"], w=[("KTA1", c3)])

                NP_ = 3
                sqs = [T(f"sq{i}", [128, 16, 64]) for i in range(NP_)]; sts = [T(f"st{i}", [128, 4, 16]) for i in range(NP_)]
                nrms = [T(f"nrm{i}", [128, 16, 64]) for i in range(NP_)]
                sq2s = [T(f"sqq{i}", [128, 16, 64]) for i in range(NP_)]; st2s = [T(f"stq{i}", [128, 4, 16]) for i in range(NP_)]
                nrm2s = [T(f"nrmq{i}", [128, 16, 64]) for i in range(NP_)]
                zfs = [T(f"zfp{i}", [128, FC]) for i in range(NP_)]
                knbs = [T(f"knb{i}", [128, RW], BF16) for i in range(NP_)]; qnbs = [T(f"qnb{i}", [128, RW], BF16) for i in range(NP_)]
                vbs = [T(f"vb{i}", [128, RW], BF16) for i in range(NP_)]
                ktiles = [T(f"ktile{i}", [64, 16, 128], BF16) for i in range(NP_)]; qtiles = [T(f"qtile{i}", [64, 16, 128], BF16) for i in range(NP_)]
                fls = [T(f"fl{i}", [128, 4, 16]) for i in range(NP_)]; ctiles = [T(f"ctile{i}", [128, 16]) for i in range(NP_)]
                pscs = [PS(f"pscp{i}", [128, 512]) for i in range(2)]

                def rmsn(src, gt, dst, tag, dkey, sq, st, nrm, kk_):
                    s3 = src.rearrange("p (h v) -> p h v", h=16)
                    op("dve", lambda g: g.tensor_tensor(out=sq[:], in0=s3, in1=s3, op=ALU.mult), r=[tag], w=[("sq",) + kk_])
                    yield
                    op("dve", lambda v: v.tensor_reduce(out=st[:, 0, :], in_=sq[:], axis=AX.X, op=ALU.add), r=[("sq",) + kk_], w=[("st", 0) + kk_])
                    op("act", lambda a: a.activation(out=st[:, 1, :], in_=st[:, 0, :], func=AF.Ln, bias=cst[:, 0:1], scale=1.0 / 64), r=[("st", 0) + kk_], w=[("st", 1) + kk_])
                    op("act", lambda a: a.activation(out=st[:, 2, :], in_=st[:, 1, :], func=AF.Exp, scale=-0.5), r=[("st", 1) + kk_], w=[("st", 2) + kk_])
                    yield
                    op("dve", lambda v: v.tensor_tensor(out=nrm[:], in0=s3, in1=bcast_last(st[:, 2, :], 64), op=ALU.mult), r=[tag, ("st", 2) + kk_], w=[("nrm",) + kk_])
                    op("dve", lambda v: v.tensor_tensor(out=dst[:], in0=nrm[:].rearrange("p h v -> p (h v)"), in1=gt[:], op=ALU.mult), r=[("nrm",) + kk_, "qgt", "kgt"], w=[dkey])
                    yield

                def cprep(ti, p):
                    own = ti >= NPT
                    zfb = zfs[p]
                    zt = ("zf", p)
                    c_lo = 0 if own else 1024
                    dma("sp", zfb[:, c_lo:FC], ZFX[ti * 128:(ti + 1) * 128, c_lo:FC], w=[zt])
                    yield
                    fl = fls[p]; ctile = ctiles[p]; psc = pscs[p % 2]
                    op("dve", lambda v: v.tensor_tensor(out=fl[:, 0, :], in0=zfb[:, 3072:3088], in1=fbt[:], op=ALU.add), r=[zt, "fbt"], w=[("fl", 0, p)])
                    op("act", lambda a: a.activation(out=fl[:, 1, :], in_=fl[:, 0, :], func=AF.Exp, scale=-1.0), r=[("fl", 0, p)], w=[("fl", 1, p)])
                    op("act", lambda a: a.activation(out=fl[:, 2, :], in_=fl[:, 1, :], func=AF.Ln, bias=cst[:, 3:4]), r=[("fl", 1, p)], w=[("fl", 2, p)])
                    op("pe", lambda t: t.matmul(psc[:, 0:16], lhsT=mUi[:], rhs=fl[:, 2, :], start=True, stop=True), r=[("fl", 2, p)], w=[("psc", p % 2)])
                    op("pe", lambda t: t.matmul(psc[:, 16:32], lhsT=onesf[:], rhs=fl[:, 2, :], start=True, stop=True), r=[("fl", 2, p)], w=[("psc", p % 2)])
                    op("dve", lambda v: v.tensor_tensor(out=ctile[:], in0=carry[:], in1=psc[:, 0:16], op=ALU.subtract), r=["carry", ("psc", p % 2)], w=[("ctile", p)])
                    op("dve", lambda v: v.tensor_tensor(out=carry[:], in0=carry[:], in1=psc[:, 16:32], op=ALU.subtract), r=["carry", ("psc", p % 2)], w=["carry"])
                    op("dve", lambda v: v.tensor_scalar(out=nbt[:, :, ti], in0=ctile[:], scalar1=-1.0, scalar2=kpt[:, ti:ti + 1], op0=ALU.mult, op1=ALU.add), r=[("ctile", p), "kpt"], w=[("nbt", ti)])
                    if own:
                        oi = ti - NPT
                        op("pe", lambda t: t.transpose(out=psc[0:16, 128:256], in_=ctile[:], identity=identf[:]), r=[("ctile", p)], w=[("psc", p % 2)])
                        op("act", lambda a: a.copy(out=cT[:, oi * 128:(oi + 1) * 128], in_=psc[0:16, 128:256]), r=[("psc", p % 2)], w=[("cT", oi)])
                    yield
                    op("act", lambda g: g.copy(out=vbs[p][:], in_=zfb[:, 2048:3072]), r=[zt], w=[("vb", p)])
                    dma("sp", VFX[ti * 128:(ti + 1) * 128, :], vbs[p][:], r=[("vb", p)], w=[("VFX", ti)])
                    yield
                    yield from rmsn(zfb[:, 1024:2048], kgt, knbs[p], zt, ("knb", p), sqs[p], sts[p], nrms[p], ("k", p))
                    for hh in range(2):
                        for i in range(8):
                            h = hh * 8 + i
                            op("pe", lambda t: t.transpose(out=psT[0][:, h, :], in_=knbs[p][:, h * 64:(h + 1) * 64], identity=ident[:]), r=[("knb", p), "ident"], w=["psT0"])
                    op("act", lambda a: a.copy(out=ktiles[p][:], in_=psT[0][:]), r=["psT0"], w=[("ktile", p)])
                    dma("sp", KTA[:, 0:64, ti * 128:(ti + 1) * 128].rearrange("h r t -> r h t"), ktiles[p][:], r=[("ktile", p)], w=[("KTA", ti)])
                    yield
                    if own:
                        oi = ti - NPT
                        yield from rmsn(zfb[:, 0:1024], qgt, qnbs[p], zt, ("qnb", p), sq2s[p], st2s[p], nrm2s[p], ("q", p))
                        for h in range(16):
                            op("pe", lambda t: t.transpose(out=psT[1][:, h, :], in_=qnbs[p][:, h * 64:(h + 1) * 64], identity=ident[:]), r=[("qnb", p), "ident"], w=["psT1"])
                        op("act", lambda a: a.copy(out=qtiles[p][:], in_=psT[1][:]), r=["psT1"], w=[("qtile", p)])
                        dma("sp", QTA[:, 0:64, oi * 128:(oi + 1) * 128].rearrange("h r t -> r h t"), qtiles[p][:], r=[("qtile", p)], w=[("QTA", oi)])
                        yield

                live = []
                nxt_t = 0
                while live or nxt_t < NT:
                    while len(live) < NP_ and nxt_t < NT:
                        live.append(cprep(nxt_t, nxt_t % NP_))
                        nxt_t += 1
                    for g_ in list(live):
                        try:
                            next(g_)
                        except StopIteration:
                            live.remove(g_)
                ctk = [("cT", i) for i in range(NOT_)]
                op("dve", lambda v: v.tensor_copy(out=chi[:, 0, :], in_=cT[:]), r=ctk, w=[("chi", 0)])
                op("dve", lambda v: v.tensor_tensor(out=r1[:], in0=cT[:], in1=chi[:, 0, :], op=ALU.subtract), r=ctk + [("chi", 0)], w=["r1"])
                op("dve", lambda v: v.tensor_copy(out=chi[:, 1, :], in_=r1[:]), r=["r1"], w=[("chi", 1)])
                op("dve", lambda v: v.tensor_tensor(out=r1[:], in0=r1[:], in1=chi[:, 1, :], op=ALU.subtract), r=["r1", ("chi", 1)], w=["r1"])
                op("dve", lambda v: v.tensor_copy(out=chi[:, 2, :], in_=r1[:]), r=["r1"], w=[("chi", 2)])
                dma("sp", QTA[:, 64:67, :], chi[:], r=[("chi", 0), ("chi", 1), ("chi", 2)], w=["QTA1"])
                kb.barrier()

            with ExitStack() as sc:
                T = lambda n, s, d=F32: sc.enter_context(SB(n, list(s), d))
                PS = lambda n, s, d=F32: sc.enter_context(PSM(n, list(s), d))
                QT = [T(f"QT{i}", [67, TO], BF16) for i in range(2)]
                KT = [T(f"KT{i}", [67, TA], BF16) for i in range(2)]
                Vh = [T(f"Vh{i}", [128, NT, 65], BF16) for i in range(2)]
                NSB = 5
                pt = [T(f"pt{i}", [128, 512], BF16) for i in range(NSB)]
                recrs = [T(f"recr{i}", [128, 512]) for i in range(2)]; recb = T("recb", [64, 512])
                deferred = []
                ybo = [T(f"ybo{i}", [64, 512], BF16) for i in range(2)]
                pss = [PS(f"pss{i}", [128, 512]) for i in range(NSB)]
                pso = [PS(f"pso{i}", [128, 512]) for i in range(2)]
                psr = PS("psr", [128, 512])
                for i in range(2):
                    op("pool", lambda g: g.memset(Vh[i][:, :, 64:65], 1.0), w=[("Vh1", i)])
                cnt = {"s": 0, "o": 0}

                def load_head(h):
                    b = h % 2
                    dma("sp", QT[b][:], QTA[h, :, :], w=[("QT", b)])
                    dma("sp", KT[b][:], KTA[h, :, :], w=[("KT", b)])
                    for q4 in range(3):
                        t0, t1_ = q4 * 11, (q4 + 1) * 11
                        dma("sp", Vh[b][:, t0:t1_, 0:64], VFX[t0 * 128:t1_ * 128, h * 64:(h + 1) * 64].rearrange("(t p) v -> p t v", p=128), w=[("Vh", b, q4)])

                steps = []
                for h in range(16):
                    for qsb in range(4):
                        d0 = NPT + 4 * qsb
                        for kbi in range(d0 + 4):
                            steps.append((h, qsb, kbi, d0))

                def emit_S(idx):
                    h, qsb, kbi, d0 = steps[idx]
                    b = h % 2
                    j0 = max(0, kbi - d0)
                    ncol = 512 - 128 * j0
                    si = idx % NSB
                    op("pe", lambda t: t.matmul(pss[si][:, 0:ncol], lhsT=KT[b][:, kbi * 128:(kbi + 1) * 128], rhs=QT[b][:, qsb * 512 + j0 * 128:(qsb + 1) * 512], start=True, stop=True), r=[("QT", b), ("KT", b)], w=[("pss", si)])

                def emit_rest(idx):
                    h, qsb, kbi, d0 = steps[idx]
                    b = h % 2
                    if qsb == 0 and kbi == 0 and h + 1 < 16:
                        load_head(h + 1)
                    vk = [("Vh", b, q4) for q4 in range(3)] + [("Vh1", b)]
                    j0 = max(0, kbi - d0)
                    ncol = 512 - 128 * j0
                    si = idx % NSB
                    oi = (h * 4 + qsb) % 2
                    op("act", lambda a: a.activation(out=pt[si][:, 0:ncol], in_=pss[si][:, 0:ncol], func=AF.Exp, bias=nbt[:, h, kbi:kbi + 1]), r=[("pss", si)], w=[("pt", si)])
                    if kbi >= d0:
                        op("dve", lambda g: g.tensor_tensor(out=pt[si][:, 0:128], in0=pt[si][:, 0:128], in1=mUi[:], op=ALU.mult), r=[("pt", si)], w=[("pt", si)])
                    op("pe", lambda t: t.matmul(pso[oi][0:65, j0 * 128:512], lhsT=Vh[b][:, kbi, :], rhs=pt[si][:, 0:ncol], start=(kbi == 0), stop=(kbi == d0 + 3)), r=[("pt", si)] + vk, w=[("pso", oi)])
                    if kbi == d0 + 3:
                        rb = recrs[oi]
                        op("dve", lambda v: v.reciprocal(out=rb[64:65, :], in_=pso[oi][64:65, :]), r=[("pso", oi)], w=[("recr", oi)])

                        def tail(h=h, qsb=qsb, oi=oi, rb=rb):
                            op("pe", lambda t: t.matmul(psr[0:64, :], lhsT=onesf[64:65, 0:64], rhs=rb[64:65, :], start=True, stop=True), r=[("recr", oi)], w=["psr"])
                            op("act", lambda a: a.copy(out=recb[:], in_=psr[0:64, :]), r=["psr"], w=["recb"])
                            op("dve", lambda v: v.tensor_tensor(out=ybo[oi][:], in0=pso[oi][0:64, :], in1=recb[:], op=ALU.mult), r=[("pso", oi), "recb"], w=[("ybo", oi)])
                            dma("sp", YBT[h * 64:(h + 1) * 64, qsb * 512:(qsb + 1) * 512], ybo[oi][:], r=[("ybo", oi)], w=[("YBT", h, qsb)])
                        deferred.append([8, tail])

                load_head(0)
                LOOK = 4
                for idx in range(min(LOOK, len(steps))):
                    emit_S(idx)
                for idx in range(len(steps)):
                    emit_rest(idx)
                    if idx + LOOK < len(steps):
                        emit_S(idx + LOOK)
                    for d_ in list(deferred):
                        d_[0] -= 1
                        if d_[0] <= 0:
                            d_[1]()
                            deferred.remove(d_)
                for d_ in deferred:
                    d_[1]()
                kb.barrier()

        MT = scr("MT", [D, TO], BF16)
        if "D" in STAGES:
            with ExitStack() as sd:
                T = lambda n, s, d=F32: sd.enter_context(SB(n, list(s), d))
                PS = lambda n, s, d=F32: sd.enter_context(PSM(n, list(s), d))
                wab = T("wab", [128, 8, D], BF16); wbb = T("wbb", [128, 8, D], BF16)
                wst = [T(f"wst{i}", [128, 2, D]) for i in range(2)]
                yaTs = T("yaTs", [128, 8, TO], BF16); ybTs = T("ybTs", [128, 8, TO], BF16)
                ybl = [T(f"ybl{i}", [128, RW], BF16) for i in range(2)]
                gat = [T(f"gat{i}", [128, 2, 512], BF16) for i in range(4)]
                m1 = [T(f"m1{i}", [128, 512]) for i in range(3)]; m2 = [T(f"m2{i}", [128, 512]) for i in range(3)]
                mo = [T(f"mo{i}", [128, 512], BF16) for i in range(3)]
                psa = [PS(f"psa{i}", [128, 512]) for i in range(3)]; psb_ = [PS(f"psb{i}", [128, 512]) for i in range(3)]
                n = 0
                for (wsrc, wdst, nm) in ((w_a, wab, "wab"), (w_b, wbb, "wbb")):
                    for i in range(4):
                        bb_ = n % 2; n += 1
                        dma("sp", wst[bb_][:], wsrc[i * 256:(i + 1) * 256, :].rearrange("(c p) n -> p c n", p=128), w=[("wst", bb_)])
                        if n % 2:
                            op("dve", lambda g: g.tensor_copy(out=wdst[:, 2 * i:2 * i + 2, :], in_=wst[bb_][:]), r=[("wst", bb_)], w=[(nm, i)])
                        else:
                            op("act", lambda g: g.copy(out=wdst[:, 2 * i:2 * i + 2, :], in_=wst[bb_][:]), r=[("wst", bb_)], w=[(nm, i)])
                dma("sp", yaTs[:], YAT.rearrange("(c p) t -> p c t", p=128), w=["yaTs"])
                dma("sp", ybTs[:], YBT.rearrange("(c p) t -> p c t", p=128), w=["ybTs"])
                wak = [("wab", i) for i in range(4)]; wbk = [("wbb", i) for i in range(4)]
                it = 0
                for dblk in range(16):
                    for tsb in range(4):
                        b = it % 3; gb = it % 4; it += 1
                        dma("sp", gat[gb][:, 0, :], GT[dblk * 128:(dblk + 1) * 128, tsb * 512:(tsb + 1) * 512], w=[("gat", gb, 0)])
                        dma("sp", gat[gb][:, 1, :], GT[2048 + dblk * 128:2048 + (dblk + 1) * 128, tsb * 512:(tsb + 1) * 512], w=[("gat", gb, 1)])
                        for kc in range(8):
                            op("pe", lambda t: t.matmul(psa[b][:, :], lhsT=wab[:, kc, dblk * 128:(dblk + 1) * 128], rhs=yaTs[:, kc, tsb * 512:(tsb + 1) * 512], start=(kc == 0), stop=(kc == 7)), r=wak + ["yaTs"], w=[("psa", b)])
                        for kc in range(8):
                            op("pe", lambda t: t.matmul(psb_[b][:, :], lhsT=wbb[:, kc, dblk * 128:(dblk + 1) * 128], rhs=ybTs[:, kc, tsb * 512:(tsb + 1) * 512], start=(kc == 0), stop=(kc == 7)), r=wbk + ["ybTs"], w=[("psb", b)])
                        op("dve", lambda v: v.tensor_tensor(out=m1[b][:], in0=psa[b][:, :], in1=gat[gb][:, 0, :], op=ALU.mult), r=[("psa", b), ("gat", gb, 0)], w=[("m1", b)])
                        op("dve", lambda v: v.tensor_tensor(out=m2[b][:], in0=psb_[b][:, :], in1=gat[gb][:, 1, :], op=ALU.mult), r=[("psb", b), ("gat", gb, 1)], w=[("m2", b)])
                        op("pool", lambda g: g.tensor_tensor(out=mo[b][:], in0=m1[b][:], in1=m2[b][:], op=ALU.add), r=[("m1", b), ("m2", b)], w=[("mo", b)])
                        dma("sp", MT[dblk * 128:(dblk + 1) * 128, tsb * 512:(tsb + 1) * 512], mo[b][:], r=[("mo", b)], w=[("MT", dblk, tsb)])
                kb.barrier()

            with ExitStack() as sd:
                T = lambda n, s, d=F32: sd.enter_context(SB(n, list(s), d))
                PS = lambda n, s, d=F32: sd.enter_context(PSM(n, list(s), d))
                wob = T("wob", [128, 16, D], BF16)
                wst = [T(f"wst{i}", [128, 2, D]) for i in range(2)]
                mTt = [T(f"mTt{i}", [128, 16, 128], BF16) for i in range(2)]
                xt = [T(f"xt{i}", [128, D]) for i in range(2)]
                h2t = [T(f"h2t{i}", [128, D]) for i in range(2)]
                g2t = T("g2t", [128, D]); xnb = T("xnb", [128, D], BF16); junk = T("junk", [128, D], BF16)
                st1 = T("st1", [128, 4]); xn2t = T("xn2t", [128, 16, 128], BF16)
                pso_ = [PS(f"pso{i}", [128, 512]) for i in range(4)]
                pstr = [PS(f"pstr{i}", [128, 8, 128], BF16) for i in range(2)]
                for i in range(8):
                    bb_ = i % 2
                    dma("sp", wst[bb_][:], w_o[i * 256:(i + 1) * 256, :].rearrange("(c p) n -> p c n", p=128), w=[("wst", bb_)])
                    if i % 2:
                        op("dve", lambda g: g.tensor_copy(out=wob[:, 2 * i:2 * i + 2, :], in_=wst[bb_][:]), r=[("wst", bb_)], w=[("wob", i)])
                    else:
                        op("act", lambda g: g.copy(out=wob[:, 2 * i:2 * i + 2, :], in_=wst[bb_][:]), r=[("wst", bb_)], w=[("wob", i)])
                wok = [("wob", i) for i in range(8)]
                dma("sp", g2t[:], g2bc[:, :], w=["g2t"])
                MT3 = MT.rearrange("(c p) t -> p c t", p=128)
                xnbs = [xnb, T("xnb1", [128, D], BF16)]
                st1s = [st1, T("st11", [128, 4])]

                def d2_mm(ti):
                    b = ti % 2
                    xb_, s1 = xnbs[b], st1s[b]
                    dma("sp", mTt[b][:], MT3[:, :, ti * 128:(ti + 1) * 128], w=[("mTt", b)])
                    dma("sp", xt[b][:], xall[TP + ti * 128:TP + (ti + 1) * 128, :], w=[("xt", b)])
                    for cb in range(4):
                        for kc in range(16):
                            op("pe", lambda t: t.matmul(pso_[cb][:, :], lhsT=mTt[b][:, kc, :], rhs=wob[:, kc, cb * 512:(cb + 1) * 512], start=(kc == 0), stop=(kc == 15)), r=[("mTt", b)] + wok, w=[("pso", cb)])
                        op("dve", lambda v: v.tensor_tensor(out=h2t[b][:, cb * 512:(cb + 1) * 512], in0=pso_[cb][:, :], in1=xt[b][:, cb * 512:(cb + 1) * 512], op=ALU.add), r=[("pso", cb), ("xt", b)], w=[("h2t", b, cb)])
                    hk = [("h2t", b, cb) for cb in range(4)]
                    dma("sp", H2[ti * 128:(ti + 1) * 128, :], h2t[b][:], r=hk, w=[("H2", ti)])
                    op("dve", lambda v: v.memset(s1[:, 0:1], 0.0), w=[("ss", b)])
                    op("act", lambda a: a.activation(out=junk[:], in_=h2t[b][:], func=AF.Square, accum_out=s1[:, 0:1]), r=hk + [("ss", b)], w=[("ss", b)])
                    op("act", lambda a: a.activation(out=s1[:, 1:2], in_=s1[:, 0:1], func=AF.Ln, bias=cst[:, 0:1], scale=1.0 / D), r=[("ss", b)], w=[("lnv", b)])
                    op("act", lambda a: a.activation(out=s1[:, 2:3], in_=s1[:, 1:2], func=AF.Exp, scale=-0.5), r=[("lnv", b)], w=[("rstd", b)])
                    op("dve", lambda v: v.scalar_tensor_tensor(out=xb_[:], in0=h2t[b][:], scalar=s1[:, 2:3], in1=g2t[:], op0=ALU.mult, op1=ALU.mult), r=hk + [("rstd", b), "g2t"], w=[("xnb", b)])

                def d2_tr(ti):
                    b = ti % 2
                    xb_ = xnbs[b]
                    for hh in range(2):
                        for c in range(8):
                            kc = hh * 8 + c
                            op("pe", lambda t: t.transpose(out=pstr[hh][:, c, :], in_=xb_[:, kc * 128:(kc + 1) * 128], identity=ident[:]), r=[("xnb", b)], w=[("pstr", hh)])
                        op("act", lambda a: a.copy(out=xn2t[:, hh * 8:(hh + 1) * 8, :], in_=pstr[hh][:]), r=[("pstr", hh)], w=[("xn2t", hh)])
                    dma("sp", XN2T.rearrange("(c p) t -> p c t", p=128)[:, :, ti * 128:(ti + 1) * 128], xn2t[:], r=[("xn2t", 0), ("xn2t", 1)], w=[("XN2T", ti)])

                d2_mm(0)
                for ti in range(1, NOT_):
                    d2_mm(ti)
                    d2_tr(ti - 1)
                d2_tr(NOT_ - 1)
                kb.barrier()

        if "E" in STAGES:
            with ExitStack() as se:
                T = lambda n, s, d=F32: se.enter_context(SB(n, list(s), d))
                PS = lambda n, s, d=F32: se.enter_context(PSM(n, list(s), d))
                xn2 = T("xn2", [128, 16, TO], BF16)
                wf = [T(f"wf{i}", [128, 16, 256]) for i in range(2)]
                wb = [T(f"wb{i}", [128, 16, 256], BF16) for i in range(2)]
                sg = [T(f"sg{i}", [128, 512]) for i in range(2)]
                ao = [T(f"ao{i}", [128, 512], BF16) for i in range(2)]
                psg = [PS(f"psg{i}", [128, 512]) for i in range(2)]; psu = [PS(f"psu{i}", [128, 512]) for i in range(2)]
                wst = [T(f"wst{i}", [128, 2, D]) for i in range(2)]
                wcv = [T(f"wcv{i}", [128, 2, D], BF16) for i in range(2)]

                def conv_wdn(i):
                    b = i % 2
                    dma("sp", wst[b][:], w_dn[i * 256:(i + 1) * 256, :].rearrange("(c p) n -> p c n", p=128), w=[("wst", b)])
                    op("dve", lambda v: v.tensor_copy(out=wcv[b][:], in_=wst[b][:]), r=[("wst", b)], w=[("wcv", b)])
                    dma("sp", WDNB[i * 256:(i + 1) * 256, :].rearrange("(c p) n -> p c n", p=128), wcv[b][:], r=[("wcv", b)], w=[("WDNB", i)])
                XN3 = XN2T.rearrange("(c p) t -> p c t", p=128)
                for q4 in range(4):
                    dma("sp", xn2[:, q4 * 4:(q4 + 1) * 4, :], XN3[:, q4 * 4:(q4 + 1) * 4, :], w=[("xn2", q4)])
                xk = [("xn2", q4) for q4 in range(4)]

                def load_gu(fb):
                    b = fb % 2
                    for q4 in range(4):
                        dma("sp", wf[b][:, q4 * 4:(q4 + 1) * 4, 0:128], w_gu[q4 * 512:(q4 + 1) * 512, fb * 128:(fb + 1) * 128].rearrange("(kc p) n -> p kc n", p=128), w=[("wf", b, q4, 0)])
                        dma("sp", wf[b][:, q4 * 4:(q4 + 1) * 4, 128:256], w_gu[q4 * 512:(q4 + 1) * 512, DFF + fb * 128:DFF + (fb + 1) * 128].rearrange("(kc p) n -> p kc n", p=128), w=[("wf", b, q4, 1)])
                    for q4 in range(4):
                        op("pool", lambda g: g.tensor_copy(out=wb[b][:, q4 * 4:(q4 + 1) * 4, :], in_=wf[b][:, q4 * 4:(q4 + 1) * 4, :]), r=[("wf", b, q4, 0), ("wf", b, q4, 1)], w=[("wb", b, q4)])

                load_gu(0)
                it = 0
                for fb in range(DFF // 128):
                    b = fb % 2
                    if fb + 1 < DFF // 128:
                        load_gu(fb + 1)
                    if fb % 2 == 0:
                        conv_wdn(fb // 2)
                    for tsb in range(4):
                        pi = it % 2; it += 1
                        for kc in range(16):
                            op("pe", lambda t: t.matmul(psg[pi][:, :], lhsT=wb[b][:, kc, 0:128], rhs=xn2[:, kc, tsb * 512:(tsb + 1) * 512], start=(kc == 0), stop=(kc == 15)), r=[("wb", b, kc // 4)] + xk, w=[("psg", pi)])
                        for kc in range(16):
                            op("pe", lambda t: t.matmul(psu[pi][:, :], lhsT=wb[b][:, kc, 128:256], rhs=xn2[:, kc, tsb * 512:(tsb + 1) * 512], start=(kc == 0), stop=(kc == 15)), r=[("wb", b, kc // 4)] + xk, w=[("psu", pi)])
                        op("act", lambda a: a.activation(out=sg[pi][:], in_=psg[pi][:, :], func=AF.Silu), r=[("psg", pi)], w=[("sg", pi)])
                        op("dve", lambda v: v.tensor_tensor(out=ao[pi][:], in0=sg[pi][:], in1=psu[pi][:, :], op=ALU.mult), r=[("sg", pi), ("psu", pi)], w=[("ao", pi)])
                        dma("sp", ACTT[fb * 128:(fb + 1) * 128, tsb * 512:(tsb + 1) * 512], ao[pi][:], r=[("ao", pi)], w=[("ACTT", fb, tsb)])
                kb.barrier()

        if "F" in STAGES:
            with ExitStack() as sf:
                T = lambda n, s, d=F32: sf.enter_context(SB(n, list(s), d))
                PS = lambda n, s, d=F32: sf.enter_context(PSM(n, list(s), d))
                NC_ = DFF // 128
                acts = T("acts", [128, NC_, 1024], BF16)
                wdb = [T(f"wdb{i}", [128, NC_, 512], BF16) for i in range(2)]
                h2s = [T(f"h2s{i}", [128, 512]) for i in range(4)]
                ot = [T(f"ot{i}", [128, 512]) for i in range(4)]
                psd = [PS(f"psd{i}", [128, 512]) for i in range(4)]
                AC3 = ACTT.rearrange("(c p) t -> p c t", p=128)
                WD3 = WDNB.rearrange("(c p) n -> p c n", p=128)
                it = 0
                def load_wd(cb, b):
                    for q4 in range(4):
                        dma("sp", wdb[b][:, q4 * 11:(q4 + 1) * 11, :], WD3[:, q4 * 11:(q4 + 1) * 11, cb * 512:(cb + 1) * 512], w=[("wdb", b, q4)])
                seq = [(th, cb) for th in range(2) for cb in range(4)]
                load_wd(seq[0][1], 0)
                for si, (th, cb) in enumerate(seq):
                    b = si % 2
                    if cb == 0:
                        for q4 in range(4):
                            dma("act", acts[:, q4 * 11:(q4 + 1) * 11, :], AC3[:, q4 * 11:(q4 + 1) * 11, th * 1024:(th + 1) * 1024], w=[("acts", q4)])
                    if si + 1 < len(seq):
                        load_wd(seq[si + 1][1], (si + 1) % 2)
                    for j in range(8):
                        pi = it % 4; it += 1
                        tok0 = th * 1024 + j * 128
                        dma("act", h2s[pi][:], H2[tok0:tok0 + 128, cb * 512:(cb + 1) * 512], w=[("h2s", pi)])
                        for c in range(NC_):
                            op("pe", lambda t: t.matmul(psd[pi][:, :], lhsT=acts[:, c, j * 128:(j + 1) * 128], rhs=wdb[b][:, c, :], start=(c == 0), stop=(c == NC_ - 1)), r=[("acts", c // 11), ("wdb", b, c // 11)], w=[("psd", pi)])
                        op("dve", lambda v: v.tensor_tensor(out=ot[pi][:], in0=psd[pi][:, :], in1=h2s[pi][:], op=ALU.add), r=[("psd", pi), ("h2s", pi)], w=[("ot", pi)])
                        dma("act", out[tok0:tok0 + 128, cb * 512:(cb + 1) * 512], ot[pi][:], r=[("ot", pi)], w=[("out", tok0, cb)])
                kb.barrier()

        kb.barrier()
    return nc


def prep_core_inputs(inp, core):
    b, half = core // 2, core % 2
    x = np.asarray(inp["x"], dtype=np.float32)
    meta = np.asarray(inp["meta_tokens"], dtype=np.float32)
    xall = np.zeros((TA, D), np.float32)
    keypen = np.zeros((128, NT), np.float32)
    if half == 0:
        xall[TP - 16:TP] = meta
        xall[TP:] = x[b, 0:2048]
        ndummy = TP - 16
    else:
        xall[112:128] = meta
        xall[128:TP] = x[b, 0:2048]
        xall[TP:] = x[b, 2048:4096]
        ndummy = 112
    kp = np.zeros(TA, np.float32)
    kp[:ndummy] = -30000.0
    keypen[:, :] = kp.reshape(NT, 128).T
    f = lambda k: np.asarray(inp[k], dtype=np.float32)[0]
    bc = lambda v, n=128: np.ascontiguousarray(np.broadcast_to(v[None, :], (n, v.shape[0])))
    mu = f("rwkv_mu")
    mu24 = np.ascontiguousarray(mu[0:3072].reshape(24, 128).T)
    mul4 = np.zeros((128, 4), np.float32)
    mul4[0:96, 0] = mu[3072:3168]
    mul4[0:96, 1] = mu[3168:3264]
    mul4[:, 2] = mu[3264:3392]
    mul4[:, 3] = mu[3392:3520]
    pp = lambda v: np.ascontiguousarray(v.reshape(8, 128).T)
    vecp = np.stack([pp(f("rwkv_a0")), pp(f("rwkv_k_k")), pp(f("rwkv_k_a")), pp(f("rwkv_r_k").reshape(-1))], axis=1)
    d = {
        "xall": xall, "keypen": keypen,
        "w_in": f("w_in"), "w_a": f("w_branch_a"), "w_b": f("w_branch_b"), "w_o": f("w_o"),
        "w_gu": f("w_gate_up"), "w_dn": f("w_down"),
        "g1bc": bc(f("norm1_g")), "g2bc": bc(f("norm2_g")),
        "mu24": mu24, "mul4": mul4, "w0bc": bc(f("rwkv_w0")),
        "w2": f("rwkv_w2"), "a2": f("rwkv_a2"), "g2": f("rwkv_g2"),
        "vecp": np.ascontiguousarray(vecp),
        "gnwbc": bc(f("rwkv_gn_w")), "gnbbc": bc(f("rwkv_gn_b")),
        "qgbc": bc(np.tile(f("fox_q_norm_g"), 16)), "kgbc": bc(np.tile(f("fox_k_norm_g"), 16)),
        "fbbc": bc(f("fox_f_bias")),
    }
    return d


_NC_CACHE = {}


def kernel(**inputs):
    if "nc" not in _NC_CACHE:
        _NC_CACHE["nc"] = build_program()
    nc = _NC_CACHE["nc"]
    shared = None
    in_maps = []
    ncores = int(os.environ.get("KCORES", "8"))
    for c in range(ncores):
        in_maps.append(prep_core_inputs(inputs, c))
    res = run_bass_kernel_spmd(nc, in_maps, core_ids=list(range(ncores)))
    B = inputs["x"].shape[0]
    outp = np.zeros((B, 4096, D), np.float32)
    for c in range(ncores):
        b, half = c // 2, c % 2
        outp[b, half * 2048:(half + 1) * 2048] = res.results[c]["out"]
    if DEBUG:
        kernel.last = res
    return outp
```
